# Optimizing a Trainium2 kernel written in Bass

```python
import math
import jax, jax.numpy as jnp
from jax import lax
import numpy as np

D_MODEL = 4096
BATCH = 4
SEQ = 4096
DEPTH = 2

ATTN_PATTERNS = ((128, 1), (512, 4), (2048, 16))
N_GROUPS_ATTN = 3
HEADS_PER_GROUP = 16
HEAD_DIM = 128
D_ATTN = HEADS_PER_GROUP * HEAD_DIM
ATTN_BLOCK = 128
QKV_COLS = N_GROUPS_ATTN * 3 * D_ATTN
IN_ATTN_COLS = QKV_COLS + D_ATTN

NUM_BUCKETS = 32
MAX_DISTANCE = 2048
N_BIAS_HEADS = N_GROUPS_ATTN * HEADS_PER_GROUP

EXPAND = 2
D_INNER = EXPAND * D_MODEL
SSM_HEAD_DIM = 64
SSM_HEADS = D_INNER // SSM_HEAD_DIM
SSM_GROUPS = 8
HEADS_PER_SSM_GROUP = SSM_HEADS // SSM_GROUPS
D_STATE = 128
CONV_WIDTH = 4
CONV_DIM = D_INNER + 2 * SSM_GROUPS * D_STATE
IN_SSM_COLS = D_INNER + CONV_DIM + SSM_HEADS
CHUNK = 128

N_MIXERS = 2
N_ATTN_LAYERS = (DEPTH + 1) // 2
N_SSM_LAYERS = DEPTH // 2
DEEPNORM_ALPHA = (2 * DEPTH) ** 0.25
DEEPNORM_BETA = (8 * DEPTH) ** -0.25
LN_EPS = 1e-5
RMS_EPS = 1e-5
NEG_INF = -1e30

kernel_name = 'hybrid_dilated_attn_mamba2_deepnorm'


def t5_causal_bucket(dist):
    max_exact = NUM_BUCKETS // 2
    d_f = jnp.maximum(dist, 1).astype(jnp.float32)
    large = max_exact + (jnp.log(d_f / max_exact) / math.log(MAX_DISTANCE / max_exact)
                         * (NUM_BUCKETS - max_exact)).astype(jnp.int32)
    large = jnp.minimum(large, NUM_BUCKETS - 1)
    return jnp.where(dist < max_exact, dist, large)


def dilated_group_attention(q, k, v, bias_table, window, dilation):
    b, s, h, dh = q.shape
    span = window // dilation
    L = s // dilation
    nb = -(-L // ATTN_BLOCK)
    lp = nb * ATTN_BLOCK

    def to_sub(t):
        t = t.reshape(b, L, dilation, h, dh).transpose(0, 2, 1, 3, 4)
        t = jnp.pad(t, ((0, 0), (0, 0), (0, lp - L), (0, 0), (0, 0)))
        return t.reshape(b, dilation, nb, ATTN_BLOCK, h, dh)

    def with_prev(t):
        prev = jnp.pad(t, ((0, 0), (0, 0), (1, 0), (0, 0), (0, 0), (0, 0)))[:, :, :-1]
        return jnp.concatenate([prev, t], axis=3)

    qb = to_sub(q)
    kk = with_prev(to_sub(k))
    vv = with_prev(to_sub(v))

    qi = jnp.arange(ATTN_BLOCK)[:, None]
    ki = jnp.arange(2 * ATTN_BLOCK)[None, :]
    delta = ATTN_BLOCK + qi - ki
    band = (delta >= 0) & (delta <= span)
    not_first = (jnp.arange(nb) > 0)[:, None, None]
    valid = band[None] & (not_first | (ki >= ATTN_BLOCK)[None])
    bucket = t5_causal_bucket(jnp.clip(delta, 0, None) * dilation)
    bias = bias_table.astype(jnp.float32)[bucket].transpose(2, 0, 1)

    logits = jnp.einsum('brnqhd,brnkhd->brnhqk', qb, kk).astype(jnp.float32)
    logits = logits * (dh ** -0.5) + bias[None, None, None]
    logits = jnp.where(valid[None, None, :, None], logits, NEG_INF)
    m = jnp.max(logits, axis=-1, keepdims=True)
    p = jnp.exp(logits - m)
    denom = jnp.sum(p, axis=-1, keepdims=True)
    o = jnp.einsum('brnhqk,brnkhd->brnqhd', p / denom, vv.astype(jnp.float32))
    lse = (m + jnp.log(denom))[..., 0]

    o = o.reshape(b, dilation, lp, h, dh)[:, :, :L].transpose(0, 2, 1, 3, 4).reshape(b, s, h, dh)
    lse = lse.transpose(0, 1, 2, 4, 3).reshape(b, dilation, lp, h)[:, :, :L]
    lse = lse.transpose(0, 2, 1, 3).reshape(b, s, h)
    return o, lse


def dilated_attention_mixer(x, w_in, w_out, rel_bias):
    b, s, _ = x.shape
    proj = jnp.einsum('bsd,de->bse', x, w_in)
    qkv = proj[..., :QKV_COLS].reshape(b, s, N_GROUPS_ATTN, 3, HEADS_PER_GROUP, HEAD_DIM)
    gate = proj[..., QKV_COLS:]
    outs, lses = [], []
    for g, (window, dilation) in enumerate(ATTN_PATTERNS):
        o, lse = dilated_group_attention(
            qkv[:, :, g, 0], qkv[:, :, g, 1], qkv[:, :, g, 2],
            rel_bias[:, g * HEADS_PER_GROUP:(g + 1) * HEADS_PER_GROUP], window, dilation)
        outs.append(o)
        lses.append(lse)
    w = jax.nn.softmax(jnp.stack(lses), axis=0)
    o = jnp.einsum('gbsh,gbshd->bshd', w, jnp.stack(outs)).reshape(b, s, D_ATTN)
    y = o.astype(x.dtype) * jax.nn.silu(gate)
    return jnp.einsum('bse,ed->bsd', y, w_out)


def ssd_chunked_scan(xs, dt, a, bm, cm):
    b, s, g, hpg, p = xs.shape
    n = bm.shape[-1]
    nc = s // CHUNK

    def chunks(t):
        return t.reshape((b, nc, CHUNK) + t.shape[2:]).swapaxes(0, 1)

    causal = jnp.tril(jnp.ones((CHUNK, CHUNK), dtype=bool))

    def step(state, inp):
        xc, dtc, bc, cc = inp
        bc = bc.astype(jnp.float32)
        cc = cc.astype(jnp.float32)
        a_cum = jnp.cumsum(dtc * a, axis=1)
        seg = a_cum[:, :, None] - a_cum[:, None, :]
        decay = jnp.exp(jnp.where(causal[None, :, :, None, None], seg, -jnp.inf))
        xdt = xc.astype(jnp.float32) * dtc[..., None]
        cb = jnp.einsum('blgn,bsgn->blsg', cc, bc)
        y_diag = jnp.einsum('blsg,blsgh,bsghp->blghp', cb, decay, xdt)
        y_off = jnp.einsum('blgn,bghpn,blgh->blghp', cc, state, jnp.exp(a_cum))
        to_end = jnp.exp(a_cum[:, -1:] - a_cum)
        new_state = (state * jnp.exp(a_cum[:, -1])[..., None, None]
                     + jnp.einsum('bsgn,bsgh,bsghp->bghpn', bc, to_end, xdt))
        return new_state, y_diag + y_off

    state0 = jnp.zeros((b, g, hpg, p, n), jnp.float32)
    _, y = lax.scan(step, state0, (chunks(xs), chunks(dt), chunks(bm), chunks(cm)))
    return y.swapaxes(0, 1).reshape(b, s, g, hpg, p)


def ssd_mixer(x, w_in, conv_w, conv_b, dt_bias, a_log, d_skip, norm_w, w_out):
    b, s, _ = x.shape
    proj = jnp.einsum('bsd,de->bse', x, w_in)
    z = proj[..., :D_INNER]
    xbc = proj[..., D_INNER:D_INNER + CONV_DIM]
    dt_raw = proj[..., D_INNER + CONV_DIM:]
    xbc = lax.conv_general_dilated(
        xbc, conv_w[:, None, :], window_strides=(1,), padding=[(CONV_WIDTH - 1, 0)],
        dimension_numbers=('NWC', 'WIO', 'NWC'), feature_group_count=CONV_DIM)
    xbc = jax.nn.silu(xbc + conv_b)
    gn = SSM_GROUPS * D_STATE
    xs = xbc[..., :D_INNER].reshape(b, s, SSM_GROUPS, HEADS_PER_SSM_GROUP, SSM_HEAD_DIM)
    bm = xbc[..., D_INNER:D_INNER + gn].reshape(b, s, SSM_GROUPS, D_STATE)
    cm = xbc[..., D_INNER + gn:].reshape(b, s, SSM_GROUPS, D_STATE)
    dt = jax.nn.softplus(dt_raw.astype(jnp.float32) + dt_bias.astype(jnp.float32))
    dt = dt.reshape(b, s, SSM_GROUPS, HEADS_PER_SSM_GROUP)
    a = -jnp.exp(a_log.astype(jnp.float32)).reshape(SSM_GROUPS, HEADS_PER_SSM_GROUP)
    y = ssd_chunked_scan(xs, dt, a, bm, cm)
    y = y + d_skip.astype(jnp.float32).reshape(SSM_GROUPS, HEADS_PER_SSM_GROUP)[:, :, None] * xs
    y = y.reshape(b, s, D_INNER) * jax.nn.silu(z.astype(jnp.float32))
    yg = y.reshape(b, s, SSM_GROUPS, D_INNER // SSM_GROUPS)
    yg = yg * lax.rsqrt(jnp.mean(yg * yg, axis=-1, keepdims=True) + RMS_EPS)
    y = yg.reshape(b, s, D_INNER) * norm_w.astype(jnp.float32)
    return jnp.einsum('bse,ed->bsd', y.astype(x.dtype), w_out)


def layer_norm(x, g, b):
    xf = x.astype(jnp.float32)
    mu = jnp.mean(xf, axis=-1, keepdims=True)
    var = jnp.mean(jnp.square(xf - mu), axis=-1, keepdims=True)
    return ((xf - mu) * lax.rsqrt(var + LN_EPS) * g.astype(jnp.float32)
            + b.astype(jnp.float32)).astype(x.dtype)


def setup_inputs(seed: int = 0) -> dict:
    key = jax.random.key(seed)
    ks = jax.random.split(key, 15)
    f32 = jnp.float32
    nrm = jax.random.normal
    x = nrm(ks[0], (BATCH, SEQ, D_MODEL), f32)
    w_in_attn = nrm(ks[1], (N_ATTN_LAYERS, D_MODEL, IN_ATTN_COLS), f32) * D_MODEL ** -0.5
    w_out_attn = nrm(ks[2], (N_ATTN_LAYERS, D_ATTN, D_MODEL), f32) * (D_ATTN ** -0.5 * DEEPNORM_BETA)
    rel_bias = nrm(ks[3], (NUM_BUCKETS, N_BIAS_HEADS), f32) * 0.5
    w_in_ssm = nrm(ks[4], (N_SSM_LAYERS, D_MODEL, IN_SSM_COLS), f32) * D_MODEL ** -0.5
    conv_w = nrm(ks[5], (N_SSM_LAYERS, CONV_WIDTH, CONV_DIM), f32) * CONV_WIDTH ** -0.5
    conv_b = nrm(ks[6], (N_SSM_LAYERS, CONV_DIM), f32) * 0.02
    dt0 = jnp.exp(jax.random.uniform(ks[7], (N_SSM_LAYERS, SSM_HEADS), f32,
                                     minval=math.log(1e-3), maxval=math.log(1e-1)))
    dt_bias = dt0 + jnp.log(-jnp.expm1(-dt0))
    a_log = jnp.log(jax.random.uniform(ks[8], (N_SSM_LAYERS, SSM_HEADS), f32, minval=1.0, maxval=16.0))
    d_skip = 1.0 + 0.1 * nrm(ks[9], (N_SSM_LAYERS, SSM_HEADS), f32)
    ssm_norm_w = 1.0 + 0.02 * nrm(ks[10], (N_SSM_LAYERS, D_INNER), f32)
    w_out_ssm = nrm(ks[11], (N_SSM_LAYERS, D_INNER, D_MODEL), f32) * (D_INNER ** -0.5 * DEEPNORM_BETA)
    ln_g = 1.0 + 0.02 * nrm(ks[12], (DEPTH, D_MODEL), f32)
    ln_b = 0.02 * nrm(ks[13], (DEPTH, D_MODEL), f32)
    return {'x': x, 'w_in_attn': w_in_attn, 'w_out_attn': w_out_attn, 'rel_bias': rel_bias,
            'w_in_ssm': w_in_ssm, 'conv_w': conv_w, 'conv_b': conv_b, 'dt_bias': dt_bias,
            'a_log': a_log, 'd_skip': d_skip, 'ssm_norm_w': ssm_norm_w, 'w_out_ssm': w_out_ssm,
            'ln_g': ln_g, 'ln_b': ln_b}


def reference(x, w_in_attn, w_out_attn, rel_bias, w_in_ssm, conv_w, conv_b, dt_bias,
              a_log, d_skip, ssm_norm_w, w_out_ssm, ln_g, ln_b):
    for i in range(DEPTH):
        j = i // N_MIXERS
        if i % N_MIXERS == 0:
            h = dilated_attention_mixer(x, w_in_attn[j], w_out_attn[j], rel_bias)
        else:
            h = ssd_mixer(x, w_in_ssm[j], conv_w[j], conv_b[j], dt_bias[j], a_log[j],
                          d_skip[j], ssm_norm_w[j], w_out_ssm[j])
        x = layer_norm(DEEPNORM_ALPHA * x + h, ln_g[i], ln_b[i])
    return x
```

```python
from contextlib import ExitStack
import math
import numpy as np
import concourse.bass as bass
import concourse.mybir as mybir
from concourse.bass_utils import run_bass_kernel_spmd

F32 = mybir.dt.float32
BF16 = mybir.dt.bfloat16
AF = mybir.ActivationFunctionType
ALU = mybir.AluOpType
AX = mybir.AxisListType

ENGS = ('pe', 'dve', 'act', 'pool', 'sp')
SIG_CH = 12000
DMA_CH = 700


class Buf:
    __slots__ = ('name', 'writers', 'readers', 'dcount', 'dsems', 'excl')

    def __init__(self, name, excl=False):
        self.name = name
        self.excl = excl
        self.writers = []
        self.readers = []
        self.dcount = 0
        self.dsems = None


class Op:
    __slots__ = ('eng', 'emit', 'deps', 'is_dma', 'dbuf', 'didx', 'need_sig', 'sig', 'idx')


class Prog:
    def __init__(self, nc, stack):
        self.nc = nc
        self.stack = stack
        self.ops = []
        self.by_eng = {e: [] for e in ENGS}
        self.bar = {}

    def op(self, eng, emit, reads=(), writes=(), dma=None, partial=False):
        o = Op()
        o.eng = eng
        o.emit = emit
        o.is_dma = dma is not None
        o.dbuf = dma
        o.need_sig = False
        o.sig = None
        o.idx = len(self.ops)
        deps = {}
        if any(b.excl for b in reads):
            writes = list(writes) + [b for b in reads if b.excl and b not in writes]
            reads = [b for b in reads if not b.excl]
        for b in reads:
            for w in b.writers:
                deps[w] = 'w'
        for b in writes:
            for w in b.writers:
                deps[w] = 'w'
            for r in b.readers:
                if r not in deps:
                    deps[r] = 'r'
        if eng in self.bar:
            for d in self.bar.pop(eng):
                deps[d] = 'w'
        deps.pop(o, None)
        o.deps = deps
        for b in reads:
            b.readers.append(o)
        for b in writes:
            if partial and not b.excl:
                b.writers.append(o)
            else:
                b.writers = [o]
            b.readers = []
        if o.is_dma:
            dma.dcount += 1
            o.didx = dma.dcount
        self.ops.append(o)
        self.by_eng[eng].append(o)
        return o

    def barrier(self):
        deps = []
        for e in ENGS:
            for o in reversed(self.by_eng[e]):
                if not o.is_dma:
                    deps.append(o)
                    break
        lastd = {}
        for o in self.ops:
            if o.is_dma:
                lastd[id(o.dbuf)] = o
        deps.extend(lastd.values())
        for e in ENGS:
            self.bar[e] = list(deps)

    def finalize(self):
        nc = self.nc
        for o in self.ops:
            for d, kind in o.deps.items():
                if d.is_dma:
                    continue
                if d.eng == o.eng and not o.is_dma:
                    if o.eng == 'pe' or kind == 'r':
                        continue
                d.need_sig = True
        cnt = {e: 0 for e in ENGS}
        for o in self.ops:
            if not o.is_dma and o.need_sig:
                cnt[o.eng] += 1
                o.sig = (o.eng, (cnt[o.eng] - 1) // SIG_CH, (cnt[o.eng] - 1) % SIG_CH + 1)
        esems = {}
        nsem = 0
        for e in ENGS:
            n = (cnt[e] + SIG_CH - 1) // SIG_CH
            esems[e] = [self.stack.enter_context(nc.semaphore(f"s_{e}_{i}")) for i in range(n)]
            nsem += n
        seen_b = set()
        for o in self.ops:
            if o.is_dma and id(o.dbuf) not in seen_b:
                seen_b.add(id(o.dbuf))
                b = o.dbuf
                n = (b.dcount + DMA_CH - 1) // DMA_CH
                b.dsems = [self.stack.enter_context(nc.semaphore(f"d_{b.name}_{i}")) for i in range(n)]
                nsem += n
        self.n_sems = nsem

        def sig_of(d):
            if d.is_dma:
                k = d.didx - 1
                return d.dbuf.dsems[k // DMA_CH], 16 * (k % DMA_CH + 1)
            e, si, v = d.sig
            return esems[e][si], v

        engh = {'pe': 'tensor', 'dve': 'vector', 'act': 'scalar', 'pool': 'gpsimd', 'sp': 'sync'}
        block = self.stack.enter_context(nc.Block())

        def make(ename):
            ops = self.by_eng[ename]

            def body(eng):
                seen = {}
                for o in ops:
                    need = {}
                    for d, kind in o.deps.items():
                        if not d.is_dma and d.eng == o.eng and not o.is_dma:
                            if o.eng == 'pe' or kind == 'r':
                                continue
                        s, v = sig_of(d)
                        k = id(s)
                        if seen.get(k, 0) >= v:
                            continue
                        if k not in need or need[k][1] < v:
                            need[k] = (s, v)
                    for k, (s, v) in need.items():
                        eng.wait_ge(s, v)
                        seen[k] = v
                    ins = o.emit(eng)
                    if o.is_dma:
                        s, v = sig_of(o)
                        ins.then_inc(s, 16)
                    elif o.sig is not None:
                        s, v = sig_of(o)
                        ins.then_inc(s, 1)
                last = {}
                for o in ops:
                    if o.is_dma:
                        s, v = sig_of(o)
                        last[id(s)] = (s, max(v, last.get(id(s), (s, 0))[1]))
                for k, (s, v) in last.items():
                    if seen.get(k, 0) < v:
                        eng.wait_ge(s, v)
            return body

        for e in ENGS:
            if self.by_eng[e]:
                getattr(block, engh[e])(make(e))


class Cfg:
    def __init__(self, D=4096, S=4096, HG=16, G8=8, HPG=16, debug=False, layers=(0, 1)):
        self.D = D
        self.S = S
        self.KC = D // 128
        self.HG = HG
        self.DATT = HG * 128
        self.NBLK0 = HG * 10
        self.patterns = ((128, 1), (512, 4), (2048, 16))
        self.G8 = G8
        self.HPG = HPG
        self.P = 64
        self.N = 128
        self.NH = G8 * HPG
        self.DI = self.NH * self.P
        self.CONV = self.DI + 2 * G8 * self.N
        self.debug = debug
        self.layers = layers
        self.alpha = (2 * 2) ** 0.25
        self.CB = 4 if D < 4096 else 8


ARENA_WORDS = 51 * 1024


class MK:
    def __init__(self, cfg):
        self.cfg = cfg
        self.nc = bass.Bass("TRN2", target_bir_lowering=False)
        self.stack = ExitStack()
        self.P = Prog(self.nc, self.stack)
        self.arena = self.stack.enter_context(self.nc.sbuf_tensor("arena", [128, ARENA_WORDS], F32))
        self.top = 0
        self.marks = []
        self.banks = []
        for i in range(8):
            t = self.stack.enter_context(self.nc.psum_tensor(f"bank{i}", [128, 512], F32))
            self.banks.append((t, Buf(f"bank{i}", excl=True)))
        self.dram = {}
        self.scr_kind = "ExternalOutput" if cfg.debug else "Internal"

    def alloc(self, name, cols, dt=F32):
        words = cols if dt == F32 else (cols + 1) // 2
        words = (words + 7) // 8 * 8
        a = self.top
        self.top += words
        assert self.top <= ARENA_WORDS, f"SBUF arena overflow at {name}: {self.top}"
        ap = self.arena[:, a:a + words]
        if dt != F32:
            ap = ap.bitcast(dt)[:, :cols]
        else:
            ap = ap[:, :cols]
        return ap, Buf(name)

    def push(self):
        self.marks.append(self.top)

    def pop(self):
        self.P.barrier()
        self.top = self.marks.pop()

    def din(self, name, shape, dt=F32):
        t = self.nc.dram_tensor(name, list(shape), dt, kind="ExternalInput")
        self.dram[name] = t
        return t.ap()

    def dout(self, name, shape, dt=F32):
        t = self.nc.dram_tensor(name, list(shape), dt, kind="ExternalOutput")
        self.dram[name] = t
        return t.ap()

    def dscr(self, name, shape, dt=F32):
        t = self.nc.dram_tensor(name, list(shape), dt, kind=self.scr_kind)
        self.dram[name] = t
        return t.ap()

    def bank(self, i, dt=F32):
        t, b = self.banks[i]
        ap = t[:]
        if dt != F32:
            ap = ap.bitcast(dt)
        return ap, b

    def setup_consts(self, ident_d):
        P = self.P
        self.ident_f, b1 = self.alloc("ident_f", 128, F32)
        self.ident_b, b2 = self.alloc("ident_b", 128, BF16)
        self.ones_f, b3 = self.alloc("ones_f", 128, F32)
        self.b_ident_f, self.b_ident_b, self.b_ones = b1, b2, b3
        P.op('sp', lambda e: e.dma_start(out=self.ident_f, in_=ident_d), writes=[b1], dma=b1)
        P.op('pool', lambda e: e.dma_start(out=self.ident_b, in_=ident_d), writes=[b2], dma=b2)
        P.op('dve', lambda e: e.memset(self.ones_f, 1.0), writes=[b3])

    def cast_dram(self, dst, src, rows, bufd, nsplit=8):
        P = self.P
        cols = src.shape[1]
        c = min(cols, 4096)
        a = cols // c
        if a > 1:
            src = src.rearrange("r (a c) -> (r a) c", c=c)
            dst = dst.rearrange("r (a c) -> (r a) c", c=c)
        R = rows * a
        step = min(R, 256)
        first = True
        for i in range(0, R, step):
            P.op('pool', lambda e, i=i: e.dma_start(out=dst[i:i + step, :], in_=src[i:i + step, :]),
                 writes=[bufd], dma=bufd, partial=not first)
            first = False

    def gemm_fm(self, W_d, xT_d, b_xT, KC, T, NBLK, CB, epilogue, tag):
        P = self.P
        self.push()
        NSUP = (NBLK + CB - 1) // CB
        TB = T // 512
        wsb = [self.alloc(f"{tag}_w{i}", KC * CB * 128, BF16) for i in range(2)]
        xsb = [self.alloc(f"{tag}_x{i}", KC * 512, BF16) for i in range(2)]
        xcnt = 0
        bcnt = 0
        for s in range(NSUP):
            w_ap, w_b = wsb[s % 2]
            w3 = w_ap.rearrange("p (k c) -> p k c", k=KC)
            nq = 4 if KC >= 4 else 1
            kq = KC // nq
            for q in range(nq):
                P.op('pool', lambda e, s=s, q=q, w3=w3: e.dma_start(
                    out=w3[:, q * kq:(q + 1) * kq, :], in_=W_d[s, :, q * kq:(q + 1) * kq, :]),
                    writes=[w_b], dma=w_b, partial=(q > 0))
            ncb = min(CB, NBLK - s * CB)
            for tb in range(TB):
                x_ap, x_b = xsb[xcnt % 2]
                xcnt += 1
                x3 = x_ap.rearrange("p (k t) -> p k t", k=KC)
                P.op('sp', lambda e, x3=x3, tb=tb: e.dma_start(
                    out=x3, in_=xT_d[:, tb * 512:(tb + 1) * 512].rearrange("(k p) t -> p k t", p=128)),
                    reads=[b_xT], writes=[x_b], dma=x_b)
                for cb in range(ncb):
                    ps, ps_b = self.bank(bcnt % 4)
                    bcnt += 1
                    for k in range(KC):
                        P.op('pe', lambda e, ps=ps, w3=w3, x3=x3, k=k, cb=cb: e.matmul(
                            ps, lhsT=w3[:, k, cb * 128:(cb + 1) * 128], rhs=x3[:, k, :],
                            start=(k == 0), stop=(k == KC - 1)),
                            reads=[w_b, x_b], writes=[ps_b], partial=(k > 0))
                    epilogue(s * CB + cb, tb, ps, ps_b)
        self.pop()

    def l0_inproj(self, W_d, xT_d, b_xT):
        cfg, P = self.cfg, self.P
        self.QKVG = self.dscr("QKVG", [cfg.NBLK0, 128, cfg.S], BF16)
        self.b_qkvg = [Buf(f"qkvg{i}") for i in range(cfg.NBLK0)]
        self.push()
        stg = [self.alloc(f"l0stg{i}", 512, BF16) for i in range(4)]
        st = {'c': 0}

        def epi(blk, tb, ps, ps_b):
            s_ap, s_b = stg[st['c'] % 4]
            eng = 'act' if st['c'] % 2 == 0 else 'dve'
            st['c'] += 1
            if eng == 'act':
                P.op('act', lambda e: e.copy(out=s_ap, in_=ps), reads=[ps_b], writes=[s_b])
            else:
                P.op('dve', lambda e: e.tensor_copy(out=s_ap, in_=ps), reads=[ps_b], writes=[s_b])
            P.op('sp', lambda e: e.dma_start(out=self.QKVG[blk, :, tb * 512:(tb + 1) * 512], in_=s_ap),
                 reads=[s_b], writes=[self.b_qkvg[blk]], dma=s_b, partial=True)

        self.gemm_fm(W_d, xT_d, b_xT, cfg.KC, cfg.S, cfg.NBLK0, cfg.CB, epi, "g0")
        self.pop()

    def l0_bias_tables(self, relb_d, onehot_d):
        cfg, P = self.cfg, self.P
        HG = cfg.HG
        self.BT = self.dscr("BT", [3, HG, 128 * 256], F32)
        self.b_bt = Buf("BT")
        self.push()
        rb, rb_b = self.alloc("rb", 3 * HG, F32)
        P.op('dve', lambda e: e.memset(rb[0:64, :], -1e30), writes=[rb_b])
        P.op('sp', lambda e: e.dma_start(out=rb[0:32, :], in_=relb_d), writes=[rb_b], dma=rb_b)
        oh = [self.alloc(f"oh{i}", 2048, F32) for i in range(2)]
        stg = [self.alloc(f"btst{i}", 2048, F32) for i in range(2)]
        c = 0
        for g in range(3):
            for ch in range(16):
                o_ap, o_b = oh[c % 2]
                s_ap, s_b = stg[c % 2]
                c += 1
                P.op('sp', lambda e, o_ap=o_ap, g=g, ch=ch: e.dma_start(
                    out=o_ap[0:33, :], in_=onehot_d[g, :, ch * 2048:(ch + 1) * 2048]),
                    writes=[o_b], dma=o_b)
                for j in range(4):
                    ps, ps_b = self.bank(j)
                    P.op('pe', lambda e, ps=ps, o_ap=o_ap, g=g, j=j: e.matmul(
                        ps[0:HG, :], lhsT=rb[0:33, g * HG:(g + 1) * HG], rhs=o_ap[0:33, j * 512:(j + 1) * 512],
                        start=True, stop=True), reads=[rb_b, o_b], writes=[ps_b])
                    P.op('act' if j % 2 == 0 else 'dve',
                         (lambda e, ps=ps, s_ap=s_ap, j=j: e.copy(out=s_ap[0:HG, j * 512:(j + 1) * 512], in_=ps[0:HG, :]))
                         if j % 2 == 0 else
                         (lambda e, ps=ps, s_ap=s_ap, j=j: e.tensor_copy(out=s_ap[0:HG, j * 512:(j + 1) * 512], in_=ps[0:HG, :])),
                         reads=[ps_b], writes=[s_b], partial=(j > 0))
                P.op('sp', lambda e, s_ap=s_ap, g=g, ch=ch: e.dma_start(
                    out=self.BT[g, :, ch * 2048:(ch + 1) * 2048], in_=s_ap[0:HG, :]),
                    reads=[s_b], writes=[self.b_bt], dma=s_b, partial=True)
        self.pop()

    def l0_attention(self):
        cfg, P = self.cfg, self.P
        S, HG = cfg.S, cfg.HG
        H2 = S // 2
        scale = 128 ** -0.5
        self.Y0 = self.dscr("Y0", [cfg.DATT, S], BF16)
        self.b_y0 = Buf("Y0")
        self.push()
        QT = [self.alloc(f"QT{g}", S, BF16) for g in range(3)]
        KT = [self.alloc(f"KT{g}", S, BF16) for g in range(3)]
        VTd = self.alloc("VTd", S, BF16)
        ld = [self.alloc(f"ld{i}", S, BF16) for i in range(2)]
        Vtok = [self.alloc(f"Vtok{g}", S, BF16) for g in range(3)]
        BM = [self.alloc(f"BM{g}", 256, F32) for g in range(3)]
        OT = [self.alloc(f"OT{g}", H2, F32) for g in range(3)]
        LSE = [self.alloc(f"LSE{g}", H2, F32) for g in range(3)]
        gT = self.alloc("gT", S, BF16)
        yT = self.alloc("yT", S, BF16)
        NR = 2
        Sb = [self.alloc(f"Sb{i}", 256, F32) for i in range(NR)]
        Pf = [self.alloc(f"Pf{i}", 256, F32) for i in range(NR)]
        Pn = [self.alloc(f"Pn{i}", 256, BF16) for i in range(NR)]
        PTs = [self.alloc(f"PTs{i}", 256, BF16) for i in range(NR)]
        Bl = [self.alloc(f"Bl{i}", 128, F32) for i in range(NR)]
        negm = [self.alloc(f"negm{i}", 1, F32) for i in range(NR)]
        den = [self.alloc(f"den{i}", 1, F32) for i in range(NR)]
        rden = [self.alloc(f"rden{i}", 1, F32) for i in range(NR)]
        lnd = [self.alloc(f"lnd{i}", 1, F32) for i in range(NR)]
        lse = [self.alloc(f"lse{i}", 1, F32) for i in range(NR)]
        CW = 512
        tmp = [self.alloc(f"ctmp{i}", CW, F32) for i in range(6)]
        Sps = [self.bank(0), self.bank(1)]
        oTps = [self.bank(2), self.bank(3)]
        LBps = [self.bank(4)]
        VTps = self.bank(5, BF16)
        PTps = [self.bank(6, BF16), self.bank(7, BF16)]
        ident_f, ident_b, ones_f = self.ident_f, self.ident_b, self.ones_f
        bif, bib, bon = self.b_ident_f, self.b_ident_b, self.b_ones
        ldc = {'c': 0}
        bc = {'c': 0}

        def load_fm(blk, g, dst, dst_b):
            d = cfg.patterns[g][1]
            src = self.QKVG[blk]
            if d == 1:
                P.op('sp', lambda e: e.dma_start(out=dst, in_=src), reads=[self.b_qkvg[blk]], writes=[dst_b], dma=dst_b)
                return
            l_ap, l_b = ld[ldc['c'] % 2]
            ldc['c'] += 1
            P.op('sp', lambda e: e.dma_start(out=l_ap, in_=src), reads=[self.b_qkvg[blk]], writes=[l_b], dma=l_b)
            L = S // d
            P.op('pool', lambda e: e.tensor_copy(out=dst.rearrange("p (r j) -> p j r", j=L),
                                                 in_=l_ap.rearrange("p (j r) -> p j r", r=d)),
                 reads=[l_b], writes=[dst_b])

        for h in range(HG):
            base = h * 10
            for g in range(3):
                load_fm(base + 3 * g + 0, g, QT[g][0], QT[g][1])
                load_fm(base + 3 * g + 1, g, KT[g][0], KT[g][1])
                load_fm(base + 3 * g + 2, g, VTd[0], VTd[1])
                v_ap, v_b = Vtok[g]
                vps, vps_b = VTps
                for q4 in range(S // 512):
                    for j in range(4):
                        blk = q4 * 4 + j
                        P.op('pe', lambda e, blk=blk, j=j: e.transpose(
                            vps[:, j * 128:(j + 1) * 128], VTd[0][:, blk * 128:(blk + 1) * 128], ident_b),
                            reads=[VTd[1], bib], writes=[vps_b], partial=(j > 0))
                    P.op('act', lambda e, q4=q4, v_ap=v_ap: e.copy(out=v_ap[:, q4 * 512:(q4 + 1) * 512], in_=vps[:, 0:512]),
                         reads=[vps_b], writes=[v_b], partial=(q4 > 0))
                bm_ap, bm_b = BM[g]
                P.op('sp', lambda e, g=g, h=h, bm_ap=bm_ap: e.dma_start(
                    out=bm_ap, in_=self.BT[g, h, :].rearrange("(q k) -> q k", k=256)),
                    reads=[self.b_bt], writes=[bm_b], dma=bm_b)
            P.op('sp', lambda e, base=base: e.dma_start(out=gT[0], in_=self.QKVG[base + 9]),
                 reads=[self.b_qkvg[base + 9]], writes=[gT[1]], dma=gT[1])

            for half in range(2):
                for g in range(3):
                    d = cfg.patterns[g][1]
                    L = S // d
                    nb = L // 128
                    hb = nb // 2
                    q_ap, q_b = QT[g]
                    k_ap, k_b = KT[g]
                    v_ap, v_b = Vtok[g]
                    bm_ap, bm_b = BM[g]
                    o_ap, o_b = OT[g]
                    l_ap, l_b = LSE[g]
                    o3 = o_ap.rearrange("p (j r) -> p j r", r=d)
                    l3 = l_ap.rearrange("p (j r) -> p j r", r=d)
                    first = True
                    for r in range(d):
                        for jb in range(half * hb, (half + 1) * hb):
                            s = bc['c'] % NR
                            bc['c'] += 1
                            c0 = r * L + jb * 128
                            nk = 256 if jb > 0 else 128
                            k0 = c0 - 128 if jb > 0 else c0
                            bmv = bm_ap[:, 0:256] if jb > 0 else bm_ap[:, 128:256]
                            sps, sps_b = Sps[s]
                            P.op('pe', lambda e, sps=sps, q_ap=q_ap, k_ap=k_ap, c0=c0, k0=k0, nk=nk: e.matmul(
                                sps[:, :nk], lhsT=q_ap[:, c0:c0 + 128], rhs=k_ap[:, k0:k0 + nk], start=True, stop=True),
                                reads=[q_b, k_b], writes=[sps_b])
                            sb, sb_b = Sb[s]
                            P.op('dve', lambda e, sb=sb, sps=sps, bmv=bmv, nk=nk: e.scalar_tensor_tensor(
                                out=sb[:, :nk], in0=sps[:, :nk], scalar=scale, in1=bmv, op0=ALU.mult, op1=ALU.add),
                                reads=[sps_b, bm_b], writes=[sb_b])
                            nm, nm_b = negm[s]
                            P.op('dve', lambda e, nm=nm, sb=sb, nk=nk: e.tensor_reduce(
                                out=nm, in_=sb[:, :nk], axis=AX.X, op=ALU.max, negate=True),
                                reads=[sb_b], writes=[nm_b])
                            pf, pf_b = Pf[s]
                            dn, dn_b = den[s]
                            P.op('act', lambda e, pf=pf, sb=sb, nm=nm, dn=dn, nk=nk: e.activation(
                                out=pf[:, :nk], in_=sb[:, :nk], func=AF.Exp, bias=nm, scale=1.0, accum_out=dn),
                                reads=[sb_b, nm_b], writes=[pf_b, dn_b])
                            rd, rd_b = rden[s]
                            P.op('dve', lambda e, rd=rd, dn=dn: e.reciprocal(out=rd, in_=dn), reads=[dn_b], writes=[rd_b])
                            ln_, ln_b = lnd[s]
                            P.op('act', lambda e, ln_=ln_, dn=dn: e.activation(out=ln_, in_=dn, func=AF.Ln),
                                 reads=[dn_b], writes=[ln_b])
                            ls, ls_b = lse[s]
                            P.op('dve', lambda e, ls=ls, ln_=ln_, nm=nm: e.tensor_tensor(out=ls, in0=ln_, in1=nm, op=ALU.subtract),
                                 reads=[ln_b, nm_b], writes=[ls_b])
                            pn, pn_b = Pn[s]
                            P.op('dve', lambda e, pn=pn, pf=pf, rd=rd, nk=nk: e.tensor_scalar(
                                out=pn[:, :nk], in0=pf[:, :nk], scalar1=rd, scalar2=None, op0=ALU.mult),
                                reads=[pf_b, rd_b], writes=[pn_b])
                            ptp, ptp_b = PTps[s]
                            ncn = nk // 128
                            for c in range(ncn):
                                P.op('pe', lambda e, ptp=ptp, pn=pn, c=c: e.transpose(
                                    ptp[:, c * 128:(c + 1) * 128], pn[:, c * 128:(c + 1) * 128], ident_b),
                                    reads=[pn_b, bib], writes=[ptp_b], partial=(c > 0))
                            pts, pts_b = PTs[s]
                            P.op('act', lambda e, pts=pts, ptp=ptp, nk=nk: e.copy(out=pts[:, :nk], in_=ptp[:, :nk]),
                                 reads=[ptp_b], writes=[pts_b])
                            otp, otp_b = oTps[s]
                            kb = (k0 // 128)
                            for c in range(ncn):
                                P.op('pe', lambda e, otp=otp, v_ap=v_ap, pts=pts, c=c, kb=kb, ncn=ncn: e.matmul(
                                    otp[:, :128], lhsT=v_ap[:, (kb + c) * 128:(kb + c + 1) * 128],
                                    rhs=pts[:, c * 128:(c + 1) * 128], start=(c == 0), stop=(c == ncn - 1)),
                                    reads=[v_b, pts_b], writes=[otp_b], partial=(c > 0))
                            jl = jb - half * hb
                            P.op('act', lambda e, o3=o3, otp=otp, jl=jl, r=r: e.copy(
                                out=o3[:, jl * 128:(jl + 1) * 128, r], in_=otp[:, :128]),
                                reads=[otp_b], writes=[o_b], partial=(not first))
                            bl, bl_b = Bl[s]
                            P.op('dve', lambda e, bl=bl, ls=ls: e.tensor_scalar(
                                out=bl, in0=ones_f, scalar1=ls, scalar2=None, op0=ALU.mult),
                                reads=[ls_b, bon], writes=[bl_b])
                            lbp, lbp_b = LBps[0]
                            P.op('pe', lambda e, lbp=lbp, bl=bl: e.transpose(lbp[:, :128], bl, ident_f),
                                 reads=[bl_b, bif], writes=[lbp_b])
                            P.op('dve', lambda e, l3=l3, lbp=lbp, jl=jl, r=r: e.tensor_copy(
                                out=l3[:, jl * 128:(jl + 1) * 128, r], in_=lbp[:, :128]),
                                reads=[lbp_b], writes=[l_b], partial=(not first))
                            first = False
                for cc in range(H2 // CW):
                    sl = slice(cc * CW, (cc + 1) * CW)
                    gsl = slice(half * H2 + cc * CW, half * H2 + (cc + 1) * CW)
                    (mx, mx_b), (e0, e0_b), (e1, e1_b), (e2, e2_b), (zz, zz_b), (acc, acc_b) = tmp
                    ee = [(e0, e0_b), (e1, e1_b), (e2, e2_b)]
                    P.op('dve', lambda e, sl=sl: e.tensor_tensor(out=mx, in0=LSE[0][0][:, sl], in1=LSE[1][0][:, sl], op=ALU.max),
                         reads=[LSE[0][1], LSE[1][1]], writes=[mx_b])
                    P.op('dve', lambda e, sl=sl: e.tensor_tensor(out=mx, in0=mx, in1=LSE[2][0][:, sl], op=ALU.max),
                         reads=[LSE[2][1], mx_b], writes=[mx_b])
                    for g in range(3):
                        ea, ea_b = ee[g]
                        P.op('pool', lambda e, ea=ea, g=g, sl=sl: e.tensor_tensor(out=ea, in0=LSE[g][0][:, sl], in1=mx, op=ALU.subtract),
                             reads=[LSE[g][1], mx_b], writes=[ea_b])
                        P.op('act', lambda e, ea=ea: e.activation(out=ea, in_=ea, func=AF.Exp), reads=[ea_b], writes=[ea_b])
                    P.op('pool', lambda e: e.tensor_tensor(out=zz, in0=e0, in1=e1, op=ALU.add), reads=[e0_b, e1_b], writes=[zz_b])
                    P.op('pool', lambda e: e.tensor_tensor(out=zz, in0=zz, in1=e2, op=ALU.add), reads=[zz_b, e2_b], writes=[zz_b])
                    P.op('dve', lambda e: e.reciprocal(out=zz, in_=zz), reads=[zz_b], writes=[zz_b])
                    for g in range(3):
                        ea, ea_b = ee[g]
                        P.op('dve', lambda e, ea=ea, g=g, sl=sl: e.tensor_tensor(out=ea, in0=ea, in1=OT[g][0][:, sl], op=ALU.mult),
                             reads=[ea_b, OT[g][1]], writes=[ea_b])
                    P.op('pool', lambda e: e.tensor_tensor(out=acc, in0=e0, in1=e1, op=ALU.add), reads=[e0_b, e1_b], writes=[acc_b])
                    P.op('pool', lambda e: e.tensor_tensor(out=acc, in0=acc, in1=e2, op=ALU.add), reads=[acc_b, e2_b], writes=[acc_b])
                    P.op('dve', lambda e: e.tensor_tensor(out=acc, in0=acc, in1=zz, op=ALU.mult), reads=[acc_b, zz_b], writes=[acc_b])
                    P.op('act', lambda e, gsl=gsl: e.activation(out=e0, in_=gT[0][:, gsl], func=AF.Exp, scale=-1.0),
                         reads=[gT[1]], writes=[e0_b])
                    P.op('pool', lambda e: e.tensor_scalar(out=e0, in0=e0, scalar1=1.0, scalar2=1.0, op0=ALU.add, op1=ALU.mult),
                         reads=[e0_b], writes=[e0_b])
                    P.op('dve', lambda e: e.reciprocal(out=e0, in_=e0), reads=[e0_b], writes=[e0_b])
                    P.op('dve', lambda e, gsl=gsl: e.tensor_tensor(out=e0, in0=e0, in1=gT[0][:, gsl], op=ALU.mult),
                         reads=[e0_b, gT[1]], writes=[e0_b])
                    P.op('dve', lambda e, gsl=gsl: e.tensor_tensor(out=yT[0][:, gsl], in0=acc, in1=e0, op=ALU.mult),
                         reads=[acc_b, e0_b], writes=[yT[1]], partial=True)
            P.op('sp', lambda e, h=h: e.dma_start(out=self.Y0[h * 128:(h + 1) * 128, :], in_=yT[0]),
                 reads=[yT[1]], writes=[self.b_y0], dma=yT[1], partial=True)
        self.pop()

    def outproj_ln(self, Y_d, b_y, W_d, b_w, KCE, x_d, b_x, g_d, bt_d, out_d, b_out, outT_d, b_outT, tag):
        cfg, P = self.cfg, self.P
        S, D = cfg.S, cfg.D
        NDB = D // 512
        KCD = D // 128
        KG = 16 if KCE >= 16 else KCE
        NKG = KCE // KG
        TS = 256
        NSUB = TS // 128
        self.push()
        Gr, Gr_b = self.alloc(f"{tag}G", D, F32)
        Br, Br_b = self.alloc(f"{tag}B", D, F32)
        P.op('sp', lambda e: e.dma_start(out=Gr, in_=g_d.partition_broadcast(128)), writes=[Gr_b], dma=Gr_b)
        P.op('sp', lambda e: e.dma_start(out=Br, in_=bt_d.partition_broadcast(128)), writes=[Br_b], dma=Br_b)
        ysb = [self.alloc(f"{tag}y{i}", KCE * TS, BF16) for i in range(2)]
        wsb = [self.alloc(f"{tag}w{i}", KG * 512, BF16) for i in range(3)]
        xr = [self.alloc(f"{tag}xr{i}", 512, F32) for i in range(4)]
        v = [self.alloc(f"{tag}v{i}", D, F32) for i in range(NSUB)]
        stats = [self.alloc(f"{tag}st{i}", 6 * NDB, F32) for i in range(NSUB)]
        mv = [self.alloc(f"{tag}mv{i}", 2, F32) for i in range(NSUB)]
        rstd = [self.alloc(f"{tag}rs{i}", 1, F32) for i in range(NSUB)]
        if outT_d is not None:
            xb = [self.alloc(f"{tag}xb{i}", D, BF16) for i in range(NSUB)]
            xts = [self.alloc(f"{tag}xt{i}", KCD * TS, BF16) for i in range(2)]
        wc = 0
        xc = 0
        for ts in range(S // TS):
            y_ap, y_b = ysb[ts % 2]
            y3 = y_ap.rearrange("p (k t) -> p k t", k=KCE)
            P.op('sp', lambda e, y3=y3, ts=ts: e.dma_start(
                out=y3, in_=Y_d[:, ts * TS:(ts + 1) * TS].rearrange("(k p) t -> p k t", p=128)),
                reads=[b_y], writes=[y_b], dma=y_b)
            for db in range(NDB):
                for kg in range(NKG):
                    w_ap, w_b = wsb[wc % 3]
                    wc += 1
                    w3 = w_ap.rearrange("p (k c) -> p k c", k=KG)
                    P.op('sp', lambda e, w3=w3, db=db, kg=kg: e.dma_start(
                        out=w3, in_=W_d[db * 128:(db + 1) * 128, kg * KG * 512:(kg + 1) * KG * 512].rearrange("p (k c) -> p k c", k=KG)), reads=[b_w], writes=[w_b], dma=w_b)
                    for k in range(KG):
                        ka = kg * KG + k
                        for sub in range(NSUB):
                            ps, ps_b = self.bank(sub + 2 * (db % 2))
                            P.op('pe', lambda e, ps=ps, y3=y3, w3=w3, ka=ka, k=k, sub=sub: e.matmul(
                                ps, lhsT=y3[:, ka, sub * 128:(sub + 1) * 128], rhs=w3[:, k, :],
                                start=(ka == 0), stop=(ka == KCE - 1)),
                                reads=[y_b, w_b], writes=[ps_b], partial=(ka > 0))
                for sub in range(NSUB):
                    ps, ps_b = self.bank(sub + 2 * (db % 2))
                    x_ap, x_b = xr[xc % 4]
                    xc += 1
                    r0 = ts * TS + sub * 128
                    P.op('sp', lambda e, x_ap=x_ap, r0=r0, db=db: e.dma_start(
                        out=x_ap, in_=x_d[r0:r0 + 128, db * 512:(db + 1) * 512]), reads=[b_x], writes=[x_b], dma=x_b)
                    v_ap, v_b = v[sub]
                    P.op('dve', lambda e, v_ap=v_ap, x_ap=x_ap, ps=ps, db=db: e.scalar_tensor_tensor(
                        out=v_ap[:, db * 512:(db + 1) * 512], in0=x_ap, scalar=cfg.alpha, in1=ps,
                        op0=ALU.mult, op1=ALU.add), reads=[x_b, ps_b], writes=[v_b], partial=(db > 0))
                    s_ap, s_b = stats[sub]
                    P.op('dve', lambda e, s_ap=s_ap, v_ap=v_ap, db=db: e.bn_stats(
                        out=s_ap[:, db * 6:(db + 1) * 6], in_=v_ap[:, db * 512:(db + 1) * 512]),
                        reads=[v_b], writes=[s_b], partial=(db > 0))
            for sub in range(NSUB):
                v_ap, v_b = v[sub]
                s_ap, s_b = stats[sub]
                m_ap, m_b = mv[sub]
                r_ap, r_b = rstd[sub]
                P.op('dve', lambda e, m_ap=m_ap, s_ap=s_ap: e.bn_aggr(out=m_ap, in_=s_ap), reads=[s_b], writes=[m_b])
                P.op('act', lambda e, r_ap=r_ap, m_ap=m_ap: e.activation(out=r_ap, in_=m_ap[:, 1:2], func=AF.Ln, bias=self.eps_ap, scale=1.0),
                     reads=[m_b, self.b_eps], writes=[r_b])
                P.op('act', lambda e, r_ap=r_ap: e.activation(out=r_ap, in_=r_ap, func=AF.Exp, scale=-0.5),
                     reads=[r_b], writes=[r_b])
                P.op('dve', lambda e, v_ap=v_ap, m_ap=m_ap, r_ap=r_ap: e.tensor_scalar(
                    out=v_ap, in0=v_ap, scalar1=m_ap[:, 0:1], scalar2=r_ap, op0=ALU.subtract, op1=ALU.mult),
                    reads=[v_b, m_b, r_b], writes=[v_b])
                P.op('pool', lambda e, v_ap=v_ap: e.tensor_tensor(out=v_ap, in0=v_ap, in1=Gr, op=ALU.mult),
                     reads=[v_b, Gr_b], writes=[v_b])
                P.op('pool', lambda e, v_ap=v_ap: e.tensor_tensor(out=v_ap, in0=v_ap, in1=Br, op=ALU.add),
                     reads=[v_b, Br_b], writes=[v_b])
                r0 = ts * TS + sub * 128
                P.op('sp', lambda e, v_ap=v_ap, r0=r0: e.dma_start(out=out_d[r0:r0 + 128, :], in_=v_ap),
                     reads=[v_b], writes=[b_out], dma=v_b, partial=True)
                if outT_d is not None:
                    xb_ap, xb_b = xb[sub]
                    P.op('act', lambda e, xb_ap=xb_ap, v_ap=v_ap: e.copy(out=xb_ap, in_=v_ap), reads=[v_b], writes=[xb_b])
                    xt_ap, xt_b = xts[ts % 2]
                    xt3 = xt_ap.rearrange("p (k t) -> p k t", k=KCD)
                    for q4 in range(KCD // 4):
                        tp, tp_b = self.bank(5 + (q4 % 2), BF16)
                        for j in range(4):
                            kk = q4 * 4 + j
                            P.op('pe', lambda e, tp=tp, xb_ap=xb_ap, kk=kk, j=j: e.transpose(
                                tp[:, j * 128:(j + 1) * 128], xb_ap[:, kk * 128:(kk + 1) * 128], self.ident_b),
                                reads=[xb_b, self.b_ident_b], writes=[tp_b], partial=(j > 0))
                        P.op('act', lambda e, tp=tp, xt3=xt3, q4=q4, sub=sub: e.copy(
                            out=xt3[:, q4 * 4:(q4 + 1) * 4, sub * 128:(sub + 1) * 128],
                            in_=tp[:, 0:512].rearrange("p (k t) -> p k t", k=4)),
                            reads=[tp_b], writes=[xt_b], partial=not (sub == 0 and q4 == 0))
            if outT_d is not None:
                xt_ap, xt_b = xts[ts % 2]
                xt3 = xt_ap.rearrange("p (k t) -> p k t", k=KCD)
                P.op('sp', lambda e, xt3=xt3, ts=ts: e.dma_start(
                    out=outT_d[:, ts * TS:(ts + 1) * TS].rearrange("(k p) t -> p k t", p=128), in_=xt3),
                    reads=[xt_b], writes=[b_outT], dma=xt_b, partial=True)
        self.pop()

    def l1_inproj(self, W_d, xT_d, b_xT):
        cfg, P = self.cfg, self.P
        NB = (cfg.DI + cfg.CONV + cfg.NH) // 128
        self.NBLK1 = NB
        self.ZXs = []
        for i in range(0, NB, 64):
            n = min(64, NB - i)
            self.ZXs.append(self.dscr(f"ZX{i // 64}", [n, 128, cfg.S], F32))
        self.b_zx = [Buf(f"zx{i}") for i in range(NB)]
        self.push()
        stg = [self.alloc(f"l1stg{i}", 512, F32) for i in range(4)]
        st = {'c': 0}

        def epi(blk, tb, ps, ps_b):
            s_ap, s_b = stg[st['c'] % 4]
            eng = 'act' if st['c'] % 2 == 0 else 'dve'
            st['c'] += 1
            if eng == 'act':
                P.op('act', lambda e: e.copy(out=s_ap, in_=ps), reads=[ps_b], writes=[s_b])
            else:
                P.op('dve', lambda e: e.tensor_copy(out=s_ap, in_=ps), reads=[ps_b], writes=[s_b])
            P.op('sp', lambda e: e.dma_start(out=self.zx(blk, 1)[0, :, tb * 512:(tb + 1) * 512], in_=s_ap),
                 reads=[s_b], writes=[self.b_zx[blk]], dma=s_b, partial=True)

        self.gemm_fm(W_d, xT_d, b_xT, cfg.KC, cfg.S, NB, cfg.CB, epi, "g1")
        self.pop()

    def zx(self, b0, nb):
        t = self.ZXs[b0 // 64]
        l0 = b0 % 64
        assert l0 + nb <= 64
        return t[l0:l0 + nb]

    def l1_ssd(self, convw_d, convb_d, dtb_d, alog_d, dsk_d, nw_d, triu_d, smask_d):
        cfg, P = self.cfg, self.P
        S, G8, HPG, NH = cfg.S, cfg.G8, cfg.HPG, cfg.NH
        XB = HPG * 64 // 128
        NXB = G8 * XB
        UB = XB + 2
        NCONV = NXB + 2 * G8
        ZB0, XB0 = 0, NXB
        BB0 = 2 * NXB
        CB0 = BB0 + G8
        DTB = CB0 + G8
        self.Y1 = self.dscr("Y1", [cfg.DI, S], BF16)
        self.b_y1 = Buf("Y1")
        self.push()
        triu, triu_b = self.alloc("triu", 128, F32)
        smask, smask_b = self.alloc("smask", 128, F32)
        ones_b, ones_bb = self.alloc("ones_b", 128, BF16)
        cw, cw_b = self.alloc("convw", NCONV * 4, F32)
        cb_, cb_b = self.alloc("convb", NCONV, F32)
        dsk, dsk_b = self.alloc("dsk", NXB, F32)
        nw, nw_b = self.alloc("nw", NXB, F32)
        dtb, dtb_b = self.alloc("dtb", 1, F32)
        acol, acol_b = self.alloc("acol", 1, F32)
        for ap_, b_, src in ((triu, triu_b, triu_d), (smask, smask_b, smask_d), (cw, cw_b, convw_d), (cb_, cb_b, convb_d),
                             (dsk, dsk_b, dsk_d), (nw, nw_b, nw_d), (dtb, dtb_b, dtb_d), (acol, acol_b, alog_d)):
            P.op('sp', lambda e, ap_=ap_, src=src: e.dma_start(out=ap_, in_=src), writes=[b_], dma=b_)
        P.op('dve', lambda e: e.memset(ones_b, 1.0), writes=[ones_bb])
        P.op('act', lambda e: e.activation(out=acol, in_=acol, func=AF.Exp), reads=[acol_b], writes=[acol_b])
        P.op('dve', lambda e: e.tensor_scalar(out=acol, in0=acol, scalar1=-1.0, scalar2=None, op0=ALU.mult),
             reads=[acol_b], writes=[acol_b])
        cw3 = cw.rearrange("p (b k) -> p b k", k=4)
        st, st_b = self.alloc("state", NH * 64, F32)
        stb, stb_b = self.alloc("stateb", NH * 64, BF16)
        st_bs = [Buf(f"st{h}") for h in range(NH)]
        stb_bs = [Buf(f"stb{h}") for h in range(NH)]
        P.op('dve', lambda e: e.memset(st, 0.0), writes=st_bs)
        P.op('pool', lambda e: e.memset(stb, 0.0), writes=stb_bs)
        def ring(name, cols, dt=F32, n=2):
            return [self.alloc(f"{name}{i}", cols, dt) for i in range(n)]
        dtr = ring("dtr", 128); xb_ = ring("xb", 128); ax = ring("ax", 128); dtT = ring("dtT", 128)
        dtaT = ring("dtaT", 128); dt_tok = ring("dt_tok", 128); dta_tok = ring("dta_tok", 128)
        acum = ring("acum", 128); ea_tok = ring("ea_tok", 128); eend = ring("eend", 128)
        toend = ring("toend", 128); dtw_tok = ring("dtw_tok", 128)
        xin = ring("xin", UB * 131); xc = ring("xc", UB * 128); xcb = ring("xcb", UB * 128, BF16)
        zin = ring("zin", XB * 128); ych = ring("ych", XB * 128); ysq = ring("ysq", XB * 128, BF16)
        yb = ring("yb", XB * 128, BF16)
        xdt = ring("xdt", XB * 128, BF16); xdtw = ring("xdtw", XB * 128, BF16)
        btok = ring("btok", 128, BF16); cbTm = ring("cbTm", 128); rs = ring("rs", 128)
        Lm = ring("Lm", 128, F32, 3); dec = ring("dec", 128, F32, 3)
        GT = ring("GT", 128, BF16, 3); CTs = ring("CTs", 128, BF16, 3)
        def sub(bank, off, w, dt=F32):
            ap, b = self.bank(bank, dt)
            return (ap[:, off:off + w], b)
        seg_ps = [sub(0, 0, 128), sub(1, 0, 128)]
        eh_ps = [sub(2, 0, 128), sub(3, 0, 128)]
        y_ps = [sub(4, 0, 128)]
        s_ps = [sub(5, 0, 64)]
        misc_ps = [sub(6, 0, 128)]
        ss_ps = [sub(6, 128, 128), sub(6, 128, 128)]
        xt_ps = [sub(7, 0, 512, BF16)]
        bt_ps = [sub(7, 512, 128, BF16)]
        ident_f, ident_b, ones_f = self.ident_f, self.ident_b, self.ones_f
        bif, bib, bon = self.b_ident_f, self.b_ident_b, self.b_ones
        cnt = {'m': 0, 'h': 0, 'y': 0, 's': 0, 'x': 0}
        NCH = S // 128
        STOP = getattr(cfg, 'ssd_stop', 9)
        dlim = getattr(cfg, 'dt_lim', 10 ** 9)
        dcn = {'c': 0}

        def dop(*a, **k):
            dcn['c'] += 1
            if dcn['c'] <= dlim:
                P.op(*a, **k)
        for c in range(min(NCH, getattr(cfg, 'ssd_chunks', NCH))):
            t0 = c * 128
            r = c % 2
            if STOP >= 1:
                dop('sp', lambda e, r=r, t0=t0: e.dma_start(out=dtr[r][0], in_=self.zx(DTB, 1)[0, :, t0:t0 + 128]),
                     reads=[self.b_zx[DTB]], writes=[dtr[r][1]], dma=dtr[r][1])
                dop('dve', lambda e, r=r: e.tensor_scalar(out=xb_[r][0], in0=dtr[r][0], scalar1=dtb, scalar2=None, op0=ALU.add),
                     reads=[dtr[r][1], dtb_b], writes=[xb_[r][1]])
                dop('dve', lambda e, r=r: e.scalar_tensor_tensor(out=ax[r][0], in0=xb_[r][0], scalar=-1.0, in1=xb_[r][0], op0=ALU.mult, op1=ALU.max),
                     reads=[xb_[r][1]], writes=[ax[r][1]])
                dop('act', lambda e, r=r: e.activation(out=ax[r][0], in_=ax[r][0], func=AF.Exp, scale=-1.0),
                     reads=[ax[r][1]], writes=[ax[r][1]])
                dop('act', lambda e, r=r: e.activation(out=ax[r][0], in_=ax[r][0], func=AF.Ln, bias=ones_f[:, 0:1], scale=1.0),
                     reads=[ax[r][1], bon], writes=[ax[r][1]])
                dop('dve', lambda e, r=r: e.scalar_tensor_tensor(out=dtT[r][0], in0=xb_[r][0], scalar=0.0, in1=ax[r][0],
                                                                 op0=ALU.max, op1=ALU.add),
                     reads=[xb_[r][1], ax[r][1]], writes=[dtT[r][1]])
                dop('dve', lambda e, r=r: e.tensor_scalar(out=dtaT[r][0], in0=dtT[r][0], scalar1=acol, scalar2=None, op0=ALU.mult),
                     reads=[dtT[r][1], acol_b], writes=[dtaT[r][1]])
                for src, dst in ((dtT, dt_tok), (dtaT, dta_tok)):
                    mp, mp_b = misc_ps[cnt['m'] % len(misc_ps)]
                    cnt['m'] += 1
                    dop('pe', lambda e, mp=mp, src=src, r=r: e.transpose(mp, src[r][0], ident_f),
                         reads=[src[r][1], bif], writes=[mp_b])
                    dop('act', lambda e, mp=mp, dst=dst, r=r: e.copy(out=dst[r][0], in_=mp), reads=[mp_b], writes=[dst[r][1]])
                mp, mp_b = misc_ps[cnt['m'] % len(misc_ps)]
                cnt['m'] += 1
                dop('pe', lambda e, mp=mp, r=r: e.matmul(mp, lhsT=triu, rhs=dta_tok[r][0], start=True, stop=True),
                     reads=[triu_b, dta_tok[r][1]], writes=[mp_b])
                dop('dve', lambda e, mp=mp, r=r: e.tensor_copy(out=acum[r][0], in_=mp), reads=[mp_b], writes=[acum[r][1]])
                dop('act', lambda e, mp=mp, r=r: e.activation(out=ea_tok[r][0], in_=mp, func=AF.Exp),
                     reads=[mp_b], writes=[ea_tok[r][1]])
                mp2, mp2_b = misc_ps[cnt['m'] % len(misc_ps)]
                cnt['m'] += 1
                dop('pe', lambda e, mp2=mp2, r=r: e.matmul(mp2, lhsT=ones_f, rhs=dta_tok[r][0], start=True, stop=True),
                     reads=[bon, dta_tok[r][1]], writes=[mp2_b])
                dop('act', lambda e, mp2=mp2, r=r: e.activation(out=eend[r][0], in_=mp2, func=AF.Exp),
                     reads=[mp2_b], writes=[eend[r][1]])
                dop('dve', lambda e, mp2=mp2, r=r: e.tensor_tensor(out=toend[r][0], in0=mp2, in1=acum[r][0], op=ALU.subtract),
                     reads=[mp2_b, acum[r][1]], writes=[toend[r][1]])
                dop('act', lambda e, r=r: e.activation(out=toend[r][0], in_=toend[r][0], func=AF.Exp),
                     reads=[toend[r][1]], writes=[toend[r][1]])
                dop('dve', lambda e, r=r: e.tensor_tensor(out=dtw_tok[r][0], in0=dt_tok[r][0], in1=toend[r][0], op=ALU.mult),
                     reads=[dt_tok[r][1], toend[r][1]], writes=[dtw_tok[r][1]])
            for g in range(G8 if STOP >= 2 else 0):
                u = (c * G8 + g) % 2
                xin_ap, xin_b = xin[u]
                xin3 = xin_ap.rearrange("p (b t) -> p b t", t=131)
                xc_ap, xc_b = xc[u]
                xc3 = xc_ap.rearrange("p (b t) -> p b t", t=128)
                xcb_ap, xcb_b = xcb[u]
                xcb3 = xcb_ap.rearrange("p (b t) -> p b t", t=128)
                zin_ap, zin_b = zin[u]
                zin3 = zin_ap.rearrange("p (b t) -> p b t", t=128)
                srcs = [(XB0 + g * XB, XB, 0), (BB0 + g, 1, XB), (CB0 + g, 1, XB + 1)]
                first = True
                if c == 0:
                    P.op('pool', lambda e, xin3=xin3: e.memset(xin3[:, :, 0:3], 0.0), writes=[xin_b])
                    first = False
                for (b0, nb_, o0) in srcs:
                    lo = 0 if c > 0 else 3
                    P.op('sp', lambda e, xin3=xin3, b0=b0, nb_=nb_, o0=o0, lo=lo, t0=t0: e.dma_start(
                        out=xin3[:, o0:o0 + nb_, lo:131],
                        in_=self.zx(b0, nb_)[:, :, t0 - 3 + lo:t0 + 128].rearrange("b p t -> p b t")),
                        reads=[self.b_zx[b0 + i] for i in range(nb_)], writes=[xin_b], dma=xin_b, partial=(not first))
                    first = False
                P.op('sp', lambda e, zin3=zin3, g=g, t0=t0: e.dma_start(
                    out=zin3, in_=self.zx(ZB0 + g * XB, XB)[:, :, t0:t0 + 128].rearrange("b p t -> p b t")),
                    reads=[self.b_zx[ZB0 + g * XB + i] for i in range(XB)], writes=[zin_b], dma=zin_b)
                for bi in range(UB):
                    cblk = (g * XB + bi) if bi < XB else (NXB + g if bi == XB else NXB + G8 + g)
                    eng = 'dve'
                    P.op(eng, lambda e, xc3=xc3, xin3=xin3, bi=bi, cblk=cblk: e.tensor_scalar(
                        out=xc3[:, bi, :], in0=xin3[:, bi, 0:128], scalar1=cw3[:, cblk, 0:1], scalar2=cb_[:, cblk:cblk + 1],
                        op0=ALU.mult, op1=ALU.add), reads=[xin_b, cw_b, cb_b], writes=[xc_b], partial=(bi > 0))
                    for k in range(1, 4):
                        P.op(eng, lambda e, xc3=xc3, xin3=xin3, bi=bi, cblk=cblk, k=k: e.scalar_tensor_tensor(
                            out=xc3[:, bi, :], in0=xin3[:, bi, k:k + 128], scalar=cw3[:, cblk, k:k + 1], in1=xc3[:, bi, :],
                            op0=ALU.mult, op1=ALU.add), reads=[xin_b, cw_b, xc_b], writes=[xc_b], partial=True)
                P.op('act', lambda e, xc_ap=xc_ap: e.activation(out=xc_ap, in_=xc_ap, func=AF.Silu), reads=[xc_b], writes=[xc_b])
                P.op('act', lambda e, zin_ap=zin_ap: e.activation(out=zin_ap, in_=zin_ap, func=AF.Silu), reads=[zin_b], writes=[zin_b])
                P.op('pool', lambda e, xcb_ap=xcb_ap, xc_ap=xc_ap: e.tensor_copy(out=xcb_ap, in_=xc_ap), reads=[xc_b], writes=[xcb_b])
                if STOP < 3:
                    continue
                xdt_ap, xdt_b = xdt[u]
                xdtw_ap, xdtw_b = xdtw[u]
                for q in range(XB // 4):
                    xp, xp_b = xt_ps[cnt['x'] % len(xt_ps)]
                    cnt['x'] += 1
                    for j in range(4):
                        P.op('pe', lambda e, xp=xp, xcb3=xcb3, q=q, j=j: e.transpose(
                            xp[:, j * 128:(j + 1) * 128], xcb3[:, q * 4 + j, :], ident_b),
                            reads=[xcb_b, bib], writes=[xp_b], partial=(j > 0))
                    h0 = g * HPG + q * 8
                    for (dst_ap, dst_b, sc) in ((xdt_ap, xdt_b, dt_tok), (xdtw_ap, xdtw_b, dtw_tok)):
                        P.op('dve', lambda e, xp=xp, dst_ap=dst_ap, sc=sc, q=q, h0=h0, r=r: e.tensor_tensor(
                            out=dst_ap[:, q * 512:(q + 1) * 512].rearrange("p (h c) -> p h c", c=64),
                            in0=xp.rearrange("p (h c) -> p h c", c=64),
                            in1=sc[r][0][:, h0:h0 + 8].unsqueeze(2).to_broadcast([128, 8, 64]), op=ALU.mult),
                            reads=[xp_b, sc[r][1]], writes=[dst_b], partial=(q > 0))
                bp, bp_b = bt_ps[cnt['m'] % len(bt_ps)]
                P.op('pe', lambda e, bp=bp, xcb3=xcb3: e.transpose(bp, xcb3[:, XB, :], ident_b),
                     reads=[xcb_b, bib], writes=[bp_b])
                bt_ap, bt_b = btok[u]
                P.op('act', lambda e, bt_ap=bt_ap, bp=bp: e.copy(out=bt_ap, in_=bp), reads=[bp_b], writes=[bt_b])
                mp, mp_b = misc_ps[cnt['m'] % len(misc_ps)]
                cnt['m'] += 1
                P.op('pe', lambda e, mp=mp, xcb3=xcb3: e.matmul(mp, lhsT=xcb3[:, XB, :], rhs=xcb3[:, XB + 1, :], start=True, stop=True),
                     reads=[xcb_b], writes=[mp_b])
                cm_ap, cm_b = cbTm[u]
                P.op('dve', lambda e, cm_ap=cm_ap, mp=mp: e.tensor_tensor(out=cm_ap, in0=mp, in1=triu, op=ALU.mult),
                     reads=[mp_b, triu_b], writes=[cm_b])
                y_ap, y_b = ych[u]
                y3 = y_ap.rearrange("p (b t) -> p b t", t=128)
                if STOP < 4:
                    continue
                for hh in range(HPG):
                    h = g * HPG + hh
                    k3 = cnt['h'] % 3
                    k4 = cnt['h'] % 2
                    cnt['h'] += 1
                    lm_ap, lm_b = Lm[k3]
                    P.op('pool', lambda e, lm_ap=lm_ap, h=h, r=r: e.tensor_scalar(
                        out=lm_ap, in0=smask, scalar1=dta_tok[r][0][:, h:h + 1], scalar2=None, op0=ALU.mult),
                        reads=[smask_b, dta_tok[r][1]], writes=[lm_b])
                    sg, sg_b = seg_ps[k4]
                    P.op('pe', lambda e, sg=sg, lm_ap=lm_ap: e.matmul(sg, lhsT=lm_ap, rhs=triu, start=True, stop=True),
                         reads=[lm_b, triu_b], writes=[sg_b])
                    dc_ap, dc_b = dec[k3]
                    P.op('act', lambda e, dc_ap=dc_ap, sg=sg: e.activation(out=dc_ap, in_=sg, func=AF.Exp),
                         reads=[sg_b], writes=[dc_b])
                    gt_ap, gt_b = GT[k3]
                    P.op('pool', lambda e, gt_ap=gt_ap, dc_ap=dc_ap, cm_ap=cm_ap: e.tensor_tensor(
                        out=gt_ap, in0=dc_ap, in1=cm_ap, op=ALU.mult), reads=[dc_b, cm_b], writes=[gt_b])
                    eh, eh_b = eh_ps[k4]
                    P.op('pe', lambda e, eh=eh, h=h, r=r: e.transpose(
                        eh, ea_tok[r][0][:, h:h + 1].to_broadcast([128, 128]), ident_f),
                        reads=[ea_tok[r][1], bif], writes=[eh_b])
                    ct_ap, ct_b = CTs[k3]
                    P.op('dve', lambda e, ct_ap=ct_ap, eh=eh, xc3=xc3: e.tensor_tensor(
                        out=ct_ap, in0=eh, in1=xc3[:, XB + 1, :], op=ALU.mult), reads=[eh_b, xc_b], writes=[ct_b])
                    if hh % 2 == 0:
                        yp, yp_b = y_ps[cnt['y'] % len(y_ps)]
                        cnt['y'] += 1
                    ro = (hh % 2) * 64
                    P.op('pe', lambda e, yp=yp, xdt_ap=xdt_ap, gt_ap=gt_ap, hh=hh, ro=ro: e.matmul(
                        yp[ro:ro + 64, :], lhsT=xdt_ap[:, hh * 64:(hh + 1) * 64], rhs=gt_ap, start=True, stop=False),
                        reads=[xdt_b, gt_b], writes=[yp_b], partial=(hh % 2 == 1))
                    P.op('pe', lambda e, yp=yp, h=h, ct_ap=ct_ap, ro=ro: e.matmul(
                        yp[ro:ro + 64, :], lhsT=stb[:, h * 64:(h + 1) * 64], rhs=ct_ap, start=False, stop=True),
                        reads=[stb_bs[h], ct_b], writes=[yp_b], partial=True)
                    sp_, sp_b = s_ps[cnt['s'] % len(s_ps)]
                    cnt['s'] += 1
                    P.op('pe', lambda e, sp_=sp_, bt_ap=bt_ap, xdtw_ap=xdtw_ap, hh=hh: e.matmul(
                        sp_, lhsT=bt_ap, rhs=xdtw_ap[:, hh * 64:(hh + 1) * 64], start=True, stop=True),
                        reads=[bt_b, xdtw_b], writes=[sp_b])
                    P.op('dve', lambda e, sp_=sp_, h=h, r=r: e.scalar_tensor_tensor(
                        out=st[:, h * 64:(h + 1) * 64], in0=st[:, h * 64:(h + 1) * 64], scalar=eend[r][0][:, h:h + 1], in1=sp_,
                        op0=ALU.mult, op1=ALU.add), reads=[st_bs[h], eend[r][1], sp_b], writes=[st_bs[h]])
                    P.op('act', lambda e, h=h: e.copy(out=stb[:, h * 64:(h + 1) * 64], in_=st[:, h * 64:(h + 1) * 64]),
                         reads=[st_bs[h]], writes=[stb_bs[h]])
                    if hh % 2 == 1:
                        bi = hh // 2
                        P.op('dve', lambda e, y3=y3, xc3=xc3, yp=yp, bi=bi, g=g: e.scalar_tensor_tensor(
                            out=y3[:, bi, :], in0=xc3[:, bi, :], scalar=dsk[:, g * XB + bi:g * XB + bi + 1], in1=yp,
                            op0=ALU.mult, op1=ALU.add), reads=[xc_b, dsk_b, yp_b], writes=[y_b], partial=(bi > 0))
                if STOP < 5:
                    continue
                P.op('dve', lambda e, y_ap=y_ap, zin_ap=zin_ap: e.tensor_tensor(out=y_ap, in0=y_ap, in1=zin_ap, op=ALU.mult),
                     reads=[y_b, zin_b], writes=[y_b])
                q_ap, q_b = ysq[u]
                q3 = q_ap.rearrange("p (b t) -> p b t", t=128)
                P.op('pool', lambda e, q_ap=q_ap, y_ap=y_ap: e.tensor_tensor(out=q_ap, in0=y_ap, in1=y_ap, op=ALU.mult),
                     reads=[y_b], writes=[q_b])
                sp2, sp2_b = ss_ps[u]
                for bi in range(XB):
                    P.op('pe', lambda e, sp2=sp2, q3=q3, bi=bi: e.matmul(sp2, lhsT=ones_b, rhs=q3[:, bi, :],
                                                                         start=(bi == 0), stop=(bi == XB - 1)),
                         reads=[ones_bb, q_b], writes=[sp2_b], partial=(bi > 0))
                rs_ap, rs_b = rs[u]
                P.op('act', lambda e, rs_ap=rs_ap, sp2=sp2: e.activation(out=rs_ap, in_=sp2, func=AF.Ln, bias=self.eps_ap,
                                                                         scale=1.0 / (XB * 128)),
                     reads=[sp2_b, self.b_eps], writes=[rs_b])
                P.op('act', lambda e, rs_ap=rs_ap: e.activation(out=rs_ap, in_=rs_ap, func=AF.Exp, scale=-0.5),
                     reads=[rs_b], writes=[rs_b])
                P.op('dve', lambda e, y3=y3, rs_ap=rs_ap: e.tensor_tensor(
                    out=y3, in0=y3, in1=rs_ap.unsqueeze(1).to_broadcast([128, XB, 128]), op=ALU.mult),
                    reads=[y_b, rs_b], writes=[y_b])
                yb_ap, yb_b = yb[u]
                yb3 = yb_ap.rearrange("p (b t) -> p b t", t=128)
                P.op('pool', lambda e, yb3=yb3, y3=y3, g=g: e.tensor_tensor(
                    out=yb3, in0=y3, in1=nw[:, g * XB:(g + 1) * XB].unsqueeze(2).to_broadcast([128, XB, 128]), op=ALU.mult),
                    reads=[y_b, nw_b], writes=[yb_b])
                P.op('sp', lambda e, yb3=yb3, g=g, t0=t0: e.dma_start(
                    out=self.Y1[g * XB * 128:(g + 1) * XB * 128, t0:t0 + 128].rearrange("(b p) t -> p b t", p=128), in_=yb3),
                    reads=[yb_b], writes=[self.b_y1], dma=yb_b, partial=True)
        self.pop()

    def setup_eps(self):
        self.eps_ap, self.b_eps = self.alloc("eps", 1, F32)
        self.P.op('dve', lambda e: e.memset(self.eps_ap, 1e-5), writes=[self.b_eps])


def _t5_bucket(dist):
    max_exact = 16
    d_f = np.maximum(dist, 1).astype(np.float32)
    large = max_exact + (np.log(d_f / np.float32(max_exact)) / np.float32(math.log(2048 / max_exact))
                         * np.float32(32 - max_exact)).astype(np.int32)
    large = np.minimum(large, 31)
    return np.where(dist < max_exact, dist, large)


def make_onehot():
    qi = np.arange(128)[:, None]
    ki = np.arange(256)[None, :]
    delta = 128 + qi - ki
    band = (delta >= 0) & (delta <= 128)
    oh = np.zeros((3, 33, 128 * 256), np.float32)
    for g, dil in enumerate((1, 4, 16)):
        bucket = _t5_bucket(np.clip(delta, 0, None) * dil)
        for b in range(32):
            oh[g, b] = ((bucket == b) & band).reshape(-1)
        oh[g, 32] = (~band).reshape(-1)
    return oh


def build_program(cfg):
    mk = MK(cfg)
    D, S = cfg.D, cfg.S
    P = mk.P
    ident_d = mk.din("ident", [128, 128])
    xT_d = mk.din("xT", [D, S])
    x_d = mk.din("x", [S, D])
    lng_d = mk.din("ln_g", [2, D])
    lnb_d = mk.din("ln_b", [2, D])
    out_d = mk.dout("out", [S, D])
    b_out = Buf("out")
    mk.setup_consts(ident_d)
    mk.setup_eps()
    b_x = Buf("x_in")
    xTb = mk.dscr("xTb", [D, S], BF16)
    b_xTb = Buf("xTb")
    mk.cast_dram(xTb, xT_d, D, b_xTb)
    has0, has1 = (0 in cfg.layers), (1 in cfg.layers)
    if has0:
        NSUP0 = (cfg.NBLK0 + cfg.CB - 1) // cfg.CB
        W0_d = mk.din("W0", [NSUP0, 128, cfg.KC, cfg.CB * 128])
        relb_d = mk.din("relb", [32, 3 * cfg.HG])
        onehot_d = mk.din("onehot", [3, 33, 128 * 256])
        KCE0 = cfg.DATT // 128
        Wo0_d = mk.din("Wo0", [D // 512 * 128, KCE0 * 512])
        Wo0b = mk.dscr("Wo0b", [D // 512 * 128, KCE0 * 512], BF16)
        b_wo0 = Buf("Wo0b")
        mk.cast_dram(Wo0b, Wo0_d, D // 512 * 128, b_wo0, nsplit=4)
        mk.l0_bias_tables(relb_d, onehot_d)
        mk.l0_inproj(W0_d, xTb, b_xTb)
        mk.l0_attention()
        if has1:
            X1 = mk.dscr("X1", [S, D], F32)
            b_x1 = Buf("X1")
            X1T = mk.dscr("X1T", [D, S], BF16)
            b_x1T = Buf("X1T")
            mk.outproj_ln(mk.Y0, mk.b_y0, Wo0b, b_wo0, KCE0, x_d, b_x, lng_d[0], lnb_d[0],
                          X1, b_x1, X1T, b_x1T, "o0")
        else:
            mk.outproj_ln(mk.Y0, mk.b_y0, Wo0b, b_wo0, KCE0, x_d, b_x, lng_d[0], lnb_d[0],
                          out_d, b_out, None, None, "o0")
    else:
        X1, b_x1, X1T, b_x1T = x_d, b_x, xTb, b_xTb
    if has1:
        NB1 = (cfg.DI + cfg.CONV + cfg.NH) // 128
        NSUP1 = (NB1 + cfg.CB - 1) // cfg.CB
        NCONV = cfg.CONV // 128
        NXB = cfg.DI // 128
        W1_d = mk.din("W1", [NSUP1, 128, cfg.KC, cfg.CB * 128])
        convw_d = mk.din("convw", [128, NCONV * 4])
        convb_d = mk.din("convb", [128, NCONV])
        dtb_d = mk.din("dtb", [128, 1])
        alog_d = mk.din("alog", [128, 1])
        dsk_d = mk.din("dsk", [128, NXB])
        nw_d = mk.din("nw", [128, NXB])
        triu_d = mk.din("triu", [128, 128])
        smask_d = mk.din("smask", [128, 128])
        KCE1 = cfg.DI // 128
        Wo1_d = mk.din("Wo1", [D // 512 * 128, KCE1 * 512])
        Wo1b = mk.dscr("Wo1b", [D // 512 * 128, KCE1 * 512], BF16)
        b_wo1 = Buf("Wo1b")
        mk.cast_dram(Wo1b, Wo1_d, D // 512 * 128, b_wo1, nsplit=4)
        stg = getattr(cfg, 'stages', ('inproj', 'ssd', 'outproj'))
        if 'inproj' in stg:
            mk.l1_inproj(W1_d, X1T, b_x1T)
        if 'ssd' in stg:
            mk.l1_ssd(convw_d, convb_d, dtb_d, alog_d, dsk_d, nw_d, triu_d, smask_d)
        else:
            mk.Y1 = mk.dscr("Y1", [cfg.DI, S], BF16)
            mk.b_y1 = Buf("Y1")
        if 'outproj' in stg:
            mk.outproj_ln(mk.Y1, mk.b_y1, Wo1b, b_wo1, KCE1, X1, b_x1, lng_d[1], lnb_d[1],
                          out_d, b_out, None, None, "o1")
    mk.P.finalize()
    return mk


def host_wout(w_out, D):
    E = w_out.shape[0]
    KCE = E // 128
    NDB = D // 512
    return np.ascontiguousarray(w_out.reshape(KCE, 128, NDB, 512).transpose(2, 1, 0, 3).reshape(NDB * 128, KCE * 512))


def host_win(w_in, cols, KC, CB):
    NBLK = len(cols) // 128
    NSUP = (NBLK + CB - 1) // CB
    if NSUP * CB > NBLK:
        cols = np.concatenate([cols, np.tile(cols[-128:], NSUP * CB - NBLK)])
    Wp = w_in[:, cols]
    return np.ascontiguousarray(Wp.reshape(KC, 128, NSUP, CB * 128).transpose(2, 1, 0, 3))


def host_layout_l1(cfg, w_in_ssm, conv_w, conv_b, dt_bias, a_log, d_skip, norm_w, w_out_ssm):
    NCONV = cfg.CONV // 128
    NXB = cfg.DI // 128
    W1 = host_win(w_in_ssm, np.arange(w_in_ssm.shape[1]), cfg.KC, cfg.CB)
    convw = np.ascontiguousarray(conv_w.reshape(4, NCONV, 128).transpose(2, 1, 0).reshape(128, NCONV * 4))
    convb = np.ascontiguousarray(conv_b.reshape(NCONV, 128).T)
    dsk = np.ascontiguousarray(np.repeat(d_skip, cfg.P).reshape(NXB, 128).T)
    nw = np.ascontiguousarray(norm_w.reshape(NXB, 128).T)
    t = np.arange(128)
    triu = (t[:, None] <= t[None, :]).astype(np.float32)
    smask = (t[:, None] > t[None, :]).astype(np.float32)
    return {"W1": W1, "convw": convw, "convb": convb, "dtb": np.ascontiguousarray(dt_bias.reshape(128, 1)),
            "alog": np.ascontiguousarray(a_log.reshape(128, 1)), "dsk": dsk, "nw": nw, "triu": triu, "smask": smask,
            "Wo1": host_wout(w_out_ssm, cfg.D)}


def host_layout_l0(cfg, w_in_attn, w_out_attn, rel_bias):
    D, HG, KC, CB = cfg.D, cfg.HG, cfg.KC, cfg.CB
    DATT = cfg.DATT
    cols = []
    for h in range(HG):
        for g in range(3):
            for j in range(3):
                c0 = g * 3 * DATT + j * DATT + h * 128
                cols.append(np.arange(c0, c0 + 128))
        c0 = 9 * DATT + h * 128
        cols.append(np.arange(c0, c0 + 128))
    NBLK = len(cols)
    NSUP = (NBLK + CB - 1) // CB
    while len(cols) < NSUP * CB:
        cols.append(cols[-1])
    cols = np.concatenate(cols)
    Wp = w_in_attn[:, cols]
    W0 = Wp.reshape(KC, 128, NSUP, CB * 128).transpose(2, 1, 0, 3)
    Wo = host_wout(w_out_attn, D)
    hs = np.concatenate([np.arange(g * (rel_bias.shape[1] // 3), g * (rel_bias.shape[1] // 3) + HG) for g in range(3)])
    return np.ascontiguousarray(W0), Wo, np.ascontiguousarray(rel_bias[:, hs])


_CACHE = {}


def kernel(x, w_in_attn, w_out_attn, rel_bias, w_in_ssm, conv_w, conv_b, dt_bias,
           a_log, d_skip, ssm_norm_w, w_out_ssm, ln_g, ln_b):
    x = np.asarray(x, dtype=np.float32)
    B, S, D = x.shape
    cfg = Cfg(D=D, S=S)
    if 'mk' not in _CACHE:
        _CACHE['mk'] = build_program(cfg)
    mk = _CACHE['mk']
    f = lambda a: np.asarray(a, dtype=np.float32)
    W0, Wo0, rb = host_layout_l0(cfg, f(w_in_attn)[0], f(w_out_attn)[0], f(rel_bias))
    shared = {"ident": np.eye(128, dtype=np.float32), "W0": W0, "relb": rb, "onehot": make_onehot(), "Wo0": Wo0,
              "ln_g": np.ascontiguousarray(f(ln_g)), "ln_b": np.ascontiguousarray(f(ln_b))}
    shared.update(host_layout_l1(cfg, f(w_in_ssm)[0], f(conv_w)[0], f(conv_b)[0], f(dt_bias)[0], f(a_log)[0],
                                 f(d_skip)[0], f(ssm_norm_w)[0], f(w_out_ssm)[0]))
    in_maps = []
    for c in range(8):
        b = c % B
        m = dict(shared)
        m["x"] = np.ascontiguousarray(x[b])
        m["xT"] = np.ascontiguousarray(x[b].T)
        in_maps.append(m)
    res = run_bass_kernel_spmd(mk.nc, in_maps, core_ids=list(range(8)))
    out = np.stack([np.asarray(res.results[b]["out"], dtype=np.float32) for b in range(B)], axis=0)
    return out
```

```python
from contextlib import ExitStack
import math
import numpy as np
import concourse.bass as bass
import concourse.mybir as mybir
from concourse.bass_utils import run_bass_kernel_spmd

F32 = mybir.dt.float32
BF16 = mybir.dt.bfloat16
AF = mybir.ActivationFunctionType
ALU = mybir.AluOpType
AX = mybir.AxisListType

ENGS = ('pe', 'dve', 'act', 'pool', 'sp')
SIG_CH = 12000
DMA_CH = 700


class Buf:
    __slots__ = ('name', 'writers', 'readers', 'dcount', 'dsems', 'excl')

    def __init__(self, name, excl=False):
        self.name = name
        self.excl = excl
        self.writers = []
        self.readers = []
        self.dcount = 0
        self.dsems = None


class Op:
    __slots__ = ('eng', 'emit', 'deps', 'is_dma', 'dbuf', 'didx', 'need_sig', 'sig', 'idx')


class Prog:
    def __init__(self, nc, stack):
        self.nc = nc
        self.stack = stack
        self.ops = []
        self.by_eng = {e: [] for e in ENGS}
        self.bar = {}

    def op(self, eng, emit, reads=(), writes=(), dma=None, partial=False):
        o = Op()
        o.eng = eng
        o.emit = emit
        o.is_dma = dma is not None
        o.dbuf = dma
        o.need_sig = False
        o.sig = None
        o.idx = len(self.ops)
        deps = {}
        if any(b.excl for b in reads):
            writes = list(writes) + [b for b in reads if b.excl and b not in writes]
            reads = [b for b in reads if not b.excl]
        for b in reads:
            for w in b.writers:
                deps[w] = 'w'
        for b in writes:
            for w in b.writers:
                deps[w] = 'w'
            for r in b.readers:
                if r not in deps:
                    deps[r] = 'r'
        if eng in self.bar:
            for d in self.bar.pop(eng):
                deps[d] = 'w'
        deps.pop(o, None)
        o.deps = deps
        for b in reads:
            b.readers.append(o)
        for b in writes:
            if partial and not b.excl:
                b.writers.append(o)
            else:
                b.writers = [o]
            b.readers = []
        if o.is_dma:
            dma.dcount += 1
            o.didx = dma.dcount
        self.ops.append(o)
        self.by_eng[eng].append(o)
        return o

    def barrier(self):
        deps = []
        for e in ENGS:
            for o in reversed(self.by_eng[e]):
                if not o.is_dma:
                    deps.append(o)
                    break
        lastd = {}
        for o in self.ops:
            if o.is_dma:
                lastd[id(o.dbuf)] = o
        deps.extend(lastd.values())
        for e in ENGS:
            self.bar[e] = list(deps)

    def finalize(self):
        nc = self.nc
        for o in self.ops:
            for d, kind in o.deps.items():
                if d.is_dma:
                    continue
                if d.eng == o.eng and not o.is_dma:
                    if o.eng == 'pe' or kind == 'r':
                        continue
                d.need_sig = True
        cnt = {e: 0 for e in ENGS}
        for o in self.ops:
            if not o.is_dma and o.need_sig:
                cnt[o.eng] += 1
                o.sig = (o.eng, (cnt[o.eng] - 1) // SIG_CH, (cnt[o.eng] - 1) % SIG_CH + 1)
        esems = {}
        nsem = 0
        for e in ENGS:
            n = (cnt[e] + SIG_CH - 1) // SIG_CH
            esems[e] = [self.stack.enter_context(nc.semaphore(f"s_{e}_{i}")) for i in range(n)]
            nsem += n
        seen_b = set()
        for o in self.ops:
            if o.is_dma and id(o.dbuf) not in seen_b:
                seen_b.add(id(o.dbuf))
                b = o.dbuf
                n = (b.dcount + DMA_CH - 1) // DMA_CH
                b.dsems = [self.stack.enter_context(nc.semaphore(f"d_{b.name}_{i}")) for i in range(n)]
                nsem += n
        self.n_sems = nsem

        def sig_of(d):
            if d.is_dma:
                k = d.didx - 1
                return d.dbuf.dsems[k // DMA_CH], 16 * (k % DMA_CH + 1)
            e, si, v = d.sig
            return esems[e][si], v

        engh = {'pe': 'tensor', 'dve': 'vector', 'act': 'scalar', 'pool': 'gpsimd', 'sp': 'sync'}
        block = self.stack.enter_context(nc.Block())

        def make(ename):
            ops = self.by_eng[ename]

            def body(eng):
                seen = {}
                for o in ops:
                    need = {}
                    for d, kind in o.deps.items():
                        if not d.is_dma and d.eng == o.eng and not o.is_dma:
                            if o.eng == 'pe' or kind == 'r':
                                continue
                        s, v = sig_of(d)
                        k = id(s)
                        if seen.get(k, 0) >= v:
                            continue
                        if k not in need or need[k][1] < v:
                            need[k] = (s, v)
                    for k, (s, v) in need.items():
                        eng.wait_ge(s, v)
                        seen[k] = v
                    ins = o.emit(eng)
                    if o.is_dma:
                        s, v = sig_of(o)
                        ins.then_inc(s, 16)
                    elif o.sig is not None:
                        s, v = sig_of(o)
                        ins.then_inc(s, 1)
                last = {}
                for o in ops:
                    if o.is_dma:
                        s, v = sig_of(o)
                        last[id(s)] = (s, max(v, last.get(id(s), (s, 0))[1]))
                for k, (s, v) in last.items():
                    if seen.get(k, 0) < v:
                        eng.wait_ge(s, v)
            return body

        for e in ENGS:
            if self.by_eng[e]:
                getattr(block, engh[e])(make(e))


class Cfg:
    def __init__(self, D=4096, S=4096, HG=16, G8=8, HPG=16, debug=False, layers=(0, 1)):
        self.D = D
        self.S = S
        self.KC = D // 128
        self.HG = HG
        self.DATT = HG * 128
        self.NBLK0 = HG * 10
        self.patterns = ((128, 1), (512, 4), (2048, 16))
        self.G8 = G8
        self.HPG = HPG
        self.P = 64
        self.N = 128
        self.NH = G8 * HPG
        self.DI = self.NH * self.P
        self.CONV = self.DI + 2 * G8 * self.N
        self.debug = debug
        self.layers = layers
        self.alpha = (2 * 2) ** 0.25
        self.CB = 4 if D < 4096 else 8


ARENA_WORDS = 51 * 1024


class MK:
    def __init__(self, cfg):
        self.cfg = cfg
        self.nc = bass.Bass("TRN2", target_bir_lowering=False)
        self.stack = ExitStack()
        self.P = Prog(self.nc, self.stack)
        self.arena = self.stack.enter_context(self.nc.sbuf_tensor("arena", [128, ARENA_WORDS], F32))
        self.top = 0
        self.marks = []
        self.banks = []
        for i in range(8):
            t = self.stack.enter_context(self.nc.psum_tensor(f"bank{i}", [128, 512], F32))
            self.banks.append((t, Buf(f"bank{i}", excl=True)))
        self.dram = {}
        self.scr_kind = "ExternalOutput" if cfg.debug else "Internal"

    def alloc(self, name, cols, dt=F32):
        words = cols if dt == F32 else (cols + 1) // 2
        words = (words + 7) // 8 * 8
        a = self.top
        self.top += words
        assert self.top <= ARENA_WORDS, f"SBUF arena overflow at {name}: {self.top}"
        ap = self.arena[:, a:a + words]
        if dt != F32:
            ap = ap.bitcast(dt)[:, :cols]
        else:
            ap = ap[:, :cols]
        return ap, Buf(name)

    def push(self):
        self.marks.append(self.top)

    def pop(self):
        self.P.barrier()
        self.top = self.marks.pop()

    def din(self, name, shape, dt=F32):
        t = self.nc.dram_tensor(name, list(shape), dt, kind="ExternalInput")
        self.dram[name] = t
        return t.ap()

    def dout(self, name, shape, dt=F32):
        t = self.nc.dram_tensor(name, list(shape), dt, kind="ExternalOutput")
        self.dram[name] = t
        return t.ap()

    def dscr(self, name, shape, dt=F32):
        t = self.nc.dram_tensor(name, list(shape), dt, kind=self.scr_kind)
        self.dram[name] = t
        return t.ap()

    def bank(self, i, dt=F32):
        t, b = self.banks[i]
        ap = t[:]
        if dt != F32:
            ap = ap.bitcast(dt)
        return ap, b

    def setup_consts(self, ident_d):
        P = self.P
        self.ident_f, b1 = self.alloc("ident_f", 128, F32)
        self.ident_b, b2 = self.alloc("ident_b", 128, BF16)
        self.ones_f, b3 = self.alloc("ones_f", 128, F32)
        self.b_ident_f, self.b_ident_b, self.b_ones = b1, b2, b3
        P.op('sp', lambda e: e.dma_start(out=self.ident_f, in_=ident_d), writes=[b1], dma=b1)
        P.op('pool', lambda e: e.dma_start(out=self.ident_b, in_=ident_d), writes=[b2], dma=b2)
        P.op('dve', lambda e: e.memset(self.ones_f, 1.0), writes=[b3])

    def cast_dram(self, dst, src, rows, bufd, nsplit=8):
        P = self.P
        cols = src.shape[1]
        c = min(cols, 4096)
        a = cols // c
        if a > 1:
            src = src.rearrange("r (a c) -> (r a) c", c=c)
            dst = dst.rearrange("r (a c) -> (r a) c", c=c)
        R = rows * a
        step = min(R, 256)
        first = True
        for i in range(0, R, step):
            P.op('pool', lambda e, i=i: e.dma_start(out=dst[i:i + step, :], in_=src[i:i + step, :]),
                 writes=[bufd], dma=bufd, partial=not first)
            first = False

    def gemm_fm(self, W_d, xT_d, b_xT, KC, T, NBLK, CB, epilogue, tag):
        P = self.P
        self.push()
        NSUP = (NBLK + CB - 1) // CB
        TB = T // 512
        wsb = [self.alloc(f"{tag}_w{i}", KC * CB * 128, BF16) for i in range(2)]
        xsb = [self.alloc(f"{tag}_x{i}", KC * 512, BF16) for i in range(2)]
        xcnt = 0
        bcnt = 0
        for s in range(NSUP):
            w_ap, w_b = wsb[s % 2]
            w3 = w_ap.rearrange("p (k c) -> p k c", k=KC)
            nq = 4 if KC >= 4 else 1
            kq = KC // nq
            for q in range(nq):
                P.op('pool', lambda e, s=s, q=q, w3=w3: e.dma_start(
                    out=w3[:, q * kq:(q + 1) * kq, :], in_=W_d[s, :, q * kq:(q + 1) * kq, :]),
                    writes=[w_b], dma=w_b, partial=(q > 0))
            ncb = min(CB, NBLK - s * CB)
            for tb in range(TB):
                x_ap, x_b = xsb[xcnt % 2]
                xcnt += 1
                x3 = x_ap.rearrange("p (k t) -> p k t", k=KC)
                P.op('sp', lambda e, x3=x3, tb=tb: e.dma_start(
                    out=x3, in_=xT_d[:, tb * 512:(tb + 1) * 512].rearrange("(k p) t -> p k t", p=128)),
                    reads=[b_xT], writes=[x_b], dma=x_b)
                for cb in range(ncb):
                    ps, ps_b = self.bank(bcnt % 4)
                    bcnt += 1
                    for k in range(KC):
                        P.op('pe', lambda e, ps=ps, w3=w3, x3=x3, k=k, cb=cb: e.matmul(
                            ps, lhsT=w3[:, k, cb * 128:(cb + 1) * 128], rhs=x3[:, k, :],
                            start=(k == 0), stop=(k == KC - 1)),
                            reads=[w_b, x_b], writes=[ps_b], partial=(k > 0))
                    epilogue(s * CB + cb, tb, ps, ps_b)
        self.pop()

    def l0_inproj(self, W_d, xT_d, b_xT):
        cfg, P = self.cfg, self.P
        self.QKVG = self.dscr("QKVG", [cfg.NBLK0, 128, cfg.S], BF16)
        self.b_qkvg = [Buf(f"qkvg{i}") for i in range(cfg.NBLK0)]
        self.push()
        stg = [self.alloc(f"l0stg{i}", 512, BF16) for i in range(4)]
        st = {'c': 0}

        def epi(blk, tb, ps, ps_b):
            s_ap, s_b = stg[st['c'] % 4]
            eng = 'act' if st['c'] % 2 == 0 else 'dve'
            st['c'] += 1
            jj = blk % 10
            d = 1 if jj == 9 else cfg.patterns[jj // 3][1]
            n = 512 // d
            o_v = s_ap.rearrange("p (r j) -> p j r", r=d) if d > 1 else s_ap
            i_v = ps.rearrange("p (j r) -> p j r", r=d) if d > 1 else ps
            if eng == 'act':
                P.op('act', lambda e: e.copy(out=o_v, in_=i_v), reads=[ps_b], writes=[s_b])
            else:
                P.op('dve', lambda e: e.tensor_copy(out=o_v, in_=i_v), reads=[ps_b], writes=[s_b])
            if d > 1:
                dst = self.QKVG[blk].rearrange("p (r j) -> p r j", r=d)[:, :, tb * n:(tb + 1) * n]
                src = s_ap.rearrange("p (r j) -> p r j", r=d)
            else:
                dst = self.QKVG[blk, :, tb * 512:(tb + 1) * 512]
                src = s_ap
            P.op('sp', lambda e: e.dma_start(out=dst, in_=src),
                 reads=[s_b], writes=[self.b_qkvg[blk]], dma=s_b, partial=True)

        self.gemm_fm(W_d, xT_d, b_xT, cfg.KC, cfg.S, cfg.NBLK0, cfg.CB, epi, "g0")
        self.pop()

    def l0_bias_tables(self, relb_d, onehot_d):
        cfg, P = self.cfg, self.P
        HG = cfg.HG
        self.BT = self.dscr("BT", [3, HG, 128 * 256], F32)
        self.b_bt = Buf("BT")
        self.push()
        rb, rb_b = self.alloc("rb", 3 * HG, F32)
        P.op('dve', lambda e: e.memset(rb[0:64, :], -1e30), writes=[rb_b])
        P.op('sp', lambda e: e.dma_start(out=rb[0:32, :], in_=relb_d), writes=[rb_b], dma=rb_b)
        oh = [self.alloc(f"oh{i}", 2048, F32) for i in range(2)]
        stg = [self.alloc(f"btst{i}", 2048, F32) for i in range(2)]
        c = 0
        for g in range(3):
            for ch in range(16):
                o_ap, o_b = oh[c % 2]
                s_ap, s_b = stg[c % 2]
                c += 1
                P.op('sp', lambda e, o_ap=o_ap, g=g, ch=ch: e.dma_start(
                    out=o_ap[0:33, :], in_=onehot_d[g, :, ch * 2048:(ch + 1) * 2048]),
                    writes=[o_b], dma=o_b)
                for j in range(4):
                    ps, ps_b = self.bank(j)
                    P.op('pe', lambda e, ps=ps, o_ap=o_ap, g=g, j=j: e.matmul(
                        ps[0:HG, :], lhsT=rb[0:33, g * HG:(g + 1) * HG], rhs=o_ap[0:33, j * 512:(j + 1) * 512],
                        start=True, stop=True), reads=[rb_b, o_b], writes=[ps_b])
                    P.op('act' if j % 2 == 0 else 'dve',
                         (lambda e, ps=ps, s_ap=s_ap, j=j: e.copy(out=s_ap[0:HG, j * 512:(j + 1) * 512], in_=ps[0:HG, :]))
                         if j % 2 == 0 else
                         (lambda e, ps=ps, s_ap=s_ap, j=j: e.tensor_copy(out=s_ap[0:HG, j * 512:(j + 1) * 512], in_=ps[0:HG, :])),
                         reads=[ps_b], writes=[s_b], partial=(j > 0))
                P.op('sp', lambda e, s_ap=s_ap, g=g, ch=ch: e.dma_start(
                    out=self.BT[g, :, ch * 2048:(ch + 1) * 2048], in_=s_ap[0:HG, :]),
                    reads=[s_b], writes=[self.b_bt], dma=s_b, partial=True)
        self.pop()

    def l0_attention(self):
        cfg, P = self.cfg, self.P
        S, HG = cfg.S, cfg.HG
        H2 = S // 2
        NBK = S // 128
        scale = 128 ** -0.5
        self.Y0 = self.dscr("Y0", [cfg.DATT, S], BF16)
        self.b_y0 = Buf("Y0")
        self.push()
        QT = [self.alloc(f"QT{g}", S, BF16) for g in range(3)]
        KT = [self.alloc(f"KT{g}", S + 128, BF16) for g in range(3)]
        VTd = self.alloc("VTd", S, BF16)
        ld = []
        Vtok = [self.alloc(f"Vtok{g}", S + 128, BF16) for g in range(3)]
        BM = [self.alloc(f"BM{g}", 256, F32) for g in range(3)]
        BM0 = [self.alloc(f"BM0{g}", 256, F32) for g in range(3)]
        OT = [self.alloc(f"OT{g}", H2, F32) for g in range(3)]
        LSE = [self.alloc(f"LSE{g}", H2, F32) for g in range(3)]
        gT = self.alloc("gT", S, BF16)
        yT = self.alloc("yT", S, BF16)
        NR = 3
        Sb = [self.alloc(f"Sb{i}", 1024, F32) for i in range(NR)]
        Pf = [self.alloc(f"Pf{i}", 1024, F32) for i in range(NR)]
        Pn = [self.alloc(f"Pn{i}", 1024, BF16) for i in range(NR)]
        PTs = [self.alloc(f"PTs{i}", 1024, BF16) for i in range(NR)]
        Bl = [self.alloc(f"Bl{i}", 512, F32) for i in range(NR)]
        negm = [self.alloc(f"negm{i}", 4, F32) for i in range(NR)]
        den = [self.alloc(f"den{i}", 4, F32) for i in range(NR)]
        rden = [self.alloc(f"rden{i}", 4, F32) for i in range(NR)]
        lnd = [self.alloc(f"lnd{i}", 4, F32) for i in range(NR)]
        lse = [self.alloc(f"lse{i}", 4, F32) for i in range(NR)]
        ones4, ones4_b = self.alloc("ones4", 512, F32)
        CW = 1024
        tmp = [(Sb[i][0][:, :CW], Sb[i][1]) for i in range(3)] + [(Pf[i][0][:, :CW], Pf[i][1]) for i in range(3)]
        SpsA = [self.bank(0), self.bank(2)]
        SpsB = [self.bank(1), self.bank(3)]
        PTps = self.bank(4, BF16)
        oTps = self.bank(5)
        LBps = self.bank(6)
        VTps = self.bank(7, BF16)
        ident_f, ident_b, ones_f = self.ident_f, self.ident_b, self.ones_f
        bif, bib, bon = self.b_ident_f, self.b_ident_b, self.b_ones
        ldc = {'c': 0}
        P.op('dve', lambda e: e.memset(ones4, 1.0), writes=[ones4_b])
        for g in range(3):
            P.op('pool', lambda e, g=g: e.memset(KT[g][0][:, 0:128], 0.0), writes=[KT[g][1]])
            P.op('pool', lambda e, g=g: e.memset(Vtok[g][0][:, 0:128], 0.0), writes=[Vtok[g][1]])

        def load_fm(blk, g, dst, dst_b):
            src = self.QKVG[blk]
            if True:
                P.op('sp', lambda e: e.dma_start(out=dst, in_=src), reads=[self.b_qkvg[blk]], writes=[dst_b], dma=dst_b,
                     partial=True)
                return
            d = cfg.patterns[g][1]
            l_ap, l_b = ld[ldc['c'] % len(ld)]
            ldc['c'] += 1
            P.op('sp', lambda e: e.dma_start(out=l_ap, in_=src), reads=[self.b_qkvg[blk]], writes=[l_b], dma=l_b)
            L = S // d
            P.op('pool', lambda e: e.tensor_copy(out=dst.rearrange("p (r j) -> p j r", j=L),
                                                 in_=l_ap.rearrange("p (j r) -> p j r", r=d)),
                 reads=[l_b], writes=[dst_b], partial=True)

        bcnt = {'c': 0}

        def batch(g, blocks, half, first_of_group):
            d = cfg.patterns[g][1]
            L = S // d
            nb = L // 128
            hb = nb // 2
            s = bcnt['c'] % NR
            bcnt['c'] += 1
            q_ap, q_b = QT[g]
            k_ap, k_b = KT[g]
            v_ap, v_b = Vtok[g]
            o_ap, o_b = OT[g]
            l_ap, l_b = LSE[g]
            o3 = o_ap.rearrange("p (j r) -> p j r", r=d)
            l3 = l_ap.rearrange("p (j r) -> p j r", r=d)
            sA, sA_b = SpsA[(bcnt['c'] - 1) % 2]
            sB, sB_b = SpsB[(bcnt['c'] - 1) % 2]
            sb, sb_b = Sb[s]
            pf, pf_b = Pf[s]
            pn, pn_b = Pn[s]
            pts, pts_b = PTs[s]
            bl, bl_b = Bl[s]
            nm, nm_b = negm[s]
            dn, dn_b = den[s]
            rd, rd_b = rden[s]
            ln_, ln_b = lnd[s]
            ls, ls_b = lse[s]
            for j, (r, jb) in enumerate(blocks):
                c0 = r * L + jb * 128
                sp_, sp_b = (sA, sA_b) if j < 2 else (sB, sB_b)
                col = (j % 2) * 256
                P.op('pe', lambda e, sp_=sp_, col=col, c0=c0: e.matmul(
                    sp_[:, col:col + 256], lhsT=q_ap[:, c0:c0 + 128], rhs=k_ap[:, c0:c0 + 256], start=True, stop=True),
                    reads=[q_b, k_b], writes=[sp_b])
            yield 0
            for j, (r, jb) in enumerate(blocks):
                sp_, sp_b = (sA, sA_b) if j < 2 else (sB, sB_b)
                col = (j % 2) * 256
                bm_ap, bm_b = BM[g] if jb > 0 else BM0[g]
                P.op('dve', lambda e, sp_=sp_, col=col, j=j, bm_ap=bm_ap: e.scalar_tensor_tensor(
                    out=sb[:, j * 256:(j + 1) * 256], in0=sp_[:, col:col + 256], scalar=scale, in1=bm_ap,
                    op0=ALU.mult, op1=ALU.add), reads=[sp_b, bm_b], writes=[sb_b], partial=(j > 0))
            P.op('dve', lambda e: e.tensor_reduce(out=nm, in_=sb.rearrange("p (b k) -> p b k", k=256), axis=AX.X, op=ALU.max,
                                                  negate=True), reads=[sb_b], writes=[nm_b])
            yield 0
            for j in range(4):
                P.op('act', lambda e, j=j: e.activation(out=pf[:, j * 256:(j + 1) * 256], in_=sb[:, j * 256:(j + 1) * 256],
                                                        func=AF.Exp, bias=nm[:, j:j + 1], scale=1.0, accum_out=dn[:, j:j + 1]),
                     reads=[sb_b, nm_b], writes=[pf_b, dn_b], partial=(j > 0))
            yield 0
            P.op('dve', lambda e: e.reciprocal(out=rd, in_=dn), reads=[dn_b], writes=[rd_b])
            P.op('act', lambda e: e.activation(out=ln_, in_=dn, func=AF.Ln), reads=[dn_b], writes=[ln_b])
            P.op('pool', lambda e: e.tensor_tensor(out=pn.rearrange("p (b k) -> p b k", k=256),
                                                   in0=pf.rearrange("p (b k) -> p b k", k=256),
                                                   in1=rd.unsqueeze(2).to_broadcast([128, 4, 256]), op=ALU.mult),
                 reads=[pf_b, rd_b], writes=[pn_b])
            P.op('dve', lambda e: e.tensor_tensor(out=ls, in0=ln_, in1=nm, op=ALU.subtract), reads=[ln_b, nm_b], writes=[ls_b])
            yield 0
            ptp, ptp_b = PTps
            for c in range(8):
                P.op('pe', lambda e, c=c: e.transpose(ptp[:, c * 128:(c + 1) * 128], pn[:, c * 128:(c + 1) * 128], ident_b),
                     reads=[pn_b, bib], writes=[ptp_b])
            P.op('act', lambda e: e.copy(out=pts, in_=ptp[:, 0:1024]), reads=[ptp_b], writes=[pts_b])
            yield 0
            otp, otp_b = oTps
            for j, (r, jb) in enumerate(blocks):
                kb = (r * L + jb * 128) // 128
                for c in range(2):
                    P.op('pe', lambda e, j=j, c=c, kb=kb: e.matmul(
                        otp[:, j * 128:(j + 1) * 128], lhsT=v_ap[:, (kb + c) * 128:(kb + c + 1) * 128],
                        rhs=pts[:, (2 * j + c) * 128:(2 * j + c + 1) * 128], start=(c == 0), stop=(c == 1)),
                        reads=[v_b, pts_b], writes=[otp_b])
            r0, jb0 = blocks[0]
            nr = len(set(r for r, _ in blocks))
            nbj = 4 // nr
            jl0 = jb0 - half * hb

            def dest(t3):
                return t3[:, jl0 * 128:(jl0 + nbj) * 128, r0:r0 + nr].rearrange("p (b j) r -> p r b j", j=128)
            P.op('act', lambda e: e.copy(out=dest(o3), in_=otp.rearrange("p (r b j) -> p r b j", r=nr, b=nbj)),
                 reads=[otp_b], writes=[o_b], partial=(not first_of_group))
            yield 0
            P.op('dve', lambda e: e.tensor_tensor(out=bl.rearrange("p (b j) -> p b j", j=128),
                                                  in0=ones4.rearrange("p (b j) -> p b j", j=128),
                                                  in1=ls.unsqueeze(2).to_broadcast([128, 4, 128]), op=ALU.mult),
                 reads=[ls_b, ones4_b], writes=[bl_b])
            lbp, lbp_b = LBps
            for j in range(4):
                P.op('pe', lambda e, j=j: e.transpose(lbp[:, j * 128:(j + 1) * 128], bl[:, j * 128:(j + 1) * 128], ident_f),
                     reads=[bl_b, bif], writes=[lbp_b])
            P.op('dve', lambda e: e.tensor_copy(out=dest(l3), in_=lbp.rearrange("p (r b j) -> p r b j", r=nr, b=nbj)),
                 reads=[lbp_b], writes=[l_b], partial=(not first_of_group))
            yield 0

        def run_batches(gens):
            active = []
            todo = list(gens)
            while todo or active:
                if todo and len(active) < 3:
                    active.append(todo.pop(0))
                for gen in list(active):
                    try:
                        next(gen)
                    except StopIteration:
                        active.remove(gen)

        for h in range(HG):
            base = h * 10
            for g in range(3):
                load_fm(base + 3 * g + 0, g, QT[g][0], QT[g][1])
                load_fm(base + 3 * g + 1, g, KT[g][0][:, 128:S + 128], KT[g][1])
                load_fm(base + 3 * g + 2, g, VTd[0], VTd[1])
                v_ap, v_b = Vtok[g]
                vps, vps_b = VTps
                for q4 in range(S // 512):
                    for j in range(4):
                        blk = q4 * 4 + j
                        P.op('pe', lambda e, blk=blk, j=j: e.transpose(
                            vps[:, j * 128:(j + 1) * 128], VTd[0][:, blk * 128:(blk + 1) * 128], ident_b),
                            reads=[VTd[1], bib], writes=[vps_b])
                    P.op('act', lambda e, q4=q4, v_ap=v_ap: e.copy(out=v_ap[:, 128 + q4 * 512:128 + (q4 + 1) * 512], in_=vps[:, 0:512]),
                         reads=[vps_b], writes=[v_b], partial=True)
                bm_ap, bm_b = BM[g]
                P.op('sp', lambda e, g=g, h=h, bm_ap=bm_ap: e.dma_start(
                    out=bm_ap, in_=self.BT[g, h, :].rearrange("(q k) -> q k", k=256)),
                    reads=[self.b_bt], writes=[bm_b], dma=bm_b)
                b0_ap, b0_b = BM0[g]
                P.op('pool', lambda e, b0_ap=b0_ap: e.memset(b0_ap[:, 0:128], -1e30), writes=[b0_b])
                P.op('pool', lambda e, b0_ap=b0_ap, bm_ap=bm_ap: e.tensor_copy(out=b0_ap[:, 128:256], in_=bm_ap[:, 128:256]),
                     reads=[bm_b], writes=[b0_b], partial=True)
            P.op('sp', lambda e, base=base: e.dma_start(out=gT[0], in_=self.QKVG[base + 9]),
                 reads=[self.b_qkvg[base + 9]], writes=[gT[1]], dma=gT[1])

            for half in range(2):
                gens = []
                for g in range(3):
                    d = cfg.patterns[g][1]
                    nb = (S // d) // 128
                    hb = nb // 2
                    blks = [(r, jb) for r in range(d) for jb in range(half * hb, (half + 1) * hb)]
                    for i in range(0, len(blks), 4):
                        gens.append(batch(g, blks[i:i + 4], half, i == 0))
                run_batches(gens)
                for cc in range(H2 // CW):
                    sl = slice(cc * CW, (cc + 1) * CW)
                    gsl = slice(half * H2 + cc * CW, half * H2 + (cc + 1) * CW)
                    (mx, mx_b), (e0, e0_b), (e1, e1_b), (e2, e2_b), (zz, zz_b), (acc, acc_b) = tmp
                    ee = [(e0, e0_b), (e1, e1_b), (e2, e2_b)]
                    P.op('dve', lambda e, sl=sl: e.tensor_tensor(out=mx, in0=LSE[0][0][:, sl], in1=LSE[1][0][:, sl], op=ALU.max),
                         reads=[LSE[0][1], LSE[1][1]], writes=[mx_b])
                    P.op('dve', lambda e, sl=sl: e.tensor_tensor(out=mx, in0=mx, in1=LSE[2][0][:, sl], op=ALU.max),
                         reads=[LSE[2][1], mx_b], writes=[mx_b])
                    for g in range(3):
                        ea, ea_b = ee[g]
                        P.op('pool', lambda e, ea=ea, g=g, sl=sl: e.tensor_tensor(out=ea, in0=LSE[g][0][:, sl], in1=mx, op=ALU.subtract),
                             reads=[LSE[g][1], mx_b], writes=[ea_b])
                        P.op('act', lambda e, ea=ea: e.activation(out=ea, in_=ea, func=AF.Exp), reads=[ea_b], writes=[ea_b])
                    P.op('pool', lambda e: e.tensor_tensor(out=zz, in0=e0, in1=e1, op=ALU.add), reads=[e0_b, e1_b], writes=[zz_b])
                    P.op('pool', lambda e: e.tensor_tensor(out=zz, in0=zz, in1=e2, op=ALU.add), reads=[zz_b, e2_b], writes=[zz_b])
                    P.op('dve', lambda e: e.reciprocal(out=zz, in_=zz), reads=[zz_b], writes=[zz_b])
                    for g in range(3):
                        ea, ea_b = ee[g]
                        P.op('dve', lambda e, ea=ea, g=g, sl=sl: e.tensor_tensor(out=ea, in0=ea, in1=OT[g][0][:, sl], op=ALU.mult),
                             reads=[ea_b, OT[g][1]], writes=[ea_b])
                    P.op('pool', lambda e: e.tensor_tensor(out=acc, in0=e0, in1=e1, op=ALU.add), reads=[e0_b, e1_b], writes=[acc_b])
                    P.op('pool', lambda e: e.tensor_tensor(out=acc, in0=acc, in1=e2, op=ALU.add), reads=[acc_b, e2_b], writes=[acc_b])
                    P.op('dve', lambda e: e.tensor_tensor(out=acc, in0=acc, in1=zz, op=ALU.mult), reads=[acc_b, zz_b], writes=[acc_b])
                    P.op('act', lambda e, gsl=gsl: e.activation(out=e0, in_=gT[0][:, gsl], func=AF.Exp, scale=-1.0),
                         reads=[gT[1]], writes=[e0_b])
                    P.op('pool', lambda e: e.tensor_scalar(out=e0, in0=e0, scalar1=1.0, scalar2=1.0, op0=ALU.add, op1=ALU.mult),
                         reads=[e0_b], writes=[e0_b])
                    P.op('dve', lambda e: e.reciprocal(out=e0, in_=e0), reads=[e0_b], writes=[e0_b])
                    P.op('dve', lambda e, gsl=gsl: e.tensor_tensor(out=e0, in0=e0, in1=gT[0][:, gsl], op=ALU.mult),
                         reads=[e0_b, gT[1]], writes=[e0_b])
                    P.op('dve', lambda e, gsl=gsl: e.tensor_tensor(out=yT[0][:, gsl], in0=acc, in1=e0, op=ALU.mult),
                         reads=[acc_b, e0_b], writes=[yT[1]], partial=True)
            P.op('sp', lambda e, h=h: e.dma_start(out=self.Y0[h * 128:(h + 1) * 128, :], in_=yT[0]),
                 reads=[yT[1]], writes=[self.b_y0], dma=yT[1], partial=True)
        self.pop()

    def outproj_ln(self, Y_d, b_y, W_d, b_w, KCE, x_d, b_x, g_d, bt_d, out_d, b_out, outT_d, b_outT, tag):
        cfg, P = self.cfg, self.P
        S, D = cfg.S, cfg.D
        NDB = D // 512
        KCD = D // 128
        KG = 16 if KCE >= 16 else KCE
        NKG = KCE // KG
        TS = 256
        NSUB = TS // 128
        self.push()
        Gr, Gr_b = self.alloc(f"{tag}G", D, F32)
        Br, Br_b = self.alloc(f"{tag}B", D, F32)
        P.op('sp', lambda e: e.dma_start(out=Gr, in_=g_d.partition_broadcast(128)), writes=[Gr_b], dma=Gr_b)
        P.op('sp', lambda e: e.dma_start(out=Br, in_=bt_d.partition_broadcast(128)), writes=[Br_b], dma=Br_b)
        ysb = [self.alloc(f"{tag}y{i}", KCE * TS, BF16) for i in range(2)]
        wsb = [self.alloc(f"{tag}w{i}", KG * 512, BF16) for i in range(3)]
        xr = [self.alloc(f"{tag}xr{i}", 512, F32) for i in range(4)]
        v = [self.alloc(f"{tag}v{i}", D, F32) for i in range(NSUB)]
        stats = [self.alloc(f"{tag}st{i}", 6 * NDB, F32) for i in range(NSUB)]
        mv = [self.alloc(f"{tag}mv{i}", 2, F32) for i in range(NSUB)]
        rstd = [self.alloc(f"{tag}rs{i}", 1, F32) for i in range(NSUB)]
        if outT_d is not None:
            xb = [self.alloc(f"{tag}xb{i}", D, BF16) for i in range(NSUB)]
            xts = [self.alloc(f"{tag}xt{i}", KCD * TS, BF16) for i in range(2)]
        wc = 0
        xc = 0
        for ts in range(S // TS):
            y_ap, y_b = ysb[ts % 2]
            y3 = y_ap.rearrange("p (k t) -> p k t", k=KCE)
            P.op('sp', lambda e, y3=y3, ts=ts: e.dma_start(
                out=y3, in_=Y_d[:, ts * TS:(ts + 1) * TS].rearrange("(k p) t -> p k t", p=128)),
                reads=[b_y], writes=[y_b], dma=y_b)
            for db in range(NDB):
                for kg in range(NKG):
                    w_ap, w_b = wsb[wc % 3]
                    wc += 1
                    w3 = w_ap.rearrange("p (k c) -> p k c", k=KG)
                    P.op('sp', lambda e, w3=w3, db=db, kg=kg: e.dma_start(
                        out=w3, in_=W_d[db * 128:(db + 1) * 128, kg * KG * 512:(kg + 1) * KG * 512].rearrange("p (k c) -> p k c", k=KG)), reads=[b_w], writes=[w_b], dma=w_b)
                    for k in range(KG):
                        ka = kg * KG + k
                        for sub in range(NSUB):
                            ps, ps_b = self.bank(sub + 2 * (db % 2))
                            P.op('pe', lambda e, ps=ps, y3=y3, w3=w3, ka=ka, k=k, sub=sub: e.matmul(
                                ps, lhsT=y3[:, ka, sub * 128:(sub + 1) * 128], rhs=w3[:, k, :],
                                start=(ka == 0), stop=(ka == KCE - 1)),
                                reads=[y_b, w_b], writes=[ps_b], partial=(ka > 0))
                for sub in range(NSUB):
                    ps, ps_b = self.bank(sub + 2 * (db % 2))
                    x_ap, x_b = xr[xc % 4]
                    xc += 1
                    r0 = ts * TS + sub * 128
                    P.op('sp', lambda e, x_ap=x_ap, r0=r0, db=db: e.dma_start(
                        out=x_ap, in_=x_d[r0:r0 + 128, db * 512:(db + 1) * 512]), reads=[b_x], writes=[x_b], dma=x_b)
                    v_ap, v_b = v[sub]
                    P.op('dve', lambda e, v_ap=v_ap, x_ap=x_ap, ps=ps, db=db: e.scalar_tensor_tensor(
                        out=v_ap[:, db * 512:(db + 1) * 512], in0=x_ap, scalar=cfg.alpha, in1=ps,
                        op0=ALU.mult, op1=ALU.add), reads=[x_b, ps_b], writes=[v_b], partial=(db > 0))
                    s_ap, s_b = stats[sub]
                    P.op('dve', lambda e, s_ap=s_ap, v_ap=v_ap, db=db: e.bn_stats(
                        out=s_ap[:, db * 6:(db + 1) * 6], in_=v_ap[:, db * 512:(db + 1) * 512]),
                        reads=[v_b], writes=[s_b], partial=(db > 0))
            for sub in range(NSUB):
                v_ap, v_b = v[sub]
                s_ap, s_b = stats[sub]
                m_ap, m_b = mv[sub]
                r_ap, r_b = rstd[sub]
                P.op('dve', lambda e, m_ap=m_ap, s_ap=s_ap: e.bn_aggr(out=m_ap, in_=s_ap), reads=[s_b], writes=[m_b])
                P.op('act', lambda e, r_ap=r_ap, m_ap=m_ap: e.activation(out=r_ap, in_=m_ap[:, 1:2], func=AF.Ln, bias=self.eps_ap, scale=1.0),
                     reads=[m_b, self.b_eps], writes=[r_b])
                P.op('act', lambda e, r_ap=r_ap: e.activation(out=r_ap, in_=r_ap, func=AF.Exp, scale=-0.5),
                     reads=[r_b], writes=[r_b])
                P.op('dve', lambda e, v_ap=v_ap, m_ap=m_ap, r_ap=r_ap: e.tensor_scalar(
                    out=v_ap, in0=v_ap, scalar1=m_ap[:, 0:1], scalar2=r_ap, op0=ALU.subtract, op1=ALU.mult),
                    reads=[v_b, m_b, r_b], writes=[v_b])
                P.op('pool', lambda e, v_ap=v_ap: e.tensor_tensor(out=v_ap, in0=v_ap, in1=Gr, op=ALU.mult),
                     reads=[v_b, Gr_b], writes=[v_b])
                P.op('pool', lambda e, v_ap=v_ap: e.tensor_tensor(out=v_ap, in0=v_ap, in1=Br, op=ALU.add),
                     reads=[v_b, Br_b], writes=[v_b])
                r0 = ts * TS + sub * 128
                P.op('sp', lambda e, v_ap=v_ap, r0=r0: e.dma_start(out=out_d[r0:r0 + 128, :], in_=v_ap),
                     reads=[v_b], writes=[b_out], dma=v_b, partial=True)
                if outT_d is not None:
                    xb_ap, xb_b = xb[sub]
                    P.op('act', lambda e, xb_ap=xb_ap, v_ap=v_ap: e.copy(out=xb_ap, in_=v_ap), reads=[v_b], writes=[xb_b])
                    xt_ap, xt_b = xts[ts % 2]
                    xt3 = xt_ap.rearrange("p (k t) -> p k t", k=KCD)
                    for q4 in range(KCD // 4):
                        tp, tp_b = self.bank(5 + (q4 % 2), BF16)
                        for j in range(4):
                            kk = q4 * 4 + j
                            P.op('pe', lambda e, tp=tp, xb_ap=xb_ap, kk=kk, j=j: e.transpose(
                                tp[:, j * 128:(j + 1) * 128], xb_ap[:, kk * 128:(kk + 1) * 128], self.ident_b),
                                reads=[xb_b, self.b_ident_b], writes=[tp_b], partial=(j > 0))
                        P.op('act', lambda e, tp=tp, xt3=xt3, q4=q4, sub=sub: e.copy(
                            out=xt3[:, q4 * 4:(q4 + 1) * 4, sub * 128:(sub + 1) * 128],
                            in_=tp[:, 0:512].rearrange("p (k t) -> p k t", k=4)),
                            reads=[tp_b], writes=[xt_b], partial=not (sub == 0 and q4 == 0))
            if outT_d is not None:
                xt_ap, xt_b = xts[ts % 2]
                xt3 = xt_ap.rearrange("p (k t) -> p k t", k=KCD)
                P.op('sp', lambda e, xt3=xt3, ts=ts: e.dma_start(
                    out=outT_d[:, ts * TS:(ts + 1) * TS].rearrange("(k p) t -> p k t", p=128), in_=xt3),
                    reads=[xt_b], writes=[b_outT], dma=xt_b, partial=True)
        self.pop()

    def l1_inproj(self, W_d, xT_d, b_xT):
        cfg, P = self.cfg, self.P
        NB = (cfg.DI + cfg.CONV + cfg.NH) // 128
        self.NBLK1 = NB
        self.ZXs = []
        for i in range(0, NB, 64):
            n = min(64, NB - i)
            self.ZXs.append(self.dscr(f"ZX{i // 64}", [n, 128, cfg.S], F32))
        self.b_zx = [Buf(f"zx{i}") for i in range(NB)]
        self.push()
        stg = [self.alloc(f"l1stg{i}", 512, F32) for i in range(4)]
        st = {'c': 0}

        def epi(blk, tb, ps, ps_b):
            s_ap, s_b = stg[st['c'] % 4]
            eng = 'act' if st['c'] % 2 == 0 else 'dve'
            st['c'] += 1
            if eng == 'act':
                P.op('act', lambda e: e.copy(out=s_ap, in_=ps), reads=[ps_b], writes=[s_b])
            else:
                P.op('dve', lambda e: e.tensor_copy(out=s_ap, in_=ps), reads=[ps_b], writes=[s_b])
            P.op('sp', lambda e: e.dma_start(out=self.zx(blk, 1)[0, :, tb * 512:(tb + 1) * 512], in_=s_ap),
                 reads=[s_b], writes=[self.b_zx[blk]], dma=s_b, partial=True)

        self.gemm_fm(W_d, xT_d, b_xT, cfg.KC, cfg.S, NB, cfg.CB, epi, "g1")
        self.pop()

    def zx(self, b0, nb):
        t = self.ZXs[b0 // 64]
        l0 = b0 % 64
        assert l0 + nb <= 64
        return t[l0:l0 + nb]

    def l1_ssd(self, convw_d, convb_d, dtb_d, alog_d, dsk_d, nw_d, triu_d, smask_d):
        cfg, P = self.cfg, self.P
        S, G8, HPG, NH = cfg.S, cfg.G8, cfg.HPG, cfg.NH
        XB = HPG * 64 // 128
        NXB = G8 * XB
        UB = XB + 2
        NCONV = NXB + 2 * G8
        ZB0, XB0 = 0, NXB
        BB0 = 2 * NXB
        CB0 = BB0 + G8
        DTB = CB0 + G8
        self.Y1 = self.dscr("Y1", [cfg.DI, S], BF16)
        self.b_y1 = Buf("Y1")
        self.push()
        triu, triu_b = self.alloc("triu", 128, F32)
        smask, smask_b = self.alloc("smask", 128, F32)
        ones_b, ones_bb = self.alloc("ones_b", 128, BF16)
        cw, cw_b = self.alloc("convw", NCONV * 4, F32)
        cb_, cb_b = self.alloc("convb", NCONV, F32)
        dsk, dsk_b = self.alloc("dsk", NXB, F32)
        nw, nw_b = self.alloc("nw", NXB, F32)
        dtb, dtb_b = self.alloc("dtb", 1, F32)
        acol, acol_b = self.alloc("acol", 1, F32)
        for ap_, b_, src in ((triu, triu_b, triu_d), (smask, smask_b, smask_d), (cw, cw_b, convw_d), (cb_, cb_b, convb_d),
                             (dsk, dsk_b, dsk_d), (nw, nw_b, nw_d), (dtb, dtb_b, dtb_d), (acol, acol_b, alog_d)):
            P.op('sp', lambda e, ap_=ap_, src=src: e.dma_start(out=ap_, in_=src), writes=[b_], dma=b_)
        P.op('dve', lambda e: e.memset(ones_b, 1.0), writes=[ones_bb])
        P.op('act', lambda e: e.activation(out=acol, in_=acol, func=AF.Exp), reads=[acol_b], writes=[acol_b])
        P.op('dve', lambda e: e.tensor_scalar(out=acol, in0=acol, scalar1=-1.0, scalar2=None, op0=ALU.mult),
             reads=[acol_b], writes=[acol_b])
        cw3 = cw.rearrange("p (b k) -> p b k", k=4)
        st, st_b = self.alloc("state", NH * 64, F32)
        stb, stb_b = self.alloc("stateb", NH * 64, BF16)
        st_bs = [Buf(f"st{h}") for h in range(NH)]
        stb_bs = [Buf(f"stb{h}") for h in range(NH)]
        P.op('dve', lambda e: e.memset(st, 0.0), writes=st_bs)
        P.op('pool', lambda e: e.memset(stb, 0.0), writes=stb_bs)
        def ring(name, cols, dt=F32, n=2):
            return [self.alloc(f"{name}{i}", cols, dt) for i in range(n)]
        dtr = ring("dtr", 128); xb_ = ring("xb", 128); ax = ring("ax", 128); dtT = ring("dtT", 128)
        dtaT = ring("dtaT", 128); dt_tok = ring("dt_tok", 128); dta_tok = ring("dta_tok", 128)
        acum = ring("acum", 128); ea_tok = ring("ea_tok", 128); eend = ring("eend", 128)
        toend = ring("toend", 128); dtw_tok = ring("dtw_tok", 128)
        xin = ring("xin", UB * 131); xc = ring("xc", UB * 128); xcb = ring("xcb", UB * 128, BF16)
        zin = ring("zin", XB * 128); ych = ring("ych", XB * 128); ysq = ring("ysq", XB * 128, BF16)
        yb = ring("yb", XB * 128, BF16)
        xdt = ring("xdt", XB * 128, BF16); xdtw = ring("xdtw", XB * 128, BF16)
        btok = ring("btok", 128, BF16); cbTm = ring("cbTm", 128); rs = ring("rs", 128)
        Rr = ring("Rr", 512); dec = ring("dec", 512)
        GT = ring("GT", 512, BF16); CTs = ring("CTs", 512, BF16)
        dsx = ring("dsx", XB * 128)
        xcs_b = [[Buf(f"xcs{i}_{j}") for j in range(UB)] for i in range(2)]
        def sub(bank, off, w, dt=F32):
            ap, b = self.bank(bank, dt)
            return (ap[:, off:off + w], b)
        seg_ps = [sub(0, 0, 512), sub(1, 0, 512)]
        eh_ps = [sub(2, 0, 512), sub(3, 0, 512)]
        y_ps = [sub(4, 0, 512)]
        s_ps = [sub(5, 0, 512)]
        misc_ps = [sub(6, 0, 128)]
        ss_ps = [sub(6, 128, 128), sub(6, 128, 128)]
        xt_ps = [sub(7, 0, 512, BF16)]
        bt_ps = [sub(7, 512, 128, BF16)]
        ident_f, ident_b, ones_f = self.ident_f, self.ident_b, self.ones_f
        bif, bib, bon = self.b_ident_f, self.b_ident_b, self.b_ones
        cnt = {'m': 0, 'h': 0, 'y': 0, 's': 0, 'x': 0}
        NCH = S // 128
        STOP = getattr(cfg, 'ssd_stop', 9)
        dlim = getattr(cfg, 'dt_lim', 10 ** 9)
        dcn = {'c': 0}

        def dop(*a, **k):
            dcn['c'] += 1
            if dcn['c'] <= dlim:
                P.op(*a, **k)
        def dt_pipe(c):
            t0 = c * 128
            r = c % 2
            if STOP >= 1:
                dop('sp', lambda e, r=r, t0=t0: e.dma_start(out=dtr[r][0], in_=self.zx(DTB, 1)[0, :, t0:t0 + 128]),
                     reads=[self.b_zx[DTB]], writes=[dtr[r][1]], dma=dtr[r][1])
                dop('dve', lambda e, r=r: e.tensor_scalar(out=xb_[r][0], in0=dtr[r][0], scalar1=dtb, scalar2=None, op0=ALU.add),
                     reads=[dtr[r][1], dtb_b], writes=[xb_[r][1]])
                dop('dve', lambda e, r=r: e.scalar_tensor_tensor(out=ax[r][0], in0=xb_[r][0], scalar=-1.0, in1=xb_[r][0], op0=ALU.mult, op1=ALU.max),
                     reads=[xb_[r][1]], writes=[ax[r][1]])
                dop('act', lambda e, r=r: e.activation(out=ax[r][0], in_=ax[r][0], func=AF.Exp, scale=-1.0),
                     reads=[ax[r][1]], writes=[ax[r][1]])
                dop('act', lambda e, r=r: e.activation(out=ax[r][0], in_=ax[r][0], func=AF.Ln, bias=ones_f[:, 0:1], scale=1.0),
                     reads=[ax[r][1], bon], writes=[ax[r][1]])
                dop('dve', lambda e, r=r: e.scalar_tensor_tensor(out=dtT[r][0], in0=xb_[r][0], scalar=0.0, in1=ax[r][0],
                                                                 op0=ALU.max, op1=ALU.add),
                     reads=[xb_[r][1], ax[r][1]], writes=[dtT[r][1]])
                dop('dve', lambda e, r=r: e.tensor_scalar(out=dtaT[r][0], in0=dtT[r][0], scalar1=acol, scalar2=None, op0=ALU.mult),
                     reads=[dtT[r][1], acol_b], writes=[dtaT[r][1]])
                for src, dst in ((dtT, dt_tok), (dtaT, dta_tok)):
                    mp, mp_b = misc_ps[cnt['m'] % len(misc_ps)]
                    cnt['m'] += 1
                    dop('pe', lambda e, mp=mp, src=src, r=r: e.transpose(mp, src[r][0], ident_f),
                         reads=[src[r][1], bif], writes=[mp_b])
                    dop('act', lambda e, mp=mp, dst=dst, r=r: e.copy(out=dst[r][0], in_=mp), reads=[mp_b], writes=[dst[r][1]])
                mp, mp_b = misc_ps[cnt['m'] % len(misc_ps)]
                cnt['m'] += 1
                dop('pe', lambda e, mp=mp, r=r: e.matmul(mp, lhsT=triu, rhs=dta_tok[r][0], start=True, stop=True),
                     reads=[triu_b, dta_tok[r][1]], writes=[mp_b])
                dop('dve', lambda e, mp=mp, r=r: e.tensor_copy(out=acum[r][0], in_=mp), reads=[mp_b], writes=[acum[r][1]])
                dop('act', lambda e, mp=mp, r=r: e.activation(out=ea_tok[r][0], in_=mp, func=AF.Exp),
                     reads=[mp_b], writes=[ea_tok[r][1]])
                mp2, mp2_b = misc_ps[cnt['m'] % len(misc_ps)]
                cnt['m'] += 1
                dop('pe', lambda e, mp2=mp2, r=r: e.matmul(mp2, lhsT=ones_f, rhs=dta_tok[r][0], start=True, stop=True),
                     reads=[bon, dta_tok[r][1]], writes=[mp2_b])
                dop('act', lambda e, mp2=mp2, r=r: e.activation(out=eend[r][0], in_=mp2, func=AF.Exp),
                     reads=[mp2_b], writes=[eend[r][1]])
                dop('dve', lambda e, mp2=mp2, r=r: e.tensor_tensor(out=toend[r][0], in0=mp2, in1=acum[r][0], op=ALU.subtract),
                     reads=[mp2_b, acum[r][1]], writes=[toend[r][1]])
                dop('act', lambda e, r=r: e.activation(out=toend[r][0], in_=toend[r][0], func=AF.Exp),
                     reads=[toend[r][1]], writes=[toend[r][1]])
                dop('dve', lambda e, r=r: e.tensor_tensor(out=dtw_tok[r][0], in0=dt_tok[r][0], in1=toend[r][0], op=ALU.mult),
                     reads=[dt_tok[r][1], toend[r][1]], writes=[dtw_tok[r][1]])
        def unit(c, g):
            t0 = c * 128
            r = c % 2
            u = (c * G8 + g) % 2
            xin_ap, xin_b = xin[u]
            xin3 = xin_ap.rearrange("p (b t) -> p b t", t=131)
            xc_ap, xc_b = xc[u]
            xc3 = xc_ap.rearrange("p (b t) -> p b t", t=128)
            xcb_ap, xcb_b = xcb[u]
            xcb3 = xcb_ap.rearrange("p (b t) -> p b t", t=128)
            zin_ap, zin_b = zin[u]
            zin3 = zin_ap.rearrange("p (b t) -> p b t", t=128)
            srcs = [(XB0 + g * XB, XB, 0), (BB0 + g, 1, XB), (CB0 + g, 1, XB + 1)]
            first = True
            if c == 0:
                P.op('pool', lambda e, xin3=xin3: e.memset(xin3[:, :, 0:3], 0.0), writes=[xin_b])
                first = False
            for (b0, nb_, o0) in srcs:
                lo = 0 if c > 0 else 3
                P.op('sp', lambda e, xin3=xin3, b0=b0, nb_=nb_, o0=o0, lo=lo, t0=t0: e.dma_start(
                    out=xin3[:, o0:o0 + nb_, lo:131],
                    in_=self.zx(b0, nb_)[:, :, t0 - 3 + lo:t0 + 128].rearrange("b p t -> p b t")),
                    reads=[self.b_zx[b0 + i] for i in range(nb_)], writes=[xin_b], dma=xin_b, partial=(not first))
                first = False
            P.op('sp', lambda e, zin3=zin3, g=g, t0=t0: e.dma_start(
                out=zin3, in_=self.zx(ZB0 + g * XB, XB)[:, :, t0:t0 + 128].rearrange("b p t -> p b t")),
                reads=[self.b_zx[ZB0 + g * XB + i] for i in range(XB)], writes=[zin_b], dma=zin_b)
            def cblk_of(bi):
                return (g * XB + bi) if bi < XB else (NXB + g if bi == XB else NXB + G8 + g)
            for bi in range(UB):
                cblk = cblk_of(bi)
                P.op('act', lambda e, xc3=xc3, xin3=xin3, bi=bi, cblk=cblk: e.activation(
                    out=xc3[:, bi, :], in_=xin3[:, bi, 0:128], func=AF.Identity, scale=cw3[:, cblk, 0:1],
                    bias=cb_[:, cblk:cblk + 1]), reads=[xin_b, cw_b, cb_b], writes=[xcs_b[u][bi], xc_b], partial=True)
            yield 0
            for k in range(1, 4):
                for bi in range(UB):
                    cblk = cblk_of(bi)
                    P.op('dve', lambda e, xc3=xc3, xin3=xin3, bi=bi, cblk=cblk, k=k: e.scalar_tensor_tensor(
                        out=xc3[:, bi, :], in0=xin3[:, bi, k:k + 128], scalar=cw3[:, cblk, k:k + 1], in1=xc3[:, bi, :],
                        op0=ALU.mult, op1=ALU.add), reads=[xin_b, cw_b, xcs_b[u][bi]], writes=[xcs_b[u][bi]])
                    if bi % 3 == 2:
                        yield 0
            P.op('act', lambda e, xc_ap=xc_ap, xcb_ap=xcb_ap: e.activation(out=xcb_ap, in_=xc_ap, func=AF.Silu),
                 reads=xcs_b[u], writes=[xcb_b])
            P.op('act', lambda e, xc_ap=xc_ap: e.activation(out=xc_ap, in_=xc_ap, func=AF.Silu), reads=xcs_b[u], writes=[xc_b] + xcs_b[u])
            P.op('act', lambda e, zin_ap=zin_ap: e.activation(out=zin_ap, in_=zin_ap, func=AF.Silu), reads=[zin_b], writes=[zin_b])
            yield 0
            if STOP < 3:
                return
            xdt_ap, xdt_b = xdt[u]
            xdtw_ap, xdtw_b = xdtw[u]
            for q in range(XB // 4):
                xp, xp_b = xt_ps[cnt['x'] % len(xt_ps)]
                cnt['x'] += 1
                for j in range(4):
                    P.op('pe', lambda e, xp=xp, xcb3=xcb3, q=q, j=j: e.transpose(
                        xp[:, j * 128:(j + 1) * 128], xcb3[:, q * 4 + j, :], ident_b),
                        reads=[xcb_b, bib], writes=[xp_b], partial=(j > 0))
                h0 = g * HPG + q * 8
                for (dst_ap, dst_b, sc) in ((xdt_ap, xdt_b, dt_tok), (xdtw_ap, xdtw_b, dtw_tok)):
                    P.op('dve', lambda e, xp=xp, dst_ap=dst_ap, sc=sc, q=q, h0=h0, r=r: e.tensor_tensor(
                        out=dst_ap[:, q * 512:(q + 1) * 512].rearrange("p (h c) -> p h c", c=64),
                        in0=xp.rearrange("p (h c) -> p h c", c=64),
                        in1=sc[r][0][:, h0:h0 + 8].unsqueeze(2).to_broadcast([128, 8, 64]), op=ALU.mult),
                        reads=[xp_b, sc[r][1]], writes=[dst_b], partial=(q > 0))
                yield 0
            bp, bp_b = bt_ps[cnt['m'] % len(bt_ps)]
            P.op('pe', lambda e, bp=bp, xcb3=xcb3: e.transpose(bp, xcb3[:, XB, :], ident_b),
                 reads=[xcb_b, bib], writes=[bp_b])
            bt_ap, bt_b = btok[u]
            P.op('act', lambda e, bt_ap=bt_ap, bp=bp: e.copy(out=bt_ap, in_=bp), reads=[bp_b], writes=[bt_b])
            mp, mp_b = misc_ps[cnt['m'] % len(misc_ps)]
            cnt['m'] += 1
            P.op('pe', lambda e, mp=mp, xcb3=xcb3: e.matmul(mp, lhsT=xcb3[:, XB, :], rhs=xcb3[:, XB + 1, :], start=True, stop=True),
                 reads=[xcb_b], writes=[mp_b])
            cm_ap, cm_b = cbTm[u]
            P.op('dve', lambda e, cm_ap=cm_ap, mp=mp: e.tensor_tensor(out=cm_ap, in0=mp, in1=triu, op=ALU.mult),
                 reads=[mp_b, triu_b], writes=[cm_b])
            yield 'SPLIT'
            y_ap, y_b = ych[u]
            y3 = y_ap.rearrange("p (b t) -> p b t", t=128)
            if STOP < 4:
                return
            dsx_ap, dsx_b = dsx[u]
            dsx3 = dsx_ap.rearrange("p (b t) -> p b t", t=128)
            P.op('pool', lambda e, dsx3=dsx3, xc3=xc3, g=g: e.tensor_tensor(
                out=dsx3, in0=xc3[:, 0:XB, :], in1=dsk[:, g * XB:(g + 1) * XB].unsqueeze(2).to_broadcast([128, XB, 128]),
                op=ALU.mult), reads=[xc_b, dsk_b], writes=[dsx_b])
            def stageA(hq):
                hs = g * HPG + hq * 4
                k2 = cnt['h'] % 2
                cnt['h'] += 1
                R_ap, R_b = Rr[k2]
                P.op('dve', lambda e, R_ap=R_ap, hs=hs, r=r: e.tensor_tensor(
                    out=R_ap.rearrange("p (j l) -> p j l", l=128),
                    in0=triu.unsqueeze(1).to_broadcast([128, 4, 128]),
                    in1=dta_tok[r][0][:, hs:hs + 4].unsqueeze(2).to_broadcast([128, 4, 128]), op=ALU.mult),
                    reads=[triu_b, dta_tok[r][1]], writes=[R_b])
                sg, sg_b = seg_ps[k2]
                P.op('pe', lambda e, sg=sg, R_ap=R_ap: e.matmul(sg, lhsT=smask, rhs=R_ap, start=True, stop=True),
                     reads=[smask_b, R_b], writes=[sg_b])
                dc_ap, dc_b = dec[k2]
                P.op('act', lambda e, dc_ap=dc_ap, sg=sg: e.activation(out=dc_ap, in_=sg, func=AF.Exp),
                     reads=[sg_b], writes=[dc_b])
                gt_ap, gt_b = GT[k2]
                P.op('pool', lambda e, gt_ap=gt_ap, dc_ap=dc_ap, cm_ap=cm_ap: e.tensor_tensor(
                    out=gt_ap.rearrange("p (j l) -> p j l", l=128), in0=dc_ap.rearrange("p (j l) -> p j l", l=128),
                    in1=cm_ap.unsqueeze(1).to_broadcast([128, 4, 128]), op=ALU.mult), reads=[dc_b, cm_b], writes=[gt_b])
                eh, eh_b = eh_ps[k2]
                for j in range(4):
                    P.op('pe', lambda e, eh=eh, hs=hs, j=j, r=r: e.transpose(
                        eh[:, j * 128:(j + 1) * 128], ea_tok[r][0][:, hs + j:hs + j + 1].to_broadcast([128, 128]), ident_f),
                        reads=[ea_tok[r][1], bif], writes=[eh_b])
                ct_ap, ct_b = CTs[k2]
                P.op('dve', lambda e, ct_ap=ct_ap, eh=eh, xc3=xc3: e.tensor_tensor(
                    out=ct_ap.rearrange("p (j l) -> p j l", l=128), in0=eh.rearrange("p (j l) -> p j l", l=128),
                    in1=xc3[:, XB + 1, :].unsqueeze(1).to_broadcast([128, 4, 128]), op=ALU.mult),
                    reads=[eh_b, xc_b], writes=[ct_b])
                return dict(gt_ap=gt_ap, gt_b=gt_b, ct_ap=ct_ap, ct_b=ct_b)

            def stageB(hq, ctx):
                gt_ap, gt_b, ct_ap, ct_b = ctx['gt_ap'], ctx['gt_b'], ctx['ct_ap'], ctx['ct_b']
                h8 = (g * HPG + hq * 4) // 8
                yp, yp_b = y_ps[0]
                sp_, sp_b = s_ps[0]
                for j in range(4):
                    hh = hq * 4 + j
                    h = g * HPG + hh
                    pr = (hh % 8) // 2
                    ro = (hh % 2) * 64
                    P.op('pe', lambda e, yp=yp, xdt_ap=xdt_ap, gt_ap=gt_ap, hh=hh, ro=ro, pr=pr, j=j: e.matmul(
                        yp[ro:ro + 64, pr * 128:(pr + 1) * 128], lhsT=xdt_ap[:, hh * 64:(hh + 1) * 64],
                        rhs=gt_ap[:, j * 128:(j + 1) * 128], start=True, stop=False),
                        reads=[xdt_b, gt_b], writes=[yp_b])
                    P.op('pe', lambda e, yp=yp, h=h, ct_ap=ct_ap, ro=ro, pr=pr, j=j: e.matmul(
                        yp[ro:ro + 64, pr * 128:(pr + 1) * 128], lhsT=stb[:, h * 64:(h + 1) * 64],
                        rhs=ct_ap[:, j * 128:(j + 1) * 128], start=False, stop=True),
                        reads=[stb_bs[h8], ct_b], writes=[yp_b])
                for j in range(4):
                    hh = hq * 4 + j
                    P.op('pe', lambda e, sp_=sp_, bt_ap=bt_ap, xdtw_ap=xdtw_ap, hh=hh: e.matmul(
                        sp_[:, (hh % 8) * 64:(hh % 8 + 1) * 64], lhsT=bt_ap, rhs=xdtw_ap[:, hh * 64:(hh + 1) * 64],
                        start=True, stop=True), reads=[bt_b, xdtw_b], writes=[sp_b])
                if hq % 2 == 1:
                    b4 = (hq // 2) * 4
                    h0 = g * HPG + (hq // 2) * 8
                    P.op('dve', lambda e, y3=y3, dsx3=dsx3, yp=yp, b4=b4: e.tensor_tensor(
                        out=y3[:, b4:b4 + 4, :], in0=yp.rearrange("p (b t) -> p b t", t=128), in1=dsx3[:, b4:b4 + 4, :],
                        op=ALU.add), reads=[yp_b, dsx_b], writes=[y_b], partial=(b4 > 0))
                    st8 = st[:, h0 * 64:(h0 + 8) * 64]
                    P.op('pool', lambda e, st8=st8, h0=h0, r=r: e.tensor_tensor(
                        out=st8.rearrange("p (h c) -> p h c", c=64), in0=st8.rearrange("p (h c) -> p h c", c=64),
                        in1=eend[r][0][:, h0:h0 + 8].unsqueeze(2).to_broadcast([128, 8, 64]), op=ALU.mult),
                        reads=[st_bs[h8], eend[r][1]], writes=[st_bs[h8]])
                    P.op('dve', lambda e, st8=st8, sp_=sp_: e.tensor_tensor(out=st8, in0=sp_, in1=st8, op=ALU.add),
                         reads=[st_bs[h8], sp_b], writes=[st_bs[h8]])
                    P.op('act', lambda e, st8=st8, h0=h0: e.copy(out=stb[:, h0 * 64:(h0 + 8) * 64], in_=st8),
                         reads=[st_bs[h8]], writes=[stb_bs[h8]])

            NB4 = HPG // 4
            ctxs = {0: stageA(0)}
            yield 0
            for hq in range(NB4):
                if hq + 1 < NB4:
                    ctxs[hq + 1] = stageA(hq + 1)
                    yield 0
                stageB(hq, ctxs.pop(hq))
                yield 0
            if STOP < 5:
                return
            P.op('dve', lambda e, y_ap=y_ap, zin_ap=zin_ap: e.tensor_tensor(out=y_ap, in0=y_ap, in1=zin_ap, op=ALU.mult),
                 reads=[y_b, zin_b], writes=[y_b])
            q_ap, q_b = ysq[u]
            q3 = q_ap.rearrange("p (b t) -> p b t", t=128)
            P.op('pool', lambda e, q_ap=q_ap, y_ap=y_ap: e.tensor_tensor(out=q_ap, in0=y_ap, in1=y_ap, op=ALU.mult),
                 reads=[y_b], writes=[q_b])
            yield 0
            sp2, sp2_b = ss_ps[u]
            for bi in range(XB):
                P.op('pe', lambda e, sp2=sp2, q3=q3, bi=bi: e.matmul(sp2, lhsT=ones_b, rhs=q3[:, bi, :],
                                                                     start=(bi == 0), stop=(bi == XB - 1)),
                     reads=[ones_bb, q_b], writes=[sp2_b], partial=(bi > 0))
            rs_ap, rs_b = rs[u]
            P.op('act', lambda e, rs_ap=rs_ap, sp2=sp2: e.activation(out=rs_ap, in_=sp2, func=AF.Ln, bias=self.eps_ap,
                                                                     scale=1.0 / (XB * 128)),
                 reads=[sp2_b, self.b_eps], writes=[rs_b])
            P.op('act', lambda e, rs_ap=rs_ap: e.activation(out=rs_ap, in_=rs_ap, func=AF.Exp, scale=-0.5),
                 reads=[rs_b], writes=[rs_b])
            yield 0
            P.op('dve', lambda e, y3=y3, rs_ap=rs_ap: e.tensor_tensor(
                out=y3, in0=y3, in1=rs_ap.unsqueeze(1).to_broadcast([128, XB, 128]), op=ALU.mult),
                reads=[y_b, rs_b], writes=[y_b])
            yb_ap, yb_b = yb[u]
            yb3 = yb_ap.rearrange("p (b t) -> p b t", t=128)
            P.op('pool', lambda e, yb3=yb3, y3=y3, g=g: e.tensor_tensor(
                out=yb3, in0=y3, in1=nw[:, g * XB:(g + 1) * XB].unsqueeze(2).to_broadcast([128, XB, 128]), op=ALU.mult),
                reads=[y_b, nw_b], writes=[yb_b])
            P.op('sp', lambda e, yb3=yb3, g=g, t0=t0: e.dma_start(
                out=self.Y1[g * XB * 128:(g + 1) * XB * 128, t0:t0 + 128].rearrange("(b p) t -> p b t", p=128), in_=yb3),
                reads=[yb_b], writes=[self.b_y1], dma=yb_b, partial=True)

        nchk = min(NCH, getattr(cfg, 'ssd_chunks', NCH))
        units = [(c, g) for c in range(nchk) for g in range(G8 if STOP >= 2 else 0)]
        if not units:
            for c in range(nchk):
                dt_pipe(c)

        def run_front(gen):
            for v in gen:
                if v == 'SPLIT':
                    return True
            return False

        gens = {}
        if units:
            dt_pipe(units[0][0])
            gens[0] = unit(*units[0])
            alive = run_front(gens[0])
            for ui in range(len(units)):
                cur = gens.pop(ui)
                nxt = None
                if ui + 1 < len(units):
                    if units[ui + 1][1] == 0:
                        dt_pipe(units[ui + 1][0])
                    nxt = unit(*units[ui + 1])
                    gens[ui + 1] = nxt
                cur_done = False
                nxt_done = nxt is None
                while not (cur_done and nxt_done):
                    if not cur_done:
                        try:
                            next(cur)
                        except StopIteration:
                            cur_done = True
                    if not nxt_done:
                        try:
                            if next(nxt) == 'SPLIT':
                                nxt_done = True
                        except StopIteration:
                            nxt_done = True
        self.pop()

    def setup_eps(self):
        self.eps_ap, self.b_eps = self.alloc("eps", 1, F32)
        self.P.op('dve', lambda e: e.memset(self.eps_ap, 1e-5), writes=[self.b_eps])


def _t5_bucket(dist):
    max_exact = 16
    d_f = np.maximum(dist, 1).astype(np.float32)
    large = max_exact + (np.log(d_f / np.float32(max_exact)) / np.float32(math.log(2048 / max_exact))
                         * np.float32(32 - max_exact)).astype(np.int32)
    large = np.minimum(large, 31)
    return np.where(dist < max_exact, dist, large)


def make_onehot():
    qi = np.arange(128)[:, None]
    ki = np.arange(256)[None, :]
    delta = 128 + qi - ki
    band = (delta >= 0) & (delta <= 128)
    oh = np.zeros((3, 33, 128 * 256), np.float32)
    for g, dil in enumerate((1, 4, 16)):
        bucket = _t5_bucket(np.clip(delta, 0, None) * dil)
        for b in range(32):
            oh[g, b] = ((bucket == b) & band).reshape(-1)
        oh[g, 32] = (~band).reshape(-1)
    return oh


def build_program(cfg):
    mk = MK(cfg)
    D, S = cfg.D, cfg.S
    P = mk.P
    ident_d = mk.din("ident", [128, 128])
    xT_d = mk.din("xT", [D, S])
    x_d = mk.din("x", [S, D])
    lng_d = mk.din("ln_g", [2, D])
    lnb_d = mk.din("ln_b", [2, D])
    out_d = mk.dout("out", [S, D])
    b_out = Buf("out")
    mk.setup_consts(ident_d)
    mk.setup_eps()
    b_x = Buf("x_in")
    xTb = mk.dscr("xTb", [D, S], BF16)
    b_xTb = Buf("xTb")
    mk.cast_dram(xTb, xT_d, D, b_xTb)
    has0, has1 = (0 in cfg.layers), (1 in cfg.layers)
    if has0:
        NSUP0 = (cfg.NBLK0 + cfg.CB - 1) // cfg.CB
        W0_d = mk.din("W0", [NSUP0, 128, cfg.KC, cfg.CB * 128])
        relb_d = mk.din("relb", [32, 3 * cfg.HG])
        onehot_d = mk.din("onehot", [3, 33, 128 * 256])
        KCE0 = cfg.DATT // 128
        Wo0_d = mk.din("Wo0", [D // 512 * 128, KCE0 * 512])
        Wo0b = mk.dscr("Wo0b", [D // 512 * 128, KCE0 * 512], BF16)
        b_wo0 = Buf("Wo0b")
        mk.cast_dram(Wo0b, Wo0_d, D // 512 * 128, b_wo0, nsplit=4)
        mk.l0_bias_tables(relb_d, onehot_d)
        mk.l0_inproj(W0_d, xTb, b_xTb)
        mk.l0_attention()
        if has1:
            X1 = mk.dscr("X1", [S, D], F32)
            b_x1 = Buf("X1")
            X1T = mk.dscr("X1T", [D, S], BF16)
            b_x1T = Buf("X1T")
            mk.outproj_ln(mk.Y0, mk.b_y0, Wo0b, b_wo0, KCE0, x_d, b_x, lng_d[0], lnb_d[0],
                          X1, b_x1, X1T, b_x1T, "o0")
        else:
            mk.outproj_ln(mk.Y0, mk.b_y0, Wo0b, b_wo0, KCE0, x_d, b_x, lng_d[0], lnb_d[0],
                          out_d, b_out, None, None, "o0")
    else:
        X1, b_x1, X1T, b_x1T = x_d, b_x, xTb, b_xTb
    if has1:
        NB1 = (cfg.DI + cfg.CONV + cfg.NH) // 128
        NSUP1 = (NB1 + cfg.CB - 1) // cfg.CB
        NCONV = cfg.CONV // 128
        NXB = cfg.DI // 128
        W1_d = mk.din("W1", [NSUP1, 128, cfg.KC, cfg.CB * 128])
        convw_d = mk.din("convw", [128, NCONV * 4])
        convb_d = mk.din("convb", [128, NCONV])
        dtb_d = mk.din("dtb", [128, 1])
        alog_d = mk.din("alog", [128, 1])
        dsk_d = mk.din("dsk", [128, NXB])
        nw_d = mk.din("nw", [128, NXB])
        triu_d = mk.din("triu", [128, 128])
        smask_d = mk.din("smask", [128, 128])
        KCE1 = cfg.DI // 128
        Wo1_d = mk.din("Wo1", [D // 512 * 128, KCE1 * 512])
        Wo1b = mk.dscr("Wo1b", [D // 512 * 128, KCE1 * 512], BF16)
        b_wo1 = Buf("Wo1b")
        mk.cast_dram(Wo1b, Wo1_d, D // 512 * 128, b_wo1, nsplit=4)
        stg = getattr(cfg, 'stages', ('inproj', 'ssd', 'outproj'))
        if 'inproj' in stg:
            mk.l1_inproj(W1_d, X1T, b_x1T)
        if 'ssd' in stg:
            mk.l1_ssd(convw_d, convb_d, dtb_d, alog_d, dsk_d, nw_d, triu_d, smask_d)
        else:
            mk.Y1 = mk.dscr("Y1", [cfg.DI, S], BF16)
            mk.b_y1 = Buf("Y1")
        if 'outproj' in stg:
            mk.outproj_ln(mk.Y1, mk.b_y1, Wo1b, b_wo1, KCE1, X1, b_x1, lng_d[1], lnb_d[1],
                          out_d, b_out, None, None, "o1")
    mk.P.finalize()
    return mk


def host_wout(w_out, D):
    E = w_out.shape[0]
    KCE = E // 128
    NDB = D // 512
    return np.ascontiguousarray(w_out.reshape(KCE, 128, NDB, 512).transpose(2, 1, 0, 3).reshape(NDB * 128, KCE * 512))


def host_win(w_in, cols, KC, CB):
    NBLK = len(cols) // 128
    NSUP = (NBLK + CB - 1) // CB
    if NSUP * CB > NBLK:
        cols = np.concatenate([cols, np.tile(cols[-128:], NSUP * CB - NBLK)])
    Wp = w_in[:, cols]
    return np.ascontiguousarray(Wp.reshape(KC, 128, NSUP, CB * 128).transpose(2, 1, 0, 3))


def host_layout_l1(cfg, w_in_ssm, conv_w, conv_b, dt_bias, a_log, d_skip, norm_w, w_out_ssm):
    NCONV = cfg.CONV // 128
    NXB = cfg.DI // 128
    W1 = host_win(w_in_ssm, np.arange(w_in_ssm.shape[1]), cfg.KC, cfg.CB)
    convw = np.ascontiguousarray(conv_w.reshape(4, NCONV, 128).transpose(2, 1, 0).reshape(128, NCONV * 4))
    convb = np.ascontiguousarray(conv_b.reshape(NCONV, 128).T)
    dsk = np.ascontiguousarray(np.repeat(d_skip, cfg.P).reshape(NXB, 128).T)
    nw = np.ascontiguousarray(norm_w.reshape(NXB, 128).T)
    t = np.arange(128)
    triu = (t[:, None] <= t[None, :]).astype(np.float32)
    smask = (t[:, None] > t[None, :]).astype(np.float32)
    return {"W1": W1, "convw": convw, "convb": convb, "dtb": np.ascontiguousarray(dt_bias.reshape(128, 1)),
            "alog": np.ascontiguousarray(a_log.reshape(128, 1)), "dsk": dsk, "nw": nw, "triu": triu, "smask": smask,
            "Wo1": host_wout(w_out_ssm, cfg.D)}


def host_layout_l0(cfg, w_in_attn, w_out_attn, rel_bias):
    D, HG, KC, CB = cfg.D, cfg.HG, cfg.KC, cfg.CB
    DATT = cfg.DATT
    cols = []
    for h in range(HG):
        for g in range(3):
            for j in range(3):
                c0 = g * 3 * DATT + j * DATT + h * 128
                cols.append(np.arange(c0, c0 + 128))
        c0 = 9 * DATT + h * 128
        cols.append(np.arange(c0, c0 + 128))
    NBLK = len(cols)
    NSUP = (NBLK + CB - 1) // CB
    while len(cols) < NSUP * CB:
        cols.append(cols[-1])
    cols = np.concatenate(cols)
    Wp = w_in_attn[:, cols]
    W0 = Wp.reshape(KC, 128, NSUP, CB * 128).transpose(2, 1, 0, 3)
    Wo = host_wout(w_out_attn, D)
    hs = np.concatenate([np.arange(g * (rel_bias.shape[1] // 3), g * (rel_bias.shape[1] // 3) + HG) for g in range(3)])
    return np.ascontiguousarray(W0), Wo, np.ascontiguousarray(rel_bias[:, hs])


_CACHE = {}


def kernel(x, w_in_attn, w_out_attn, rel_bias, w_in_ssm, conv_w, conv_b, dt_bias,
           a_log, d_skip, ssm_norm_w, w_out_ssm, ln_g, ln_b):
    x = np.asarray(x, dtype=np.float32)
    B, S, D = x.shape
    cfg = Cfg(D=D, S=S)
    if 'mk' not in _CACHE:
        _CACHE['mk'] = build_program(cfg)
    mk = _CACHE['mk']
    f = lambda a: np.asarray(a, dtype=np.float32)
    W0, Wo0, rb = host_layout_l0(cfg, f(w_in_attn)[0], f(w_out_attn)[0], f(rel_bias))
    shared = {"ident": np.eye(128, dtype=np.float32), "W0": W0, "relb": rb, "onehot": make_onehot(), "Wo0": Wo0,
              "ln_g": np.ascontiguousarray(f(ln_g)), "ln_b": np.ascontiguousarray(f(ln_b))}
    shared.update(host_layout_l1(cfg, f(w_in_ssm)[0], f(conv_w)[0], f(conv_b)[0], f(dt_bias)[0], f(a_log)[0],
                                 f(d_skip)[0], f(ssm_norm_w)[0], f(w_out_ssm)[0]))
    active = [0, 2, 4, 6]
    zeros = {k: np.zeros_like(v) for k, v in shared.items()}
    zeros["x"] = np.zeros((S, D), np.float32)
    zeros["xT"] = np.zeros((D, S), np.float32)
    in_maps = []
    for c in range(8):
        if c in active:
            b = active.index(c)
            m = dict(shared)
            m["x"] = np.ascontiguousarray(x[b])
            m["xT"] = np.ascontiguousarray(x[b].T)
        else:
            m = zeros
        in_maps.append(m)
    res = run_bass_kernel_spmd(mk.nc, in_maps, core_ids=list(range(8)))
    out = np.stack([np.asarray(res.results[active[b]]["out"], dtype=np.float32) for b in range(B)], axis=0)
    return out
```

```python
from contextlib import ExitStack
import math
import numpy as np
import concourse.bass as bass
import concourse.mybir as mybir
from concourse.bass_utils import run_bass_kernel_spmd

F32 = mybir.dt.float32
BF16 = mybir.dt.bfloat16
AF = mybir.ActivationFunctionType
ALU = mybir.AluOpType
AX = mybir.AxisListType

ENGS = ('pe', 'dve', 'act', 'pool', 'sp')
SIG_CH = 12000
DMA_CH = 700


class Buf:
    __slots__ = ('name', 'writers', 'readers', 'dcount', 'dsems', 'excl')

    def __init__(self, name, excl=False):
        self.name = name
        self.excl = excl
        self.writers = []
        self.readers = []
        self.dcount = 0
        self.dsems = None


class Op:
    __slots__ = ('eng', 'emit', 'deps', 'is_dma', 'dbuf', 'didx', 'need_sig', 'sig', 'idx')


class Prog:
    def __init__(self, nc, stack):
        self.nc = nc
        self.stack = stack
        self.ops = []
        self.by_eng = {e: [] for e in ENGS}
        self.bar = {}

    def op(self, eng, emit, reads=(), writes=(), dma=None, partial=False):
        o = Op()
        o.eng = eng
        o.emit = emit
        o.is_dma = dma is not None
        o.dbuf = dma
        o.need_sig = False
        o.sig = None
        o.idx = len(self.ops)
        deps = {}
        if any(b.excl for b in reads):
            writes = list(writes) + [b for b in reads if b.excl and b not in writes]
            reads = [b for b in reads if not b.excl]
        for b in reads:
            for w in b.writers:
                deps[w] = 'w'
        for b in writes:
            for w in b.writers:
                deps[w] = 'w'
            for r in b.readers:
                if r not in deps:
                    deps[r] = 'r'
        if eng in self.bar:
            for d in self.bar.pop(eng):
                deps[d] = 'w'
        deps.pop(o, None)
        o.deps = deps
        for b in reads:
            b.readers.append(o)
        for b in writes:
            if partial and not b.excl:
                b.writers.append(o)
            else:
                b.writers = [o]
            b.readers = []
        if o.is_dma:
            dma.dcount += 1
            o.didx = dma.dcount
        self.ops.append(o)
        self.by_eng[eng].append(o)
        return o

    def barrier(self):
        deps = []
        for e in ENGS:
            for o in reversed(self.by_eng[e]):
                if not o.is_dma:
                    deps.append(o)
                    break
        lastd = {}
        for o in self.ops:
            if o.is_dma:
                lastd[id(o.dbuf)] = o
        deps.extend(lastd.values())
        for e in ENGS:
            self.bar[e] = list(deps)

    def finalize(self):
        nc = self.nc
        for o in self.ops:
            for d, kind in o.deps.items():
                if d.is_dma:
                    continue
                if d.eng == o.eng and not o.is_dma:
                    if o.eng == 'pe' or kind == 'r':
                        continue
                d.need_sig = True
        cnt = {e: 0 for e in ENGS}
        for o in self.ops:
            if not o.is_dma and o.need_sig:
                cnt[o.eng] += 1
                o.sig = (o.eng, (cnt[o.eng] - 1) // SIG_CH, (cnt[o.eng] - 1) % SIG_CH + 1)
        esems = {}
        nsem = 0
        for e in ENGS:
            n = (cnt[e] + SIG_CH - 1) // SIG_CH
            esems[e] = [self.stack.enter_context(nc.semaphore(f"s_{e}_{i}")) for i in range(n)]
            nsem += n
        seen_b = set()
        for o in self.ops:
            if o.is_dma and id(o.dbuf) not in seen_b:
                seen_b.add(id(o.dbuf))
                b = o.dbuf
                n = (b.dcount + DMA_CH - 1) // DMA_CH
                b.dsems = [self.stack.enter_context(nc.semaphore(f"d_{b.name}_{i}")) for i in range(n)]
                nsem += n
        self.n_sems = nsem

        def sig_of(d):
            if d.is_dma:
                k = d.didx - 1
                return d.dbuf.dsems[k // DMA_CH], 16 * (k % DMA_CH + 1)
            e, si, v = d.sig
            return esems[e][si], v

        engh = {'pe': 'tensor', 'dve': 'vector', 'act': 'scalar', 'pool': 'gpsimd', 'sp': 'sync'}
        block = self.stack.enter_context(nc.Block())

        def make(ename):
            ops = self.by_eng[ename]

            def body(eng):
                seen = {}
                for o in ops:
                    need = {}
                    for d, kind in o.deps.items():
                        if not d.is_dma and d.eng == o.eng and not o.is_dma:
                            if o.eng == 'pe' or kind == 'r':
                                continue
                        s, v = sig_of(d)
                        k = id(s)
                        if seen.get(k, 0) >= v:
                            continue
                        if k not in need or need[k][1] < v:
                            need[k] = (s, v)
                    for k, (s, v) in need.items():
                        eng.wait_ge(s, v)
                        seen[k] = v
                    ins = o.emit(eng)
                    if o.is_dma:
                        s, v = sig_of(o)
                        ins.then_inc(s, 16)
                    elif o.sig is not None:
                        s, v = sig_of(o)
                        ins.then_inc(s, 1)
                last = {}
                for o in ops:
                    if o.is_dma:
                        s, v = sig_of(o)
                        last[id(s)] = (s, max(v, last.get(id(s), (s, 0))[1]))
                for k, (s, v) in last.items():
                    if seen.get(k, 0) < v:
                        eng.wait_ge(s, v)
            return body

        for e in ENGS:
            if self.by_eng[e]:
                getattr(block, engh[e])(make(e))


class Cfg:
    def __init__(self, D=4096, S=4096, HG=16, G8=8, HPG=16, debug=False, layers=(0, 1)):
        self.D = D
        self.S = S
        self.KC = D // 128
        self.HG = HG
        self.DATT = HG * 128
        self.NBLK0 = HG * 10
        self.patterns = ((128, 1), (512, 4), (2048, 16))
        self.G8 = G8
        self.HPG = HPG
        self.P = 64
        self.N = 128
        self.NH = G8 * HPG
        self.DI = self.NH * self.P
        self.CONV = self.DI + 2 * G8 * self.N
        self.debug = debug
        self.layers = layers
        self.alpha = (2 * 2) ** 0.25
        self.CB = 4 if D < 4096 else 8


ARENA_WORDS = 51 * 1024


class MK:
    def __init__(self, cfg):
        self.cfg = cfg
        self.nc = bass.Bass("TRN2", target_bir_lowering=False)
        self.stack = ExitStack()
        self.P = Prog(self.nc, self.stack)
        self.arena = self.stack.enter_context(self.nc.sbuf_tensor("arena", [128, ARENA_WORDS], F32))
        self.top = 0
        self.marks = []
        self.banks = []
        for i in range(8):
            t = self.stack.enter_context(self.nc.psum_tensor(f"bank{i}", [128, 512], F32))
            self.banks.append((t, Buf(f"bank{i}", excl=True)))
        self.dram = {}
        self.scr_kind = "ExternalOutput" if cfg.debug else "Internal"

    def alloc(self, name, cols, dt=F32):
        words = cols if dt == F32 else (cols + 1) // 2
        words = (words + 7) // 8 * 8
        a = self.top
        self.top += words
        assert self.top <= ARENA_WORDS, f"SBUF arena overflow at {name}: {self.top}"
        ap = self.arena[:, a:a + words]
        if dt != F32:
            ap = ap.bitcast(dt)[:, :cols]
        else:
            ap = ap[:, :cols]
        return ap, Buf(name)

    def push(self):
        self.marks.append(self.top)

    def pop(self):
        self.P.barrier()
        self.top = self.marks.pop()

    def din(self, name, shape, dt=F32):
        t = self.nc.dram_tensor(name, list(shape), dt, kind="ExternalInput")
        self.dram[name] = t
        return t.ap()

    def dout(self, name, shape, dt=F32):
        t = self.nc.dram_tensor(name, list(shape), dt, kind="ExternalOutput")
        self.dram[name] = t
        return t.ap()

    def dscr(self, name, shape, dt=F32):
        t = self.nc.dram_tensor(name, list(shape), dt, kind=self.scr_kind)
        self.dram[name] = t
        return t.ap()

    def bank(self, i, dt=F32):
        t, b = self.banks[i]
        ap = t[:]
        if dt != F32:
            ap = ap.bitcast(dt)
        return ap, b

    def setup_consts(self, ident_d):
        P = self.P
        self.ident_f, b1 = self.alloc("ident_f", 128, F32)
        self.ident_b, b2 = self.alloc("ident_b", 128, BF16)
        self.ones_f, b3 = self.alloc("ones_f", 128, F32)
        self.b_ident_f, self.b_ident_b, self.b_ones = b1, b2, b3
        P.op('sp', lambda e: e.dma_start(out=self.ident_f, in_=ident_d), writes=[b1], dma=b1)
        P.op('pool', lambda e: e.dma_start(out=self.ident_b, in_=ident_d), writes=[b2], dma=b2)
        P.op('dve', lambda e: e.memset(self.ones_f, 1.0), writes=[b3])

    def cast_dram(self, dst, src, rows, bufd, nsplit=8):
        P = self.P
        cols = src.shape[1]
        c = min(cols, 4096)
        a = cols // c
        if a > 1:
            src = src.rearrange("r (a c) -> (r a) c", c=c)
            dst = dst.rearrange("r (a c) -> (r a) c", c=c)
        R = rows * a
        step = min(R, 256)
        first = True
        for i in range(0, R, step):
            P.op('pool', lambda e, i=i: e.dma_start(out=dst[i:i + step, :], in_=src[i:i + step, :]),
                 writes=[bufd], dma=bufd, partial=not first)
            first = False

    def gemm_fm(self, W_d, xT_d, b_xT, KC, T, NBLK, CB, epilogue, tag):
        P = self.P
        self.push()
        NSUP = (NBLK + CB - 1) // CB
        TB = T // 512
        wsb = [self.alloc(f"{tag}_w{i}", KC * CB * 128, BF16) for i in range(2)]
        xsb = [self.alloc(f"{tag}_x{i}", KC * 512, BF16) for i in range(2)]
        steps = [(s, tb) for s in range(NSUP) for tb in range(TB)]

        def load_w(s):
            w_ap, w_b = wsb[s % 2]
            w3 = w_ap.rearrange("p (k c) -> p k c", k=KC)
            nq = 4 if KC >= 4 else 1
            kq = KC // nq
            for q in range(nq):
                P.op('pool', lambda e, s=s, q=q, w3=w3: e.dma_start(
                    out=w3[:, q * kq:(q + 1) * kq, :], in_=W_d[s, :, q * kq:(q + 1) * kq, :]),
                    writes=[w_b], dma=w_b, partial=(q > 0))

        def load_x(i):
            s, tb = steps[i]
            x_ap, x_b = xsb[i % 2]
            x3 = x_ap.rearrange("p (k t) -> p k t", k=KC)
            P.op('sp', lambda e, x3=x3, tb=tb: e.dma_start(
                out=x3, in_=xT_d[:, tb * 512:(tb + 1) * 512].rearrange("(k p) t -> p k t", p=128)),
                reads=[b_xT], writes=[x_b], dma=x_b)

        load_w(0)
        load_x(0)
        bcnt = 0
        for i, (s, tb) in enumerate(steps):
            if tb == 0 and s + 1 < NSUP:
                load_w(s + 1)
            if i + 1 < len(steps):
                load_x(i + 1)
            w_ap, w_b = wsb[s % 2]
            w3 = w_ap.rearrange("p (k c) -> p k c", k=KC)
            x_ap, x_b = xsb[i % 2]
            x3 = x_ap.rearrange("p (k t) -> p k t", k=KC)
            ncb = min(CB, NBLK - s * CB)
            for cb in range(ncb):
                ps, ps_b = self.bank(bcnt % 4)
                bcnt += 1
                for k in range(KC):
                    P.op('pe', lambda e, ps=ps, w3=w3, x3=x3, k=k, cb=cb: e.matmul(
                        ps, lhsT=w3[:, k, cb * 128:(cb + 1) * 128], rhs=x3[:, k, :],
                        start=(k == 0), stop=(k == KC - 1)),
                        reads=[w_b, x_b], writes=[ps_b], partial=(k > 0))
                epilogue(s * CB + cb, tb, ps, ps_b)
        self.pop()

    def l0_inproj(self, W_d, xT_d, b_xT):
        cfg, P = self.cfg, self.P
        self.QKVG = self.dscr("QKVG", [cfg.NBLK0, 128, cfg.S], BF16)
        self.b_qkvg = [Buf(f"qkvg{i}") for i in range(cfg.NBLK0)]
        self.push()
        stg = [self.alloc(f"l0stg{i}", 512, BF16) for i in range(4)]
        st = {'c': 0}

        def epi(blk, tb, ps, ps_b):
            s_ap, s_b = stg[st['c'] % 4]
            eng = 'act' if st['c'] % 2 == 0 else 'dve'
            st['c'] += 1
            jj = blk % 10
            d = 1 if jj == 9 else cfg.patterns[jj // 3][1]
            n = 512 // d
            o_v = s_ap.rearrange("p (r j) -> p j r", r=d) if d > 1 else s_ap
            i_v = ps.rearrange("p (j r) -> p j r", r=d) if d > 1 else ps
            if eng == 'act':
                P.op('act', lambda e: e.copy(out=o_v, in_=i_v), reads=[ps_b], writes=[s_b])
            else:
                P.op('dve', lambda e: e.tensor_copy(out=o_v, in_=i_v), reads=[ps_b], writes=[s_b])
            if d > 1:
                dst = self.QKVG[blk].rearrange("p (r j) -> p r j", r=d)[:, :, tb * n:(tb + 1) * n]
                src = s_ap.rearrange("p (r j) -> p r j", r=d)
            else:
                dst = self.QKVG[blk, :, tb * 512:(tb + 1) * 512]
                src = s_ap
            P.op('act', lambda e: e.dma_start(out=dst, in_=src),
                 reads=[s_b], writes=[self.b_qkvg[blk]], dma=s_b, partial=True)

        self.gemm_fm(W_d, xT_d, b_xT, cfg.KC, cfg.S, cfg.NBLK0, cfg.CB, epi, "g0")
        self.pop()

    def l0_bias_tables(self, relb_d, onehot_d):
        cfg, P = self.cfg, self.P
        HG = cfg.HG
        self.BT = self.dscr("BT", [3, HG, 128 * 256], F32)
        self.b_bt = Buf("BT")
        self.push()
        rb, rb_b = self.alloc("rb", 3 * HG, F32)
        P.op('dve', lambda e: e.memset(rb[0:64, :], -1e30), writes=[rb_b])
        P.op('sp', lambda e: e.dma_start(out=rb[0:32, :], in_=relb_d), writes=[rb_b], dma=rb_b)
        oh = [self.alloc(f"oh{i}", 2048, F32) for i in range(2)]
        stg = [self.alloc(f"btst{i}", 2048, F32) for i in range(2)]
        c = 0
        for g in range(3):
            for ch in range(16):
                o_ap, o_b = oh[c % 2]
                s_ap, s_b = stg[c % 2]
                c += 1
                P.op('sp', lambda e, o_ap=o_ap, g=g, ch=ch: e.dma_start(
                    out=o_ap[0:33, :], in_=onehot_d[g, :, ch * 2048:(ch + 1) * 2048]),
                    writes=[o_b], dma=o_b)
                for j in range(4):
                    ps, ps_b = self.bank(j)
                    P.op('pe', lambda e, ps=ps, o_ap=o_ap, g=g, j=j: e.matmul(
                        ps[0:HG, :], lhsT=rb[0:33, g * HG:(g + 1) * HG], rhs=o_ap[0:33, j * 512:(j + 1) * 512],
                        start=True, stop=True), reads=[rb_b, o_b], writes=[ps_b])
                    P.op('act' if j % 2 == 0 else 'dve',
                         (lambda e, ps=ps, s_ap=s_ap, j=j: e.copy(out=s_ap[0:HG, j * 512:(j + 1) * 512], in_=ps[0:HG, :]))
                         if j % 2 == 0 else
                         (lambda e, ps=ps, s_ap=s_ap, j=j: e.tensor_copy(out=s_ap[0:HG, j * 512:(j + 1) * 512], in_=ps[0:HG, :])),
                         reads=[ps_b], writes=[s_b], partial=(j > 0))
                P.op('sp', lambda e, s_ap=s_ap, g=g, ch=ch: e.dma_start(
                    out=self.BT[g, :, ch * 2048:(ch + 1) * 2048], in_=s_ap[0:HG, :]),
                    reads=[s_b], writes=[self.b_bt], dma=s_b, partial=True)
        self.pop()

    def l0_attention(self):
        cfg, P = self.cfg, self.P
        S, HG = cfg.S, cfg.HG
        H2 = S // 2
        NBK = S // 128
        scale = 128 ** -0.5
        self.Y0 = self.dscr("Y0", [cfg.DATT, S], BF16)
        self.b_y0 = Buf("Y0")
        self.push()
        QT = [self.alloc(f"QT{g}", S, BF16) for g in range(3)]
        KT = [self.alloc(f"KT{g}", S + 128, BF16) for g in range(3)]
        VTd = self.alloc("VTd", S, BF16)
        ld = []
        Vtok = [self.alloc(f"Vtok{g}", S + 128, BF16) for g in range(3)]
        BM = [self.alloc(f"BM{g}", 256, F32) for g in range(3)]
        BM0 = [self.alloc(f"BM0{g}", 256, F32) for g in range(3)]
        OT = [self.alloc(f"OT{g}", H2, F32) for g in range(3)]
        LSE = [self.alloc(f"LSE{g}", H2, F32) for g in range(3)]
        gT = self.alloc("gT", S, BF16)
        yT = self.alloc("yT", S, BF16)
        NR = 3
        Sb = [self.alloc(f"Sb{i}", 1024, F32) for i in range(NR)]
        Pf = [self.alloc(f"Pf{i}", 1024, F32) for i in range(NR)]
        Pn = [self.alloc(f"Pn{i}", 1024, BF16) for i in range(NR)]
        PTs = [self.alloc(f"PTs{i}", 1024, BF16) for i in range(NR)]
        Bl = [self.alloc(f"Bl{i}", 512, F32) for i in range(NR)]
        negm = [self.alloc(f"negm{i}", 4, F32) for i in range(NR)]
        den = [self.alloc(f"den{i}", 4, F32) for i in range(NR)]
        rden = [self.alloc(f"rden{i}", 4, F32) for i in range(NR)]
        lnd = [self.alloc(f"lnd{i}", 4, F32) for i in range(NR)]
        lse = [self.alloc(f"lse{i}", 4, F32) for i in range(NR)]
        ones4, ones4_b = self.alloc("ones4", 512, F32)
        CW = 1024
        tmp = [(Sb[i][0][:, :CW], Sb[i][1]) for i in range(3)] + [(Pf[i][0][:, :CW], Pf[i][1]) for i in range(3)]
        SpsA = [self.bank(0), self.bank(2)]
        SpsB = [self.bank(1), self.bank(3)]
        PTps = self.bank(4, BF16)
        oTps = self.bank(5)
        LBps = self.bank(6)
        VTps = self.bank(7, BF16)
        ident_f, ident_b, ones_f = self.ident_f, self.ident_b, self.ones_f
        bif, bib, bon = self.b_ident_f, self.b_ident_b, self.b_ones
        ldc = {'c': 0}
        P.op('dve', lambda e: e.memset(ones4, 1.0), writes=[ones4_b])
        for g in range(3):
            P.op('pool', lambda e, g=g: e.memset(KT[g][0][:, 0:128], 0.0), writes=[KT[g][1]])
            P.op('pool', lambda e, g=g: e.memset(Vtok[g][0][:, 0:128], 0.0), writes=[Vtok[g][1]])

        def load_fm(blk, g, dst, dst_b):
            src = self.QKVG[blk]
            if True:
                P.op('sp', lambda e: e.dma_start(out=dst, in_=src), reads=[self.b_qkvg[blk]], writes=[dst_b], dma=dst_b,
                     partial=True)
                return
            d = cfg.patterns[g][1]
            l_ap, l_b = ld[ldc['c'] % len(ld)]
            ldc['c'] += 1
            P.op('sp', lambda e: e.dma_start(out=l_ap, in_=src), reads=[self.b_qkvg[blk]], writes=[l_b], dma=l_b)
            L = S // d
            P.op('pool', lambda e: e.tensor_copy(out=dst.rearrange("p (r j) -> p j r", j=L),
                                                 in_=l_ap.rearrange("p (j r) -> p j r", r=d)),
                 reads=[l_b], writes=[dst_b], partial=True)

        bcnt = {'c': 0}

        def batch(g, blocks, half, first_of_group):
            d = cfg.patterns[g][1]
            L = S // d
            nb = L // 128
            hb = nb // 2
            s = bcnt['c'] % NR
            bcnt['c'] += 1
            q_ap, q_b = QT[g]
            k_ap, k_b = KT[g]
            v_ap, v_b = Vtok[g]
            o_ap, o_b = OT[g]
            l_ap, l_b = LSE[g]
            o3 = o_ap.rearrange("p (j r) -> p j r", r=d)
            l3 = l_ap.rearrange("p (j r) -> p j r", r=d)
            sA, sA_b = SpsA[(bcnt['c'] - 1) % 2]
            sB, sB_b = SpsB[(bcnt['c'] - 1) % 2]
            sb, sb_b = Sb[s]
            pf, pf_b = Pf[s]
            pn, pn_b = Pn[s]
            pts, pts_b = PTs[s]
            bl, bl_b = Bl[s]
            nm, nm_b = negm[s]
            dn, dn_b = den[s]
            rd, rd_b = rden[s]
            ln_, ln_b = lnd[s]
            ls, ls_b = lse[s]
            for j, (r, jb) in enumerate(blocks):
                c0 = r * L + jb * 128
                sp_, sp_b = (sA, sA_b) if j < 2 else (sB, sB_b)
                col = (j % 2) * 256
                P.op('pe', lambda e, sp_=sp_, col=col, c0=c0: e.matmul(
                    sp_[:, col:col + 256], lhsT=q_ap[:, c0:c0 + 128], rhs=k_ap[:, c0:c0 + 256], start=True, stop=True),
                    reads=[q_b, k_b], writes=[sp_b])
            yield 0
            for j, (r, jb) in enumerate(blocks):
                sp_, sp_b = (sA, sA_b) if j < 2 else (sB, sB_b)
                col = (j % 2) * 256
                bm_ap, bm_b = BM[g] if jb > 0 else BM0[g]
                P.op('dve', lambda e, sp_=sp_, col=col, j=j, bm_ap=bm_ap: e.scalar_tensor_tensor(
                    out=sb[:, j * 256:(j + 1) * 256], in0=sp_[:, col:col + 256], scalar=scale, in1=bm_ap,
                    op0=ALU.mult, op1=ALU.add), reads=[sp_b, bm_b], writes=[sb_b], partial=(j > 0))
            P.op('dve', lambda e: e.tensor_reduce(out=nm, in_=sb.rearrange("p (b k) -> p b k", k=256), axis=AX.X, op=ALU.max,
                                                  negate=True), reads=[sb_b], writes=[nm_b])
            yield 0
            for j in range(4):
                P.op('act', lambda e, j=j: e.activation(out=pf[:, j * 256:(j + 1) * 256], in_=sb[:, j * 256:(j + 1) * 256],
                                                        func=AF.Exp, bias=nm[:, j:j + 1], scale=1.0, accum_out=dn[:, j:j + 1]),
                     reads=[sb_b, nm_b], writes=[pf_b, dn_b], partial=(j > 0))
            yield 0
            P.op('dve', lambda e: e.reciprocal(out=rd, in_=dn), reads=[dn_b], writes=[rd_b])
            P.op('act', lambda e: e.activation(out=ln_, in_=dn, func=AF.Ln), reads=[dn_b], writes=[ln_b])
            P.op('pool', lambda e: e.tensor_tensor(out=pn.rearrange("p (b k) -> p b k", k=256),
                                                   in0=pf.rearrange("p (b k) -> p b k", k=256),
                                                   in1=rd.unsqueeze(2).to_broadcast([128, 4, 256]), op=ALU.mult),
                 reads=[pf_b, rd_b], writes=[pn_b])
            P.op('dve', lambda e: e.tensor_tensor(out=ls, in0=ln_, in1=nm, op=ALU.subtract), reads=[ln_b, nm_b], writes=[ls_b])
            yield 0
            ptp, ptp_b = PTps
            for c in range(8):
                P.op('pe', lambda e, c=c: e.transpose(ptp[:, c * 128:(c + 1) * 128], pn[:, c * 128:(c + 1) * 128], ident_b),
                     reads=[pn_b, bib], writes=[ptp_b])
            P.op('act', lambda e: e.copy(out=pts, in_=ptp[:, 0:1024]), reads=[ptp_b], writes=[pts_b])
            yield 0
            otp, otp_b = oTps
            for j, (r, jb) in enumerate(blocks):
                kb = (r * L + jb * 128) // 128
                for c in range(2):
                    P.op('pe', lambda e, j=j, c=c, kb=kb: e.matmul(
                        otp[:, j * 128:(j + 1) * 128], lhsT=v_ap[:, (kb + c) * 128:(kb + c + 1) * 128],
                        rhs=pts[:, (2 * j + c) * 128:(2 * j + c + 1) * 128], start=(c == 0), stop=(c == 1)),
                        reads=[v_b, pts_b], writes=[otp_b])
            r0, jb0 = blocks[0]
            nr = len(set(r for r, _ in blocks))
            nbj = 4 // nr
            jl0 = jb0 - half * hb

            def dest(t3):
                return t3[:, jl0 * 128:(jl0 + nbj) * 128, r0:r0 + nr].rearrange("p (b j) r -> p r b j", j=128)
            P.op('act', lambda e: e.copy(out=dest(o3), in_=otp.rearrange("p (r b j) -> p r b j", r=nr, b=nbj)),
                 reads=[otp_b], writes=[o_b], partial=(not first_of_group))
            yield 0
            P.op('dve', lambda e: e.tensor_tensor(out=bl.rearrange("p (b j) -> p b j", j=128),
                                                  in0=ones4.rearrange("p (b j) -> p b j", j=128),
                                                  in1=ls.unsqueeze(2).to_broadcast([128, 4, 128]), op=ALU.mult),
                 reads=[ls_b, ones4_b], writes=[bl_b])
            lbp, lbp_b = LBps
            for j in range(4):
                P.op('pe', lambda e, j=j: e.transpose(lbp[:, j * 128:(j + 1) * 128], bl[:, j * 128:(j + 1) * 128], ident_f),
                     reads=[bl_b, bif], writes=[lbp_b])
            P.op('dve', lambda e: e.tensor_copy(out=dest(l3), in_=lbp.rearrange("p (r b j) -> p r b j", r=nr, b=nbj)),
                 reads=[lbp_b], writes=[l_b], partial=(not first_of_group))
            yield 0

        def run_batches(gens):
            active = []
            todo = list(gens)
            while todo or active:
                if todo and len(active) < 3:
                    active.append(todo.pop(0))
                for gen in list(active):
                    try:
                        next(gen)
                    except StopIteration:
                        active.remove(gen)

        for h in range(HG):
            base = h * 10
            for g in range(3):
                load_fm(base + 3 * g + 0, g, QT[g][0], QT[g][1])
                load_fm(base + 3 * g + 1, g, KT[g][0][:, 128:S + 128], KT[g][1])
                load_fm(base + 3 * g + 2, g, VTd[0], VTd[1])
                v_ap, v_b = Vtok[g]
                vps, vps_b = VTps
                for q4 in range(S // 512):
                    for j in range(4):
                        blk = q4 * 4 + j
                        P.op('pe', lambda e, blk=blk, j=j: e.transpose(
                            vps[:, j * 128:(j + 1) * 128], VTd[0][:, blk * 128:(blk + 1) * 128], ident_b),
                            reads=[VTd[1], bib], writes=[vps_b])
                    P.op('act', lambda e, q4=q4, v_ap=v_ap: e.copy(out=v_ap[:, 128 + q4 * 512:128 + (q4 + 1) * 512], in_=vps[:, 0:512]),
                         reads=[vps_b], writes=[v_b], partial=True)
                bm_ap, bm_b = BM[g]
                P.op('sp', lambda e, g=g, h=h, bm_ap=bm_ap: e.dma_start(
                    out=bm_ap, in_=self.BT[g, h, :].rearrange("(q k) -> q k", k=256)),
                    reads=[self.b_bt], writes=[bm_b], dma=bm_b)
                b0_ap, b0_b = BM0[g]
                P.op('pool', lambda e, b0_ap=b0_ap: e.memset(b0_ap[:, 0:128], -1e30), writes=[b0_b])
                P.op('pool', lambda e, b0_ap=b0_ap, bm_ap=bm_ap: e.tensor_copy(out=b0_ap[:, 128:256], in_=bm_ap[:, 128:256]),
                     reads=[bm_b], writes=[b0_b], partial=True)
            P.op('sp', lambda e, base=base: e.dma_start(out=gT[0], in_=self.QKVG[base + 9]),
                 reads=[self.b_qkvg[base + 9]], writes=[gT[1]], dma=gT[1])

            for half in range(2):
                gens = []
                for g in range(3):
                    d = cfg.patterns[g][1]
                    nb = (S // d) // 128
                    hb = nb // 2
                    blks = [(r, jb) for r in range(d) for jb in range(half * hb, (half + 1) * hb)]
                    for i in range(0, len(blks), 4):
                        gens.append(batch(g, blks[i:i + 4], half, i == 0))
                run_batches(gens)
                for cc in range(H2 // CW):
                    sl = slice(cc * CW, (cc + 1) * CW)
                    gsl = slice(half * H2 + cc * CW, half * H2 + (cc + 1) * CW)
                    (mx, mx_b), (e0, e0_b), (e1, e1_b), (e2, e2_b), (zz, zz_b), (acc, acc_b) = tmp
                    ee = [(e0, e0_b), (e1, e1_b), (e2, e2_b)]
                    P.op('dve', lambda e, sl=sl: e.tensor_tensor(out=mx, in0=LSE[0][0][:, sl], in1=LSE[1][0][:, sl], op=ALU.max),
                         reads=[LSE[0][1], LSE[1][1]], writes=[mx_b])
                    P.op('dve', lambda e, sl=sl: e.tensor_tensor(out=mx, in0=mx, in1=LSE[2][0][:, sl], op=ALU.max),
                         reads=[LSE[2][1], mx_b], writes=[mx_b])
                    for g in range(3):
                        ea, ea_b = ee[g]
                        P.op('pool', lambda e, ea=ea, g=g, sl=sl: e.tensor_tensor(out=ea, in0=LSE[g][0][:, sl], in1=mx, op=ALU.subtract),
                             reads=[LSE[g][1], mx_b], writes=[ea_b])
                        P.op('act', lambda e, ea=ea: e.activation(out=ea, in_=ea, func=AF.Exp), reads=[ea_b], writes=[ea_b])
                    P.op('pool', lambda e: e.tensor_tensor(out=zz, in0=e0, in1=e1, op=ALU.add), reads=[e0_b, e1_b], writes=[zz_b])
                    P.op('pool', lambda e: e.tensor_tensor(out=zz, in0=zz, in1=e2, op=ALU.add), reads=[zz_b, e2_b], writes=[zz_b])
                    P.op('dve', lambda e: e.reciprocal(out=zz, in_=zz), reads=[zz_b], writes=[zz_b])
                    for g in range(3):
                        ea, ea_b = ee[g]
                        P.op('dve', lambda e, ea=ea, g=g, sl=sl: e.tensor_tensor(out=ea, in0=ea, in1=OT[g][0][:, sl], op=ALU.mult),
                             reads=[ea_b, OT[g][1]], writes=[ea_b])
                    P.op('pool', lambda e: e.tensor_tensor(out=acc, in0=e0, in1=e1, op=ALU.add), reads=[e0_b, e1_b], writes=[acc_b])
                    P.op('pool', lambda e: e.tensor_tensor(out=acc, in0=acc, in1=e2, op=ALU.add), reads=[acc_b, e2_b], writes=[acc_b])
                    P.op('dve', lambda e: e.tensor_tensor(out=acc, in0=acc, in1=zz, op=ALU.mult), reads=[acc_b, zz_b], writes=[acc_b])
                    P.op('act', lambda e, gsl=gsl: e.activation(out=e0, in_=gT[0][:, gsl], func=AF.Exp, scale=-1.0),
                         reads=[gT[1]], writes=[e0_b])
                    P.op('pool', lambda e: e.tensor_scalar(out=e0, in0=e0, scalar1=1.0, scalar2=1.0, op0=ALU.add, op1=ALU.mult),
                         reads=[e0_b], writes=[e0_b])
                    P.op('dve', lambda e: e.reciprocal(out=e0, in_=e0), reads=[e0_b], writes=[e0_b])
                    P.op('dve', lambda e, gsl=gsl: e.tensor_tensor(out=e0, in0=e0, in1=gT[0][:, gsl], op=ALU.mult),
                         reads=[e0_b, gT[1]], writes=[e0_b])
                    P.op('dve', lambda e, gsl=gsl: e.tensor_tensor(out=yT[0][:, gsl], in0=acc, in1=e0, op=ALU.mult),
                         reads=[acc_b, e0_b], writes=[yT[1]], partial=True)
            P.op('act', lambda e, h=h: e.dma_start(out=self.Y0[h * 128:(h + 1) * 128, :], in_=yT[0]),
                 reads=[yT[1]], writes=[self.b_y0], dma=yT[1], partial=True)
        self.pop()

    def outproj_ln(self, Y_d, b_y, W_d, b_w, KCE, x_d, b_x, g_d, bt_d, out_d, b_out, outT_d, b_outT, tag):
        cfg, P = self.cfg, self.P
        S, D = cfg.S, cfg.D
        NDB = D // 512
        KCD = D // 128
        KG = 16 if KCE >= 16 else KCE
        NKG = KCE // KG
        TS = 256
        NSUB = TS // 128
        self.push()
        Gr, Gr_b = self.alloc(f"{tag}G", D, F32)
        Br, Br_b = self.alloc(f"{tag}B", D, F32)
        P.op('sp', lambda e: e.dma_start(out=Gr, in_=g_d.partition_broadcast(128)), writes=[Gr_b], dma=Gr_b)
        P.op('sp', lambda e: e.dma_start(out=Br, in_=bt_d.partition_broadcast(128)), writes=[Br_b], dma=Br_b)
        ysb = [self.alloc(f"{tag}y{i}", KCE * TS, BF16) for i in range(2)]
        wsb = [self.alloc(f"{tag}w{i}", KG * 512, BF16) for i in range(3)]
        xr = [self.alloc(f"{tag}xr{i}", 512, F32) for i in range(4)]
        v = [self.alloc(f"{tag}v{i}", D, F32) for i in range(NSUB)]
        stats = [self.alloc(f"{tag}st{i}", 6 * NDB, F32) for i in range(NSUB)]
        mv = [self.alloc(f"{tag}mv{i}", 2, F32) for i in range(NSUB)]
        rstd = [self.alloc(f"{tag}rs{i}", 1, F32) for i in range(NSUB)]
        if outT_d is not None:
            xb = [self.alloc(f"{tag}xb{i}", D, BF16) for i in range(NSUB)]
            xts = [self.alloc(f"{tag}xt{i}", KCD * TS, BF16) for i in range(2)]
        wc = 0
        xc = 0
        for ts in range(S // TS):
            y_ap, y_b = ysb[ts % 2]
            y3 = y_ap.rearrange("p (k t) -> p k t", k=KCE)
            P.op('sp', lambda e, y3=y3, ts=ts: e.dma_start(
                out=y3, in_=Y_d[:, ts * TS:(ts + 1) * TS].rearrange("(k p) t -> p k t", p=128)),
                reads=[b_y], writes=[y_b], dma=y_b)
            for db in range(NDB):
                for kg in range(NKG):
                    w_ap, w_b = wsb[wc % 3]
                    wc += 1
                    w3 = w_ap.rearrange("p (k c) -> p k c", k=KG)
                    P.op('sp', lambda e, w3=w3, db=db, kg=kg: e.dma_start(
                        out=w3, in_=W_d[db * 128:(db + 1) * 128, kg * KG * 512:(kg + 1) * KG * 512].rearrange("p (k c) -> p k c", k=KG)), reads=[b_w], writes=[w_b], dma=w_b)
                    for k in range(KG):
                        ka = kg * KG + k
                        for sub in range(NSUB):
                            ps, ps_b = self.bank(sub + 2 * (db % 2))
                            P.op('pe', lambda e, ps=ps, y3=y3, w3=w3, ka=ka, k=k, sub=sub: e.matmul(
                                ps, lhsT=y3[:, ka, sub * 128:(sub + 1) * 128], rhs=w3[:, k, :],
                                start=(ka == 0), stop=(ka == KCE - 1)),
                                reads=[y_b, w_b], writes=[ps_b], partial=(ka > 0))
                for sub in range(NSUB):
                    ps, ps_b = self.bank(sub + 2 * (db % 2))
                    x_ap, x_b = xr[xc % 4]
                    xc += 1
                    r0 = ts * TS + sub * 128
                    P.op('sp', lambda e, x_ap=x_ap, r0=r0, db=db: e.dma_start(
                        out=x_ap, in_=x_d[r0:r0 + 128, db * 512:(db + 1) * 512]), reads=[b_x], writes=[x_b], dma=x_b)
                    v_ap, v_b = v[sub]
                    P.op('dve', lambda e, v_ap=v_ap, x_ap=x_ap, ps=ps, db=db: e.scalar_tensor_tensor(
                        out=v_ap[:, db * 512:(db + 1) * 512], in0=x_ap, scalar=cfg.alpha, in1=ps,
                        op0=ALU.mult, op1=ALU.add), reads=[x_b, ps_b], writes=[v_b], partial=(db > 0))
                    s_ap, s_b = stats[sub]
                    P.op('dve', lambda e, s_ap=s_ap, v_ap=v_ap, db=db: e.bn_stats(
                        out=s_ap[:, db * 6:(db + 1) * 6], in_=v_ap[:, db * 512:(db + 1) * 512]),
                        reads=[v_b], writes=[s_b], partial=(db > 0))
            for sub in range(NSUB):
                v_ap, v_b = v[sub]
                s_ap, s_b = stats[sub]
                m_ap, m_b = mv[sub]
                r_ap, r_b = rstd[sub]
                P.op('dve', lambda e, m_ap=m_ap, s_ap=s_ap: e.bn_aggr(out=m_ap, in_=s_ap), reads=[s_b], writes=[m_b])
                P.op('act', lambda e, r_ap=r_ap, m_ap=m_ap: e.activation(out=r_ap, in_=m_ap[:, 1:2], func=AF.Ln, bias=self.eps_ap, scale=1.0),
                     reads=[m_b, self.b_eps], writes=[r_b])
                P.op('act', lambda e, r_ap=r_ap: e.activation(out=r_ap, in_=r_ap, func=AF.Exp, scale=-0.5),
                     reads=[r_b], writes=[r_b])
                P.op('dve', lambda e, v_ap=v_ap, m_ap=m_ap, r_ap=r_ap: e.tensor_scalar(
                    out=v_ap, in0=v_ap, scalar1=m_ap[:, 0:1], scalar2=r_ap, op0=ALU.subtract, op1=ALU.mult),
                    reads=[v_b, m_b, r_b], writes=[v_b])
                P.op('pool', lambda e, v_ap=v_ap: e.tensor_tensor(out=v_ap, in0=v_ap, in1=Gr, op=ALU.mult),
                     reads=[v_b, Gr_b], writes=[v_b])
                P.op('pool', lambda e, v_ap=v_ap: e.tensor_tensor(out=v_ap, in0=v_ap, in1=Br, op=ALU.add),
                     reads=[v_b, Br_b], writes=[v_b])
                r0 = ts * TS + sub * 128
                P.op('pool', lambda e, v_ap=v_ap, r0=r0: e.dma_start(out=out_d[r0:r0 + 128, :], in_=v_ap),
                     reads=[v_b], writes=[b_out], dma=v_b, partial=True)
                if outT_d is not None:
                    xb_ap, xb_b = xb[sub]
                    P.op('act', lambda e, xb_ap=xb_ap, v_ap=v_ap: e.copy(out=xb_ap, in_=v_ap), reads=[v_b], writes=[xb_b])
                    xt_ap, xt_b = xts[ts % 2]
                    xt3 = xt_ap.rearrange("p (k t) -> p k t", k=KCD)
                    for q4 in range(KCD // 4):
                        tp, tp_b = self.bank(5 + (q4 % 2), BF16)
                        for j in range(4):
                            kk = q4 * 4 + j
                            P.op('pe', lambda e, tp=tp, xb_ap=xb_ap, kk=kk, j=j: e.transpose(
                                tp[:, j * 128:(j + 1) * 128], xb_ap[:, kk * 128:(kk + 1) * 128], self.ident_b),
                                reads=[xb_b, self.b_ident_b], writes=[tp_b], partial=(j > 0))
                        P.op('act', lambda e, tp=tp, xt3=xt3, q4=q4, sub=sub: e.copy(
                            out=xt3[:, q4 * 4:(q4 + 1) * 4, sub * 128:(sub + 1) * 128],
                            in_=tp[:, 0:512].rearrange("p (k t) -> p k t", k=4)),
                            reads=[tp_b], writes=[xt_b], partial=not (sub == 0 and q4 == 0))
            if outT_d is not None:
                xt_ap, xt_b = xts[ts % 2]
                xt3 = xt_ap.rearrange("p (k t) -> p k t", k=KCD)
                P.op('act', lambda e, xt3=xt3, ts=ts: e.dma_start(
                    out=outT_d[:, ts * TS:(ts + 1) * TS].rearrange("(k p) t -> p k t", p=128), in_=xt3),
                    reads=[xt_b], writes=[b_outT], dma=xt_b, partial=True)
        self.pop()

    def l1_inproj(self, W_d, xT_d, b_xT):
        cfg, P = self.cfg, self.P
        NB = (cfg.DI + cfg.CONV + cfg.NH) // 128
        self.NBLK1 = NB
        self.ZXs = []
        for i in range(0, NB, 64):
            n = min(64, NB - i)
            self.ZXs.append(self.dscr(f"ZX{i // 64}", [n, 128, cfg.S], F32))
        self.b_zx = [Buf(f"zx{i}") for i in range(NB)]
        self.push()
        stg = [self.alloc(f"l1stg{i}", 512, F32) for i in range(4)]
        st = {'c': 0}

        def epi(blk, tb, ps, ps_b):
            s_ap, s_b = stg[st['c'] % 4]
            eng = 'act' if st['c'] % 2 == 0 else 'dve'
            st['c'] += 1
            if eng == 'act':
                P.op('act', lambda e: e.copy(out=s_ap, in_=ps), reads=[ps_b], writes=[s_b])
            else:
                P.op('dve', lambda e: e.tensor_copy(out=s_ap, in_=ps), reads=[ps_b], writes=[s_b])
            P.op('act', lambda e: e.dma_start(out=self.zx(blk, 1)[0, :, tb * 512:(tb + 1) * 512], in_=s_ap),
                 reads=[s_b], writes=[self.b_zx[blk]], dma=s_b, partial=True)

        self.gemm_fm(W_d, xT_d, b_xT, cfg.KC, cfg.S, NB, cfg.CB, epi, "g1")
        self.pop()

    def zx(self, b0, nb):
        t = self.ZXs[b0 // 64]
        l0 = b0 % 64
        assert l0 + nb <= 64
        return t[l0:l0 + nb]

    def l1_ssd(self, convw_d, convb_d, dtb_d, alog_d, dsk_d, nw_d, triu_d, smask_d):
        cfg, P = self.cfg, self.P
        S, G8, HPG, NH = cfg.S, cfg.G8, cfg.HPG, cfg.NH
        XB = HPG * 64 // 128
        NXB = G8 * XB
        UB = XB + 2
        NCONV = NXB + 2 * G8
        ZB0, XB0 = 0, NXB
        BB0 = 2 * NXB
        CB0 = BB0 + G8
        DTB = CB0 + G8
        self.Y1 = self.dscr("Y1", [cfg.DI, S], BF16)
        self.b_y1 = Buf("Y1")
        self.push()
        triu, triu_b = self.alloc("triu", 128, F32)
        smask, smask_b = self.alloc("smask", 128, F32)
        ones_b, ones_bb = self.alloc("ones_b", 128, BF16)
        cw, cw_b = self.alloc("convw", NCONV * 4, F32)
        cb_, cb_b = self.alloc("convb", NCONV, F32)
        dsk, dsk_b = self.alloc("dsk", NXB, F32)
        nw, nw_b = self.alloc("nw", NXB, F32)
        dtb, dtb_b = self.alloc("dtb", 1, F32)
        acol, acol_b = self.alloc("acol", 1, F32)
        for ap_, b_, src in ((triu, triu_b, triu_d), (smask, smask_b, smask_d), (cw, cw_b, convw_d), (cb_, cb_b, convb_d),
                             (dsk, dsk_b, dsk_d), (nw, nw_b, nw_d), (dtb, dtb_b, dtb_d), (acol, acol_b, alog_d)):
            P.op('sp', lambda e, ap_=ap_, src=src: e.dma_start(out=ap_, in_=src), writes=[b_], dma=b_)
        P.op('dve', lambda e: e.memset(ones_b, 1.0), writes=[ones_bb])
        P.op('act', lambda e: e.activation(out=acol, in_=acol, func=AF.Exp), reads=[acol_b], writes=[acol_b])
        P.op('dve', lambda e: e.tensor_scalar(out=acol, in0=acol, scalar1=-1.0, scalar2=None, op0=ALU.mult),
             reads=[acol_b], writes=[acol_b])
        cw3 = cw.rearrange("p (b k) -> p b k", k=4)
        st, st_b = self.alloc("state", NH * 64, F32)
        stb, stb_b = self.alloc("stateb", NH * 64, BF16)
        st_bs = [Buf(f"st{h}") for h in range(NH)]
        stb_bs = [Buf(f"stb{h}") for h in range(NH)]
        P.op('dve', lambda e: e.memset(st, 0.0), writes=st_bs)
        P.op('pool', lambda e: e.memset(stb, 0.0), writes=stb_bs)
        def ring(name, cols, dt=F32, n=2):
            return [self.alloc(f"{name}{i}", cols, dt) for i in range(n)]
        dtr = ring("dtr", 128); xb_ = ring("xb", 128); ax = ring("ax", 128); dtT = ring("dtT", 128)
        dtaT = ring("dtaT", 128); dt_tok = ring("dt_tok", 128); dta_tok = ring("dta_tok", 128)
        acum = ring("acum", 128); ea_tok = ring("ea_tok", 128); eend = ring("eend", 128)
        toend = ring("toend", 128); dtw_tok = ring("dtw_tok", 128)
        xin = ring("xin", UB * 131); xc = ring("xc", UB * 128); xcb = ring("xcb", UB * 128, BF16)
        zin = ring("zin", XB * 128); ych = ring("ych", XB * 128); ysq = ring("ysq", XB * 128, BF16)
        yb = ring("yb", XB * 128, BF16)
        xdt = ring("xdt", XB * 128, BF16); xdtw = ring("xdtw", XB * 128, BF16)
        btok = ring("btok", 128, BF16); cbTm = ring("cbTm", 128); rs = ring("rs", 128)
        Rr = ring("Rr", 512); dec = ring("dec", 512)
        GT = ring("GT", 512, BF16); CTs = ring("CTs", 512, BF16)
        dsx = ring("dsx", XB * 128)
        xcs_b = [[Buf(f"xcs{i}_{j}") for j in range(UB)] for i in range(2)]
        def sub(bank, off, w, dt=F32):
            ap, b = self.bank(bank, dt)
            return (ap[:, off:off + w], b)
        seg_ps = [sub(0, 0, 512), sub(1, 0, 512)]
        eh_ps = [sub(2, 0, 512), sub(3, 0, 512)]
        y_ps = [sub(4, 0, 512)]
        s_ps = [sub(5, 0, 512)]
        misc_ps = [sub(6, 0, 128)]
        ss_ps = [sub(6, 128, 128), sub(6, 128, 128)]
        xt_ps = [sub(7, 0, 512, BF16)]
        bt_ps = [sub(7, 512, 128, BF16)]
        ident_f, ident_b, ones_f = self.ident_f, self.ident_b, self.ones_f
        bif, bib, bon = self.b_ident_f, self.b_ident_b, self.b_ones
        cnt = {'m': 0, 'h': 0, 'y': 0, 's': 0, 'x': 0}
        NCH = S // 128
        STOP = getattr(cfg, 'ssd_stop', 9)
        dlim = getattr(cfg, 'dt_lim', 10 ** 9)
        dcn = {'c': 0}

        def dop(*a, **k):
            dcn['c'] += 1
            if dcn['c'] <= dlim:
                P.op(*a, **k)
        def dt_pipe(c):
            t0 = c * 128
            r = c % 2
            if STOP >= 1:
                dop('sp', lambda e, r=r, t0=t0: e.dma_start(out=dtr[r][0], in_=self.zx(DTB, 1)[0, :, t0:t0 + 128]),
                     reads=[self.b_zx[DTB]], writes=[dtr[r][1]], dma=dtr[r][1])
                dop('dve', lambda e, r=r: e.tensor_scalar(out=xb_[r][0], in0=dtr[r][0], scalar1=dtb, scalar2=None, op0=ALU.add),
                     reads=[dtr[r][1], dtb_b], writes=[xb_[r][1]])
                dop('dve', lambda e, r=r: e.scalar_tensor_tensor(out=ax[r][0], in0=xb_[r][0], scalar=-1.0, in1=xb_[r][0], op0=ALU.mult, op1=ALU.max),
                     reads=[xb_[r][1]], writes=[ax[r][1]])
                dop('act', lambda e, r=r: e.activation(out=ax[r][0], in_=ax[r][0], func=AF.Exp, scale=-1.0),
                     reads=[ax[r][1]], writes=[ax[r][1]])
                dop('act', lambda e, r=r: e.activation(out=ax[r][0], in_=ax[r][0], func=AF.Ln, bias=ones_f[:, 0:1], scale=1.0),
                     reads=[ax[r][1], bon], writes=[ax[r][1]])
                dop('dve', lambda e, r=r: e.scalar_tensor_tensor(out=dtT[r][0], in0=xb_[r][0], scalar=0.0, in1=ax[r][0],
                                                                 op0=ALU.max, op1=ALU.add),
                     reads=[xb_[r][1], ax[r][1]], writes=[dtT[r][1]])
                dop('dve', lambda e, r=r: e.tensor_scalar(out=dtaT[r][0], in0=dtT[r][0], scalar1=acol, scalar2=None, op0=ALU.mult),
                     reads=[dtT[r][1], acol_b], writes=[dtaT[r][1]])
                for src, dst in ((dtT, dt_tok), (dtaT, dta_tok)):
                    mp, mp_b = misc_ps[cnt['m'] % len(misc_ps)]
                    cnt['m'] += 1
                    dop('pe', lambda e, mp=mp, src=src, r=r: e.transpose(mp, src[r][0], ident_f),
                         reads=[src[r][1], bif], writes=[mp_b])
                    dop('act', lambda e, mp=mp, dst=dst, r=r: e.copy(out=dst[r][0], in_=mp), reads=[mp_b], writes=[dst[r][1]])
                mp, mp_b = misc_ps[cnt['m'] % len(misc_ps)]
                cnt['m'] += 1
                dop('pe', lambda e, mp=mp, r=r: e.matmul(mp, lhsT=triu, rhs=dta_tok[r][0], start=True, stop=True),
                     reads=[triu_b, dta_tok[r][1]], writes=[mp_b])
                dop('dve', lambda e, mp=mp, r=r: e.tensor_copy(out=acum[r][0], in_=mp), reads=[mp_b], writes=[acum[r][1]])
                dop('act', lambda e, mp=mp, r=r: e.activation(out=ea_tok[r][0], in_=mp, func=AF.Exp),
                     reads=[mp_b], writes=[ea_tok[r][1]])
                mp2, mp2_b = misc_ps[cnt['m'] % len(misc_ps)]
                cnt['m'] += 1
                dop('pe', lambda e, mp2=mp2, r=r: e.matmul(mp2, lhsT=ones_f, rhs=dta_tok[r][0], start=True, stop=True),
                     reads=[bon, dta_tok[r][1]], writes=[mp2_b])
                dop('act', lambda e, mp2=mp2, r=r: e.activation(out=eend[r][0], in_=mp2, func=AF.Exp),
                     reads=[mp2_b], writes=[eend[r][1]])
                dop('dve', lambda e, mp2=mp2, r=r: e.tensor_tensor(out=toend[r][0], in0=mp2, in1=acum[r][0], op=ALU.subtract),
                     reads=[mp2_b, acum[r][1]], writes=[toend[r][1]])
                dop('act', lambda e, r=r: e.activation(out=toend[r][0], in_=toend[r][0], func=AF.Exp),
                     reads=[toend[r][1]], writes=[toend[r][1]])
                dop('dve', lambda e, r=r: e.tensor_tensor(out=dtw_tok[r][0], in0=dt_tok[r][0], in1=toend[r][0], op=ALU.mult),
                     reads=[dt_tok[r][1], toend[r][1]], writes=[dtw_tok[r][1]])
        def unit(c, g):
            t0 = c * 128
            r = c % 2
            u = (c * G8 + g) % 2
            xin_ap, xin_b = xin[u]
            xin3 = xin_ap.rearrange("p (b t) -> p b t", t=131)
            xc_ap, xc_b = xc[u]
            xc3 = xc_ap.rearrange("p (b t) -> p b t", t=128)
            xcb_ap, xcb_b = xcb[u]
            xcb3 = xcb_ap.rearrange("p (b t) -> p b t", t=128)
            zin_ap, zin_b = zin[u]
            zin3 = zin_ap.rearrange("p (b t) -> p b t", t=128)
            srcs = [(XB0 + g * XB, XB, 0), (BB0 + g, 1, XB), (CB0 + g, 1, XB + 1)]
            first = True
            if c == 0:
                P.op('pool', lambda e, xin3=xin3: e.memset(xin3[:, :, 0:3], 0.0), writes=[xin_b])
                first = False
            for (b0, nb_, o0) in srcs:
                lo = 0 if c > 0 else 3
                P.op('sp', lambda e, xin3=xin3, b0=b0, nb_=nb_, o0=o0, lo=lo, t0=t0: e.dma_start(
                    out=xin3[:, o0:o0 + nb_, lo:131],
                    in_=self.zx(b0, nb_)[:, :, t0 - 3 + lo:t0 + 128].rearrange("b p t -> p b t")),
                    reads=[self.b_zx[b0 + i] for i in range(nb_)], writes=[xin_b], dma=xin_b, partial=(not first))
                first = False
            P.op('sp', lambda e, zin3=zin3, g=g, t0=t0: e.dma_start(
                out=zin3, in_=self.zx(ZB0 + g * XB, XB)[:, :, t0:t0 + 128].rearrange("b p t -> p b t")),
                reads=[self.b_zx[ZB0 + g * XB + i] for i in range(XB)], writes=[zin_b], dma=zin_b)
            def cblk_of(bi):
                return (g * XB + bi) if bi < XB else (NXB + g if bi == XB else NXB + G8 + g)
            for bi in range(UB):
                cblk = cblk_of(bi)
                P.op('act', lambda e, xc3=xc3, xin3=xin3, bi=bi, cblk=cblk: e.activation(
                    out=xc3[:, bi, :], in_=xin3[:, bi, 0:128], func=AF.Identity, scale=cw3[:, cblk, 0:1],
                    bias=cb_[:, cblk:cblk + 1]), reads=[xin_b, cw_b, cb_b], writes=[xcs_b[u][bi], xc_b], partial=True)
            yield 0
            for k in range(1, 4):
                for bi in range(UB):
                    cblk = cblk_of(bi)
                    P.op('dve', lambda e, xc3=xc3, xin3=xin3, bi=bi, cblk=cblk, k=k: e.scalar_tensor_tensor(
                        out=xc3[:, bi, :], in0=xin3[:, bi, k:k + 128], scalar=cw3[:, cblk, k:k + 1], in1=xc3[:, bi, :],
                        op0=ALU.mult, op1=ALU.add), reads=[xin_b, cw_b, xcs_b[u][bi]], writes=[xcs_b[u][bi]])
                    if bi % 3 == 2:
                        yield 0
            P.op('act', lambda e, xc_ap=xc_ap, xcb_ap=xcb_ap: e.activation(out=xcb_ap, in_=xc_ap, func=AF.Silu),
                 reads=xcs_b[u], writes=[xcb_b])
            P.op('act', lambda e, xc_ap=xc_ap: e.activation(out=xc_ap, in_=xc_ap, func=AF.Silu), reads=xcs_b[u], writes=[xc_b] + xcs_b[u])
            P.op('act', lambda e, zin_ap=zin_ap: e.activation(out=zin_ap, in_=zin_ap, func=AF.Silu), reads=[zin_b], writes=[zin_b])
            yield 0
            if STOP < 3:
                return
            xdt_ap, xdt_b = xdt[u]
            xdtw_ap, xdtw_b = xdtw[u]
            for q in range(XB // 4):
                xp, xp_b = xt_ps[cnt['x'] % len(xt_ps)]
                cnt['x'] += 1
                for j in range(4):
                    P.op('pe', lambda e, xp=xp, xcb3=xcb3, q=q, j=j: e.transpose(
                        xp[:, j * 128:(j + 1) * 128], xcb3[:, q * 4 + j, :], ident_b),
                        reads=[xcb_b, bib], writes=[xp_b], partial=(j > 0))
                h0 = g * HPG + q * 8
                for (dst_ap, dst_b, sc) in ((xdt_ap, xdt_b, dt_tok), (xdtw_ap, xdtw_b, dtw_tok)):
                    P.op('dve', lambda e, xp=xp, dst_ap=dst_ap, sc=sc, q=q, h0=h0, r=r: e.tensor_tensor(
                        out=dst_ap[:, q * 512:(q + 1) * 512].rearrange("p (h c) -> p h c", c=64),
                        in0=xp.rearrange("p (h c) -> p h c", c=64),
                        in1=sc[r][0][:, h0:h0 + 8].unsqueeze(2).to_broadcast([128, 8, 64]), op=ALU.mult),
                        reads=[xp_b, sc[r][1]], writes=[dst_b], partial=(q > 0))
                yield 0
            bp, bp_b = bt_ps[cnt['m'] % len(bt_ps)]
            P.op('pe', lambda e, bp=bp, xcb3=xcb3: e.transpose(bp, xcb3[:, XB, :], ident_b),
                 reads=[xcb_b, bib], writes=[bp_b])
            bt_ap, bt_b = btok[u]
            P.op('act', lambda e, bt_ap=bt_ap, bp=bp: e.copy(out=bt_ap, in_=bp), reads=[bp_b], writes=[bt_b])
            mp, mp_b = misc_ps[cnt['m'] % len(misc_ps)]
            cnt['m'] += 1
            P.op('pe', lambda e, mp=mp, xcb3=xcb3: e.matmul(mp, lhsT=xcb3[:, XB, :], rhs=xcb3[:, XB + 1, :], start=True, stop=True),
                 reads=[xcb_b], writes=[mp_b])
            cm_ap, cm_b = cbTm[u]
            P.op('dve', lambda e, cm_ap=cm_ap, mp=mp: e.tensor_tensor(out=cm_ap, in0=mp, in1=triu, op=ALU.mult),
                 reads=[mp_b, triu_b], writes=[cm_b])
            yield 'SPLIT'
            y_ap, y_b = ych[u]
            y3 = y_ap.rearrange("p (b t) -> p b t", t=128)
            if STOP < 4:
                return
            dsx_ap, dsx_b = dsx[u]
            dsx3 = dsx_ap.rearrange("p (b t) -> p b t", t=128)
            P.op('pool', lambda e, dsx3=dsx3, xc3=xc3, g=g: e.tensor_tensor(
                out=dsx3, in0=xc3[:, 0:XB, :], in1=dsk[:, g * XB:(g + 1) * XB].unsqueeze(2).to_broadcast([128, XB, 128]),
                op=ALU.mult), reads=[xc_b, dsk_b], writes=[dsx_b])
            def stageA(hq):
                hs = g * HPG + hq * 4
                k2 = cnt['h'] % 2
                cnt['h'] += 1
                R_ap, R_b = Rr[k2]
                P.op('dve', lambda e, R_ap=R_ap, hs=hs, r=r: e.tensor_tensor(
                    out=R_ap.rearrange("p (j l) -> p j l", l=128),
                    in0=triu.unsqueeze(1).to_broadcast([128, 4, 128]),
                    in1=dta_tok[r][0][:, hs:hs + 4].unsqueeze(2).to_broadcast([128, 4, 128]), op=ALU.mult),
                    reads=[triu_b, dta_tok[r][1]], writes=[R_b])
                sg, sg_b = seg_ps[k2]
                P.op('pe', lambda e, sg=sg, R_ap=R_ap: e.matmul(sg, lhsT=smask, rhs=R_ap, start=True, stop=True),
                     reads=[smask_b, R_b], writes=[sg_b])
                dc_ap, dc_b = dec[k2]
                P.op('act', lambda e, dc_ap=dc_ap, sg=sg: e.activation(out=dc_ap, in_=sg, func=AF.Exp),
                     reads=[sg_b], writes=[dc_b])
                gt_ap, gt_b = GT[k2]
                P.op('pool', lambda e, gt_ap=gt_ap, dc_ap=dc_ap, cm_ap=cm_ap: e.tensor_tensor(
                    out=gt_ap.rearrange("p (j l) -> p j l", l=128), in0=dc_ap.rearrange("p (j l) -> p j l", l=128),
                    in1=cm_ap.unsqueeze(1).to_broadcast([128, 4, 128]), op=ALU.mult), reads=[dc_b, cm_b], writes=[gt_b])
                eh, eh_b = eh_ps[k2]
                for j in range(4):
                    P.op('pe', lambda e, eh=eh, hs=hs, j=j, r=r: e.transpose(
                        eh[:, j * 128:(j + 1) * 128], ea_tok[r][0][:, hs + j:hs + j + 1].to_broadcast([128, 128]), ident_f),
                        reads=[ea_tok[r][1], bif], writes=[eh_b])
                ct_ap, ct_b = CTs[k2]
                P.op('dve', lambda e, ct_ap=ct_ap, eh=eh, xc3=xc3: e.tensor_tensor(
                    out=ct_ap.rearrange("p (j l) -> p j l", l=128), in0=eh.rearrange("p (j l) -> p j l", l=128),
                    in1=xc3[:, XB + 1, :].unsqueeze(1).to_broadcast([128, 4, 128]), op=ALU.mult),
                    reads=[eh_b, xc_b], writes=[ct_b])
                return dict(gt_ap=gt_ap, gt_b=gt_b, ct_ap=ct_ap, ct_b=ct_b)

            def stageB(hq, ctx):
                gt_ap, gt_b, ct_ap, ct_b = ctx['gt_ap'], ctx['gt_b'], ctx['ct_ap'], ctx['ct_b']
                h8 = (g * HPG + hq * 4) // 8
                yp, yp_b = y_ps[0]
                sp_, sp_b = s_ps[0]
                for j in range(4):
                    hh = hq * 4 + j
                    h = g * HPG + hh
                    pr = (hh % 8) // 2
                    ro = (hh % 2) * 64
                    P.op('pe', lambda e, yp=yp, xdt_ap=xdt_ap, gt_ap=gt_ap, hh=hh, ro=ro, pr=pr, j=j: e.matmul(
                        yp[ro:ro + 64, pr * 128:(pr + 1) * 128], lhsT=xdt_ap[:, hh * 64:(hh + 1) * 64],
                        rhs=gt_ap[:, j * 128:(j + 1) * 128], start=True, stop=False),
                        reads=[xdt_b, gt_b], writes=[yp_b])
                    P.op('pe', lambda e, yp=yp, h=h, ct_ap=ct_ap, ro=ro, pr=pr, j=j: e.matmul(
                        yp[ro:ro + 64, pr * 128:(pr + 1) * 128], lhsT=stb[:, h * 64:(h + 1) * 64],
                        rhs=ct_ap[:, j * 128:(j + 1) * 128], start=False, stop=True),
                        reads=[stb_bs[h8], ct_b], writes=[yp_b])
                for j in range(4):
                    hh = hq * 4 + j
                    P.op('pe', lambda e, sp_=sp_, bt_ap=bt_ap, xdtw_ap=xdtw_ap, hh=hh: e.matmul(
                        sp_[:, (hh % 8) * 64:(hh % 8 + 1) * 64], lhsT=bt_ap, rhs=xdtw_ap[:, hh * 64:(hh + 1) * 64],
                        start=True, stop=True), reads=[bt_b, xdtw_b], writes=[sp_b])
                if hq % 2 == 1:
                    b4 = (hq // 2) * 4
                    h0 = g * HPG + (hq // 2) * 8
                    P.op('dve', lambda e, y3=y3, dsx3=dsx3, yp=yp, b4=b4: e.tensor_tensor(
                        out=y3[:, b4:b4 + 4, :], in0=yp.rearrange("p (b t) -> p b t", t=128), in1=dsx3[:, b4:b4 + 4, :],
                        op=ALU.add), reads=[yp_b, dsx_b], writes=[y_b], partial=(b4 > 0))
                    st8 = st[:, h0 * 64:(h0 + 8) * 64]
                    P.op('pool', lambda e, st8=st8, h0=h0, r=r: e.tensor_tensor(
                        out=st8.rearrange("p (h c) -> p h c", c=64), in0=st8.rearrange("p (h c) -> p h c", c=64),
                        in1=eend[r][0][:, h0:h0 + 8].unsqueeze(2).to_broadcast([128, 8, 64]), op=ALU.mult),
                        reads=[st_bs[h8], eend[r][1]], writes=[st_bs[h8]])
                    P.op('dve', lambda e, st8=st8, sp_=sp_: e.tensor_tensor(out=st8, in0=sp_, in1=st8, op=ALU.add),
                         reads=[st_bs[h8], sp_b], writes=[st_bs[h8]])
                    P.op('act', lambda e, st8=st8, h0=h0: e.copy(out=stb[:, h0 * 64:(h0 + 8) * 64], in_=st8),
                         reads=[st_bs[h8]], writes=[stb_bs[h8]])

            NB4 = HPG // 4
            ctxs = {0: stageA(0)}
            yield 0
            for hq in range(NB4):
                if hq + 1 < NB4:
                    ctxs[hq + 1] = stageA(hq + 1)
                    yield 0
                stageB(hq, ctxs.pop(hq))
                yield 0
            if STOP < 5:
                return
            P.op('dve', lambda e, y_ap=y_ap, zin_ap=zin_ap: e.tensor_tensor(out=y_ap, in0=y_ap, in1=zin_ap, op=ALU.mult),
                 reads=[y_b, zin_b], writes=[y_b])
            q_ap, q_b = ysq[u]
            q3 = q_ap.rearrange("p (b t) -> p b t", t=128)
            P.op('pool', lambda e, q_ap=q_ap, y_ap=y_ap: e.tensor_tensor(out=q_ap, in0=y_ap, in1=y_ap, op=ALU.mult),
                 reads=[y_b], writes=[q_b])
            yield 0
            sp2, sp2_b = ss_ps[u]
            for bi in range(XB):
                P.op('pe', lambda e, sp2=sp2, q3=q3, bi=bi: e.matmul(sp2, lhsT=ones_b, rhs=q3[:, bi, :],
                                                                     start=(bi == 0), stop=(bi == XB - 1)),
                     reads=[ones_bb, q_b], writes=[sp2_b], partial=(bi > 0))
            rs_ap, rs_b = rs[u]
            P.op('act', lambda e, rs_ap=rs_ap, sp2=sp2: e.activation(out=rs_ap, in_=sp2, func=AF.Ln, bias=self.eps_ap,
                                                                     scale=1.0 / (XB * 128)),
                 reads=[sp2_b, self.b_eps], writes=[rs_b])
            P.op('act', lambda e, rs_ap=rs_ap: e.activation(out=rs_ap, in_=rs_ap, func=AF.Exp, scale=-0.5),
                 reads=[rs_b], writes=[rs_b])
            yield 0
            P.op('dve', lambda e, y3=y3, rs_ap=rs_ap: e.tensor_tensor(
                out=y3, in0=y3, in1=rs_ap.unsqueeze(1).to_broadcast([128, XB, 128]), op=ALU.mult),
                reads=[y_b, rs_b], writes=[y_b])
            yb_ap, yb_b = yb[u]
            yb3 = yb_ap.rearrange("p (b t) -> p b t", t=128)
            P.op('pool', lambda e, yb3=yb3, y3=y3, g=g: e.tensor_tensor(
                out=yb3, in0=y3, in1=nw[:, g * XB:(g + 1) * XB].unsqueeze(2).to_broadcast([128, XB, 128]), op=ALU.mult),
                reads=[y_b, nw_b], writes=[yb_b])
            P.op('pool', lambda e, yb3=yb3, g=g, t0=t0: e.dma_start(
                out=self.Y1[g * XB * 128:(g + 1) * XB * 128, t0:t0 + 128].rearrange("(b p) t -> p b t", p=128), in_=yb3),
                reads=[yb_b], writes=[self.b_y1], dma=yb_b, partial=True)

        nchk = min(NCH, getattr(cfg, 'ssd_chunks', NCH))
        units = [(c, g) for c in range(nchk) for g in range(G8 if STOP >= 2 else 0)]
        if not units:
            for c in range(nchk):
                dt_pipe(c)

        def run_front(gen):
            for v in gen:
                if v == 'SPLIT':
                    return True
            return False

        gens = {}
        if units:
            dt_pipe(units[0][0])
            gens[0] = unit(*units[0])
            alive = run_front(gens[0])
            for ui in range(len(units)):
                cur = gens.pop(ui)
                nxt = None
                if ui + 1 < len(units):
                    if units[ui + 1][1] == 0:
                        dt_pipe(units[ui + 1][0])
                    nxt = unit(*units[ui + 1])
                    gens[ui + 1] = nxt
                cur_done = False
                nxt_done = nxt is None
                while not (cur_done and nxt_done):
                    if not cur_done:
                        try:
                            next(cur)
                        except StopIteration:
                            cur_done = True
                    if not nxt_done:
                        try:
                            if next(nxt) == 'SPLIT':
                                nxt_done = True
                        except StopIteration:
                            nxt_done = True
        self.pop()

    def setup_eps(self):
        self.eps_ap, self.b_eps = self.alloc("eps", 1, F32)
        self.P.op('dve', lambda e: e.memset(self.eps_ap, 1e-5), writes=[self.b_eps])


def _t5_bucket(dist):
    max_exact = 16
    d_f = np.maximum(dist, 1).astype(np.float32)
    large = max_exact + (np.log(d_f / np.float32(max_exact)) / np.float32(math.log(2048 / max_exact))
                         * np.float32(32 - max_exact)).astype(np.int32)
    large = np.minimum(large, 31)
    return np.where(dist < max_exact, dist, large)


def make_onehot():
    qi = np.arange(128)[:, None]
    ki = np.arange(256)[None, :]
    delta = 128 + qi - ki
    band = (delta >= 0) & (delta <= 128)
    oh = np.zeros((3, 33, 128 * 256), np.float32)
    for g, dil in enumerate((1, 4, 16)):
        bucket = _t5_bucket(np.clip(delta, 0, None) * dil)
        for b in range(32):
            oh[g, b] = ((bucket == b) & band).reshape(-1)
        oh[g, 32] = (~band).reshape(-1)
    return oh


def build_program(cfg):
    mk = MK(cfg)
    D, S = cfg.D, cfg.S
    P = mk.P
    ident_d = mk.din("ident", [128, 128])
    xT_d = mk.din("xT", [D, S])
    x_d = mk.din("x", [S, D])
    lng_d = mk.din("ln_g", [2, D])
    lnb_d = mk.din("ln_b", [2, D])
    out_d = mk.dout("out", [S, D])
    b_out = Buf("out")
    mk.setup_consts(ident_d)
    mk.setup_eps()
    b_x = Buf("x_in")
    xTb = mk.dscr("xTb", [D, S], BF16)
    b_xTb = Buf("xTb")
    mk.cast_dram(xTb, xT_d, D, b_xTb)
    has0, has1 = (0 in cfg.layers), (1 in cfg.layers)
    if has0:
        NSUP0 = (cfg.NBLK0 + cfg.CB - 1) // cfg.CB
        W0_d = mk.din("W0", [NSUP0, 128, cfg.KC, cfg.CB * 128])
        relb_d = mk.din("relb", [32, 3 * cfg.HG])
        onehot_d = mk.din("onehot", [3, 33, 128 * 256])
        KCE0 = cfg.DATT // 128
        Wo0_d = mk.din("Wo0", [D // 512 * 128, KCE0 * 512])
        Wo0b = mk.dscr("Wo0b", [D // 512 * 128, KCE0 * 512], BF16)
        b_wo0 = Buf("Wo0b")
        mk.cast_dram(Wo0b, Wo0_d, D // 512 * 128, b_wo0, nsplit=4)
        mk.l0_bias_tables(relb_d, onehot_d)
        mk.l0_inproj(W0_d, xTb, b_xTb)
        mk.l0_attention()
        if has1:
            X1 = mk.dscr("X1", [S, D], F32)
            b_x1 = Buf("X1")
            X1T = mk.dscr("X1T", [D, S], BF16)
            b_x1T = Buf("X1T")
            mk.outproj_ln(mk.Y0, mk.b_y0, Wo0b, b_wo0, KCE0, x_d, b_x, lng_d[0], lnb_d[0],
                          X1, b_x1, X1T, b_x1T, "o0")
        else:
            mk.outproj_ln(mk.Y0, mk.b_y0, Wo0b, b_wo0, KCE0, x_d, b_x, lng_d[0], lnb_d[0],
                          out_d, b_out, None, None, "o0")
    else:
        X1, b_x1, X1T, b_x1T = x_d, b_x, xTb, b_xTb
    if has1:
        NB1 = (cfg.DI + cfg.CONV + cfg.NH) // 128
        NSUP1 = (NB1 + cfg.CB - 1) // cfg.CB
        NCONV = cfg.CONV // 128
        NXB = cfg.DI // 128
        W1_d = mk.din("W1", [NSUP1, 128, cfg.KC, cfg.CB * 128])
        convw_d = mk.din("convw", [128, NCONV * 4])
        convb_d = mk.din("convb", [128, NCONV])
        dtb_d = mk.din("dtb", [128, 1])
        alog_d = mk.din("alog", [128, 1])
        dsk_d = mk.din("dsk", [128, NXB])
        nw_d = mk.din("nw", [128, NXB])
        triu_d = mk.din("triu", [128, 128])
        smask_d = mk.din("smask", [128, 128])
        KCE1 = cfg.DI // 128
        Wo1_d = mk.din("Wo1", [D // 512 * 128, KCE1 * 512])
        Wo1b = mk.dscr("Wo1b", [D // 512 * 128, KCE1 * 512], BF16)
        b_wo1 = Buf("Wo1b")
        mk.cast_dram(Wo1b, Wo1_d, D // 512 * 128, b_wo1, nsplit=4)
        stg = getattr(cfg, 'stages', ('inproj', 'ssd', 'outproj'))
        if 'inproj' in stg:
            mk.l1_inproj(W1_d, X1T, b_x1T)
        if 'ssd' in stg:
            mk.l1_ssd(convw_d, convb_d, dtb_d, alog_d, dsk_d, nw_d, triu_d, smask_d)
        else:
            mk.Y1 = mk.dscr("Y1", [cfg.DI, S], BF16)
            mk.b_y1 = Buf("Y1")
        if 'outproj' in stg:
            mk.outproj_ln(mk.Y1, mk.b_y1, Wo1b, b_wo1, KCE1, X1, b_x1, lng_d[1], lnb_d[1],
                          out_d, b_out, None, None, "o1")
    mk.P.finalize()
    return mk


def host_wout(w_out, D):
    E = w_out.shape[0]
    KCE = E // 128
    NDB = D // 512
    return np.ascontiguousarray(w_out.reshape(KCE, 128, NDB, 512).transpose(2, 1, 0, 3).reshape(NDB * 128, KCE * 512))


def host_win(w_in, cols, KC, CB):
    NBLK = len(cols) // 128
    NSUP = (NBLK + CB - 1) // CB
    if NSUP * CB > NBLK:
        cols = np.concatenate([cols, np.tile(cols[-128:], NSUP * CB - NBLK)])
    Wp = w_in[:, cols]
    return np.ascontiguousarray(Wp.reshape(KC, 128, NSUP, CB * 128).transpose(2, 1, 0, 3))


def host_layout_l1(cfg, w_in_ssm, conv_w, conv_b, dt_bias, a_log, d_skip, norm_w, w_out_ssm):
    NCONV = cfg.CONV // 128
    NXB = cfg.DI // 128
    W1 = host_win(w_in_ssm, np.arange(w_in_ssm.shape[1]), cfg.KC, cfg.CB)
    convw = np.ascontiguousarray(conv_w.reshape(4, NCONV, 128).transpose(2, 1, 0).reshape(128, NCONV * 4))
    convb = np.ascontiguousarray(conv_b.reshape(NCONV, 128).T)
    dsk = np.ascontiguousarray(np.repeat(d_skip, cfg.P).reshape(NXB, 128).T)
    nw = np.ascontiguousarray(norm_w.reshape(NXB, 128).T)
    t = np.arange(128)
    triu = (t[:, None] <= t[None, :]).astype(np.float32)
    smask = (t[:, None] > t[None, :]).astype(np.float32)
    return {"W1": W1, "convw": convw, "convb": convb, "dtb": np.ascontiguousarray(dt_bias.reshape(128, 1)),
            "alog": np.ascontiguousarray(a_log.reshape(128, 1)), "dsk": dsk, "nw": nw, "triu": triu, "smask": smask,
            "Wo1": host_wout(w_out_ssm, cfg.D)}


def host_layout_l0(cfg, w_in_attn, w_out_attn, rel_bias):
    D, HG, KC, CB = cfg.D, cfg.HG, cfg.KC, cfg.CB
    DATT = cfg.DATT
    cols = []
    for h in range(HG):
        for g in range(3):
            for j in range(3):
                c0 = g * 3 * DATT + j * DATT + h * 128
                cols.append(np.arange(c0, c0 + 128))
        c0 = 9 * DATT + h * 128
        cols.append(np.arange(c0, c0 + 128))
    NBLK = len(cols)
    NSUP = (NBLK + CB - 1) // CB
    while len(cols) < NSUP * CB:
        cols.append(cols[-1])
    cols = np.concatenate(cols)
    Wp = w_in_attn[:, cols]
    W0 = Wp.reshape(KC, 128, NSUP, CB * 128).transpose(2, 1, 0, 3)
    Wo = host_wout(w_out_attn, D)
    hs = np.concatenate([np.arange(g * (rel_bias.shape[1] // 3), g * (rel_bias.shape[1] // 3) + HG) for g in range(3)])
    return np.ascontiguousarray(W0), Wo, np.ascontiguousarray(rel_bias[:, hs])


_CACHE = {}


def kernel(x, w_in_attn, w_out_attn, rel_bias, w_in_ssm, conv_w, conv_b, dt_bias,
           a_log, d_skip, ssm_norm_w, w_out_ssm, ln_g, ln_b):
    x = np.asarray(x, dtype=np.float32)
    B, S, D = x.shape
    cfg = Cfg(D=D, S=S)
    if 'mk' not in _CACHE:
        _CACHE['mk'] = build_program(cfg)
    mk = _CACHE['mk']
    f = lambda a: np.asarray(a, dtype=np.float32)
    W0, Wo0, rb = host_layout_l0(cfg, f(w_in_attn)[0], f(w_out_attn)[0], f(rel_bias))
    shared = {"ident": np.eye(128, dtype=np.float32), "W0": W0, "relb": rb, "onehot": make_onehot(), "Wo0": Wo0,
              "ln_g": np.ascontiguousarray(f(ln_g)), "ln_b": np.ascontiguousarray(f(ln_b))}
    shared.update(host_layout_l1(cfg, f(w_in_ssm)[0], f(conv_w)[0], f(conv_b)[0], f(dt_bias)[0], f(a_log)[0],
                                 f(d_skip)[0], f(ssm_norm_w)[0], f(w_out_ssm)[0]))
    active = [0, 2, 4, 6]
    zeros = {k: np.zeros_like(v) for k, v in shared.items()}
    zeros["x"] = np.zeros((S, D), np.float32)
    zeros["xT"] = np.zeros((D, S), np.float32)
    in_maps = []
    for c in range(8):
        if c in active:
            b = active.index(c)
            m = dict(shared)
            m["x"] = np.ascontiguousarray(x[b])
            m["xT"] = np.ascontiguousarray(x[b].T)
        else:
            m = zeros
        in_maps.append(m)
    res = run_bass_kernel_spmd(mk.nc, in_maps, core_ids=list(range(8)))
    out = np.stack([np.asarray(res.results[active[b]]["out"], dtype=np.float32) for b in range(B)], axis=0)
    return out
```

```python
from contextlib import ExitStack
import math
import numpy as np
import concourse.bass as bass
import concourse.mybir as mybir
from concourse.bass_utils import run_bass_kernel_spmd

F32 = mybir.dt.float32
BF16 = mybir.dt.bfloat16
AF = mybir.ActivationFunctionType
ALU = mybir.AluOpType
AX = mybir.AxisListType

ENGS = ('pe', 'dve', 'act', 'pool', 'sp')
SIG_CH = 12000
DMA_CH = 700


class Buf:
    __slots__ = ('name', 'writers', 'readers', 'dcount', 'dsems', 'excl')

    def __init__(self, name, excl=False):
        self.name = name
        self.excl = excl
        self.writers = []
        self.readers = []
        self.dcount = 0
        self.dsems = None


class Op:
    __slots__ = ('eng', 'emit', 'deps', 'is_dma', 'dbuf', 'didx', 'need_sig', 'sig', 'idx')


class Prog:
    def __init__(self, nc, stack):
        self.nc = nc
        self.stack = stack
        self.ops = []
        self.by_eng = {e: [] for e in ENGS}
        self.bar = {}

    def op(self, eng, emit, reads=(), writes=(), dma=None, partial=False):
        o = Op()
        o.eng = eng
        o.emit = emit
        o.is_dma = dma is not None
        o.dbuf = dma
        o.need_sig = False
        o.sig = None
        o.idx = len(self.ops)
        deps = {}
        if any(b.excl for b in reads):
            writes = list(writes) + [b for b in reads if b.excl and b not in writes]
            reads = [b for b in reads if not b.excl]
        for b in reads:
            for w in b.writers:
                deps[w] = 'w'
        for b in writes:
            for w in b.writers:
                deps[w] = 'w'
            for r in b.readers:
                if r not in deps:
                    deps[r] = 'r'
        if eng in self.bar:
            for d in self.bar.pop(eng):
                deps[d] = 'w'
        deps.pop(o, None)
        o.deps = deps
        for b in reads:
            b.readers.append(o)
        for b in writes:
            if partial and not b.excl:
                b.writers.append(o)
            else:
                b.writers = [o]
            b.readers = []
        if o.is_dma:
            dma.dcount += 1
            o.didx = dma.dcount
        self.ops.append(o)
        self.by_eng[eng].append(o)
        return o

    def barrier(self):
        deps = []
        for e in ENGS:
            for o in reversed(self.by_eng[e]):
                if not o.is_dma:
                    deps.append(o)
                    break
        lastd = {}
        for o in self.ops:
            if o.is_dma:
                lastd[id(o.dbuf)] = o
        deps.extend(lastd.values())
        for e in ENGS:
            self.bar[e] = list(deps)

    def finalize(self):
        nc = self.nc
        for o in self.ops:
            for d, kind in o.deps.items():
                if d.is_dma:
                    continue
                if d.eng == o.eng and not o.is_dma:
                    if o.eng == 'pe' or kind == 'r':
                        continue
                d.need_sig = True
        cnt = {e: 0 for e in ENGS}
        for o in self.ops:
            if not o.is_dma and o.need_sig:
                cnt[o.eng] += 1
                o.sig = (o.eng, (cnt[o.eng] - 1) // SIG_CH, (cnt[o.eng] - 1) % SIG_CH + 1)
        esems = {}
        nsem = 0
        for e in ENGS:
            n = (cnt[e] + SIG_CH - 1) // SIG_CH
            esems[e] = [self.stack.enter_context(nc.semaphore(f"s_{e}_{i}")) for i in range(n)]
            nsem += n
        seen_b = set()
        for o in self.ops:
            if o.is_dma and id(o.dbuf) not in seen_b:
                seen_b.add(id(o.dbuf))
                b = o.dbuf
                n = (b.dcount + DMA_CH - 1) // DMA_CH
                b.dsems = [self.stack.enter_context(nc.semaphore(f"d_{b.name}_{i}")) for i in range(n)]
                nsem += n
        self.n_sems = nsem

        def sig_of(d):
            if d.is_dma:
                k = d.didx - 1
                return d.dbuf.dsems[k // DMA_CH], 16 * (k % DMA_CH + 1)
            e, si, v = d.sig
            return esems[e][si], v

        engh = {'pe': 'tensor', 'dve': 'vector', 'act': 'scalar', 'pool': 'gpsimd', 'sp': 'sync'}
        block = self.stack.enter_context(nc.Block())

        def make(ename):
            ops = self.by_eng[ename]

            def body(eng):
                seen = {}
                for o in ops:
                    need = {}
                    for d, kind in o.deps.items():
                        if not d.is_dma and d.eng == o.eng and not o.is_dma:
                            if o.eng == 'pe' or kind == 'r':
                                continue
                        s, v = sig_of(d)
                        k = id(s)
                        if seen.get(k, 0) >= v:
                            continue
                        if k not in need or need[k][1] < v:
                            need[k] = (s, v)
                    for k, (s, v) in need.items():
                        eng.wait_ge(s, v)
                        seen[k] = v
                    ins = o.emit(eng)
                    if o.is_dma:
                        s, v = sig_of(o)
                        ins.then_inc(s, 16)
                    elif o.sig is not None:
                        s, v = sig_of(o)
                        ins.then_inc(s, 1)
                last = {}
                for o in ops:
                    if o.is_dma:
                        s, v = sig_of(o)
                        last[id(s)] = (s, max(v, last.get(id(s), (s, 0))[1]))
                for k, (s, v) in last.items():
                    if seen.get(k, 0) < v:
                        eng.wait_ge(s, v)
            return body

        for e in ENGS:
            if self.by_eng[e]:
                getattr(block, engh[e])(make(e))


class Cfg:
    def __init__(self, D=4096, S=4096, HG=16, G8=8, HPG=16, debug=False, layers=(0, 1)):
        self.D = D
        self.S = S
        self.KC = D // 128
        self.HG = HG
        self.DATT = HG * 128
        self.NBLK0 = HG * 10
        self.patterns = ((128, 1), (512, 4), (2048, 16))
        self.G8 = G8
        self.HPG = HPG
        self.P = 64
        self.N = 128
        self.NH = G8 * HPG
        self.DI = self.NH * self.P
        self.CONV = self.DI + 2 * G8 * self.N
        self.debug = debug
        self.layers = layers
        self.alpha = (2 * 2) ** 0.25
        self.CB = 4 if D < 4096 else 8


ARENA_WORDS = 51 * 1024


class MK:
    def __init__(self, cfg):
        self.cfg = cfg
        self.nc = bass.Bass("TRN2", target_bir_lowering=False)
        self.stack = ExitStack()
        self.P = Prog(self.nc, self.stack)
        self.arena = self.stack.enter_context(self.nc.sbuf_tensor("arena", [128, ARENA_WORDS], F32))
        self.top = 0
        self.marks = []
        self.banks = []
        for i in range(8):
            t = self.stack.enter_context(self.nc.psum_tensor(f"bank{i}", [128, 512], F32))
            self.banks.append((t, Buf(f"bank{i}", excl=True)))
        self.dram = {}
        self.scr_kind = "ExternalOutput" if cfg.debug else "Internal"

    def alloc(self, name, cols, dt=F32):
        words = cols if dt == F32 else (cols + 1) // 2
        words = (words + 7) // 8 * 8
        a = self.top
        self.top += words
        assert self.top <= ARENA_WORDS, f"SBUF arena overflow at {name}: {self.top}"
        ap = self.arena[:, a:a + words]
        if dt != F32:
            ap = ap.bitcast(dt)[:, :cols]
        else:
            ap = ap[:, :cols]
        return ap, Buf(name)

    def push(self):
        self.marks.append(self.top)

    def pop(self):
        self.P.barrier()
        self.top = self.marks.pop()

    def din(self, name, shape, dt=F32):
        t = self.nc.dram_tensor(name, list(shape), dt, kind="ExternalInput")
        self.dram[name] = t
        return t.ap()

    def dout(self, name, shape, dt=F32):
        t = self.nc.dram_tensor(name, list(shape), dt, kind="ExternalOutput")
        self.dram[name] = t
        return t.ap()

    def dscr(self, name, shape, dt=F32):
        t = self.nc.dram_tensor(name, list(shape), dt, kind=self.scr_kind)
        self.dram[name] = t
        return t.ap()

    def bank(self, i, dt=F32):
        t, b = self.banks[i]
        ap = t[:]
        if dt != F32:
            ap = ap.bitcast(dt)
        return ap, b

    def setup_consts(self, ident_d):
        P = self.P
        self.ident_f, b1 = self.alloc("ident_f", 128, F32)
        self.ident_b, b2 = self.alloc("ident_b", 128, BF16)
        self.ones_f, b3 = self.alloc("ones_f", 128, F32)
        self.b_ident_f, self.b_ident_b, self.b_ones = b1, b2, b3
        P.op('sp', lambda e: e.dma_start(out=self.ident_f, in_=ident_d), writes=[b1], dma=b1)
        P.op('pool', lambda e: e.dma_start(out=self.ident_b, in_=ident_d), writes=[b2], dma=b2)
        P.op('dve', lambda e: e.memset(self.ones_f, 1.0), writes=[b3])

    def cast_dram(self, dst, src, rows, bufd, nsplit=8):
        P = self.P
        cols = src.shape[1]
        c = min(cols, 4096)
        a = cols // c
        if a > 1:
            src = src.rearrange("r (a c) -> (r a) c", c=c)
            dst = dst.rearrange("r (a c) -> (r a) c", c=c)
        R = rows * a
        step = min(R, 256)
        first = True
        for i in range(0, R, step):
            P.op('pool', lambda e, i=i: e.dma_start(out=dst[i:i + step, :], in_=src[i:i + step, :]),
                 writes=[bufd], dma=bufd, partial=not first)
            first = False

    def gemm_fm(self, W_d, xT_d, b_xT, KC, T, NBLK, CB, epilogue, tag):
        P = self.P
        self.push()
        NSUP = (NBLK + CB - 1) // CB
        TB = T // 512
        wsb = [self.alloc(f"{tag}_w{i}", KC * CB * 128, BF16) for i in range(2)]
        xsb = [self.alloc(f"{tag}_x{i}", KC * 512, BF16) for i in range(2)]
        steps = [(s, tb) for s in range(NSUP) for tb in range(TB)]

        def load_w(s):
            w_ap, w_b = wsb[s % 2]
            w3 = w_ap.rearrange("p (k c) -> p k c", k=KC)
            nq = 4 if KC >= 4 else 1
            kq = KC // nq
            for q in range(nq):
                P.op('pool', lambda e, s=s, q=q, w3=w3: e.dma_start(
                    out=w3[:, q * kq:(q + 1) * kq, :], in_=W_d[s, :, q * kq:(q + 1) * kq, :]),
                    writes=[w_b], dma=w_b, partial=(q > 0))

        def load_x(i):
            s, tb = steps[i]
            x_ap, x_b = xsb[i % 2]
            x3 = x_ap.rearrange("p (k t) -> p k t", k=KC)
            P.op('sp', lambda e, x3=x3, tb=tb: e.dma_start(
                out=x3, in_=xT_d[:, tb * 512:(tb + 1) * 512].rearrange("(k p) t -> p k t", p=128)),
                reads=[b_xT], writes=[x_b], dma=x_b)

        load_w(0)
        load_x(0)
        bcnt = 0
        for i, (s, tb) in enumerate(steps):
            if tb == 0 and s + 1 < NSUP:
                load_w(s + 1)
            if i + 1 < len(steps):
                load_x(i + 1)
            w_ap, w_b = wsb[s % 2]
            w3 = w_ap.rearrange("p (k c) -> p k c", k=KC)
            x_ap, x_b = xsb[i % 2]
            x3 = x_ap.rearrange("p (k t) -> p k t", k=KC)
            ncb = min(CB, NBLK - s * CB)
            for cb in range(ncb):
                ps, ps_b = self.bank(bcnt % 4)
                bcnt += 1
                for k in range(KC):
                    P.op('pe', lambda e, ps=ps, w3=w3, x3=x3, k=k, cb=cb: e.matmul(
                        ps, lhsT=w3[:, k, cb * 128:(cb + 1) * 128], rhs=x3[:, k, :],
                        start=(k == 0), stop=(k == KC - 1)),
                        reads=[w_b, x_b], writes=[ps_b], partial=(k > 0))
                epilogue(s * CB + cb, tb, ps, ps_b)
        self.pop()

    def l0_inproj(self, W_d, xT_d, b_xT):
        cfg, P = self.cfg, self.P
        self.QKVG = self.dscr("QKVG", [cfg.NBLK0, 128, cfg.S], BF16)
        self.b_qkvg = [Buf(f"qkvg{i}") for i in range(cfg.NBLK0)]
        self.push()
        stg = [self.alloc(f"l0stg{i}", 512, BF16) for i in range(4)]
        st = {'c': 0}

        def epi(blk, tb, ps, ps_b):
            s_ap, s_b = stg[st['c'] % 4]
            eng = 'act' if st['c'] % 2 == 0 else 'dve'
            st['c'] += 1
            jj = blk % 10
            d = 1 if jj == 9 else cfg.patterns[jj // 3][1]
            n = 512 // d
            o_v = s_ap.rearrange("p (r j) -> p j r", r=d) if d > 1 else s_ap
            i_v = ps.rearrange("p (j r) -> p j r", r=d) if d > 1 else ps
            if eng == 'act':
                P.op('act', lambda e: e.copy(out=o_v, in_=i_v), reads=[ps_b], writes=[s_b])
            else:
                P.op('dve', lambda e: e.tensor_copy(out=o_v, in_=i_v), reads=[ps_b], writes=[s_b])
            if d > 1:
                dst = self.QKVG[blk].rearrange("p (r j) -> p r j", r=d)[:, :, tb * n:(tb + 1) * n]
                src = s_ap.rearrange("p (r j) -> p r j", r=d)
            else:
                dst = self.QKVG[blk, :, tb * 512:(tb + 1) * 512]
                src = s_ap
            P.op('act', lambda e: e.dma_start(out=dst, in_=src),
                 reads=[s_b], writes=[self.b_qkvg[blk]], dma=s_b, partial=True)

        self.gemm_fm(W_d, xT_d, b_xT, cfg.KC, cfg.S, cfg.NBLK0, cfg.CB, epi, "g0")
        self.pop()

    def l0_bias_tables(self, relb_d, onehot_d):
        cfg, P = self.cfg, self.P
        HG = cfg.HG
        self.BT = self.dscr("BT", [3, HG, 128 * 256], F32)
        self.b_bt = Buf("BT")
        self.push()
        rb, rb_b = self.alloc("rb", 3 * HG, F32)
        P.op('dve', lambda e: e.memset(rb[0:64, :], -1e30), writes=[rb_b])
        P.op('sp', lambda e: e.dma_start(out=rb[0:32, :], in_=relb_d), writes=[rb_b], dma=rb_b)
        oh = [self.alloc(f"oh{i}", 2048, F32) for i in range(2)]
        stg = [self.alloc(f"btst{i}", 2048, F32) for i in range(2)]
        c = 0
        for g in range(3):
            for ch in range(16):
                o_ap, o_b = oh[c % 2]
                s_ap, s_b = stg[c % 2]
                c += 1
                P.op('sp', lambda e, o_ap=o_ap, g=g, ch=ch: e.dma_start(
                    out=o_ap[0:33, :], in_=onehot_d[g, :, ch * 2048:(ch + 1) * 2048]),
                    writes=[o_b], dma=o_b)
                for j in range(4):
                    ps, ps_b = self.bank(j)
                    P.op('pe', lambda e, ps=ps, o_ap=o_ap, g=g, j=j: e.matmul(
                        ps[0:HG, :], lhsT=rb[0:33, g * HG:(g + 1) * HG], rhs=o_ap[0:33, j * 512:(j + 1) * 512],
                        start=True, stop=True), reads=[rb_b, o_b], writes=[ps_b])
                    P.op('act' if j % 2 == 0 else 'dve',
                         (lambda e, ps=ps, s_ap=s_ap, j=j: e.copy(out=s_ap[0:HG, j * 512:(j + 1) * 512], in_=ps[0:HG, :]))
                         if j % 2 == 0 else
                         (lambda e, ps=ps, s_ap=s_ap, j=j: e.tensor_copy(out=s_ap[0:HG, j * 512:(j + 1) * 512], in_=ps[0:HG, :])),
                         reads=[ps_b], writes=[s_b], partial=(j > 0))
                P.op('sp', lambda e, s_ap=s_ap, g=g, ch=ch: e.dma_start(
                    out=self.BT[g, :, ch * 2048:(ch + 1) * 2048], in_=s_ap[0:HG, :]),
                    reads=[s_b], writes=[self.b_bt], dma=s_b, partial=True)
        self.pop()

    def l0_attention(self):
        cfg, P = self.cfg, self.P
        S, HG = cfg.S, cfg.HG
        H2 = S // 2
        NBK = S // 128
        scale = 128 ** -0.5
        self.Y0 = self.dscr("Y0", [cfg.DATT, S], BF16)
        self.b_y0 = Buf("Y0")
        self.push()
        QT = [self.alloc(f"QT{g}", S, BF16) for g in range(3)]
        KT = [self.alloc(f"KT{g}", S + 128, BF16) for g in range(3)]
        VTd = self.alloc("VTd", S, BF16)
        ld = []
        Vtok = [self.alloc(f"Vtok{g}", S + 128, BF16) for g in range(3)]
        BM = [self.alloc(f"BM{g}", 256, F32) for g in range(3)]
        BM0 = [self.alloc(f"BM0{g}", 256, F32) for g in range(3)]
        OT = [self.alloc(f"OT{g}", H2, F32) for g in range(3)]
        LSE = [self.alloc(f"LSE{g}", H2, F32) for g in range(3)]
        gT = self.alloc("gT", S, BF16)
        yT = self.alloc("yT", S, BF16)
        NR = 3
        Sb = [self.alloc(f"Sb{i}", 1024, F32) for i in range(NR)]
        Pf = [self.alloc(f"Pf{i}", 1024, F32) for i in range(NR)]
        Pn = [self.alloc(f"Pn{i}", 1024, BF16) for i in range(NR)]
        PTs = [self.alloc(f"PTs{i}", 1024, BF16) for i in range(NR)]
        Bl = [self.alloc(f"Bl{i}", 512, F32) for i in range(NR)]
        negm = [self.alloc(f"negm{i}", 4, F32) for i in range(NR)]
        den = [self.alloc(f"den{i}", 4, F32) for i in range(NR)]
        rden = [self.alloc(f"rden{i}", 4, F32) for i in range(NR)]
        lnd = [self.alloc(f"lnd{i}", 4, F32) for i in range(NR)]
        lse = [self.alloc(f"lse{i}", 4, F32) for i in range(NR)]
        ones4, ones4_b = self.alloc("ones4", 512, F32)
        CW = 1024
        tmp = [(Sb[i][0][:, :CW], Sb[i][1]) for i in range(3)] + [(Pf[i][0][:, :CW], Pf[i][1]) for i in range(3)]
        SpsA = [self.bank(0), self.bank(2)]
        SpsB = [self.bank(1), self.bank(3)]
        PTps = self.bank(4, BF16)
        oTps = self.bank(5)
        LBps = self.bank(6)
        VTps = self.bank(7, BF16)
        ident_f, ident_b, ones_f = self.ident_f, self.ident_b, self.ones_f
        bif, bib, bon = self.b_ident_f, self.b_ident_b, self.b_ones
        ldc = {'c': 0}
        P.op('dve', lambda e: e.memset(ones4, 1.0), writes=[ones4_b])
        for g in range(3):
            P.op('pool', lambda e, g=g: e.memset(KT[g][0][:, 0:128], 0.0), writes=[KT[g][1]])
            P.op('pool', lambda e, g=g: e.memset(Vtok[g][0][:, 0:128], 0.0), writes=[Vtok[g][1]])

        def load_fm(blk, g, dst, dst_b):
            src = self.QKVG[blk]
            if True:
                P.op('sp', lambda e: e.dma_start(out=dst, in_=src), reads=[self.b_qkvg[blk]], writes=[dst_b], dma=dst_b,
                     partial=True)
                return
            d = cfg.patterns[g][1]
            l_ap, l_b = ld[ldc['c'] % len(ld)]
            ldc['c'] += 1
            P.op('sp', lambda e: e.dma_start(out=l_ap, in_=src), reads=[self.b_qkvg[blk]], writes=[l_b], dma=l_b)
            L = S // d
            P.op('pool', lambda e: e.tensor_copy(out=dst.rearrange("p (r j) -> p j r", j=L),
                                                 in_=l_ap.rearrange("p (j r) -> p j r", r=d)),
                 reads=[l_b], writes=[dst_b], partial=True)

        bcnt = {'c': 0}

        def batch(g, blocks, half, first_of_group):
            d = cfg.patterns[g][1]
            L = S // d
            nb = L // 128
            hb = nb // 2
            s = bcnt['c'] % NR
            bcnt['c'] += 1
            q_ap, q_b = QT[g]
            k_ap, k_b = KT[g]
            v_ap, v_b = Vtok[g]
            o_ap, o_b = OT[g]
            l_ap, l_b = LSE[g]
            o3 = o_ap.rearrange("p (j r) -> p j r", r=d)
            l3 = l_ap.rearrange("p (j r) -> p j r", r=d)
            sA, sA_b = SpsA[(bcnt['c'] - 1) % 2]
            sB, sB_b = SpsB[(bcnt['c'] - 1) % 2]
            sb, sb_b = Sb[s]
            pf, pf_b = Pf[s]
            pn, pn_b = Pn[s]
            pts, pts_b = PTs[s]
            bl, bl_b = Bl[s]
            nm, nm_b = negm[s]
            dn, dn_b = den[s]
            rd, rd_b = rden[s]
            ln_, ln_b = lnd[s]
            ls, ls_b = lse[s]
            for j, (r, jb) in enumerate(blocks):
                c0 = r * L + jb * 128
                sp_, sp_b = (sA, sA_b) if j < 2 else (sB, sB_b)
                col = (j % 2) * 256
                P.op('pe', lambda e, sp_=sp_, col=col, c0=c0: e.matmul(
                    sp_[:, col:col + 256], lhsT=q_ap[:, c0:c0 + 128], rhs=k_ap[:, c0:c0 + 256], start=True, stop=True),
                    reads=[q_b, k_b], writes=[sp_b])
            yield 0
            for j, (r, jb) in enumerate(blocks):
                sp_, sp_b = (sA, sA_b) if j < 2 else (sB, sB_b)
                col = (j % 2) * 256
                bm_ap, bm_b = BM[g] if jb > 0 else BM0[g]
                P.op('dve', lambda e, sp_=sp_, col=col, j=j, bm_ap=bm_ap: e.scalar_tensor_tensor(
                    out=sb[:, j * 256:(j + 1) * 256], in0=sp_[:, col:col + 256], scalar=scale, in1=bm_ap,
                    op0=ALU.mult, op1=ALU.add), reads=[sp_b, bm_b], writes=[sb_b], partial=(j > 0))
            P.op('dve', lambda e: e.tensor_reduce(out=nm, in_=sb.rearrange("p (b k) -> p b k", k=256), axis=AX.X, op=ALU.max,
                                                  negate=True), reads=[sb_b], writes=[nm_b])
            yield 0
            for j in range(4):
                P.op('act', lambda e, j=j: e.activation(out=pf[:, j * 256:(j + 1) * 256], in_=sb[:, j * 256:(j + 1) * 256],
                                                        func=AF.Exp, bias=nm[:, j:j + 1], scale=1.0, accum_out=dn[:, j:j + 1]),
                     reads=[sb_b, nm_b], writes=[pf_b, dn_b], partial=(j > 0))
            yield 0
            P.op('dve', lambda e: e.reciprocal(out=rd, in_=dn), reads=[dn_b], writes=[rd_b])
            P.op('act', lambda e: e.activation(out=ln_, in_=dn, func=AF.Ln), reads=[dn_b], writes=[ln_b])
            P.op('pool', lambda e: e.tensor_tensor(out=pn.rearrange("p (b k) -> p b k", k=256),
                                                   in0=pf.rearrange("p (b k) -> p b k", k=256),
                                                   in1=rd.unsqueeze(2).to_broadcast([128, 4, 256]), op=ALU.mult),
                 reads=[pf_b, rd_b], writes=[pn_b])
            P.op('dve', lambda e: e.tensor_tensor(out=ls, in0=ln_, in1=nm, op=ALU.subtract), reads=[ln_b, nm_b], writes=[ls_b])
            yield 0
            ptp, ptp_b = PTps
            for c in range(8):
                P.op('pe', lambda e, c=c: e.transpose(ptp[:, c * 128:(c + 1) * 128], pn[:, c * 128:(c + 1) * 128], ident_b),
                     reads=[pn_b, bib], writes=[ptp_b])
            P.op('act', lambda e: e.copy(out=pts, in_=ptp[:, 0:1024]), reads=[ptp_b], writes=[pts_b])
            yield 0
            otp, otp_b = oTps
            for j, (r, jb) in enumerate(blocks):
                kb = (r * L + jb * 128) // 128
                for c in range(2):
                    P.op('pe', lambda e, j=j, c=c, kb=kb: e.matmul(
                        otp[:, j * 128:(j + 1) * 128], lhsT=v_ap[:, (kb + c) * 128:(kb + c + 1) * 128],
                        rhs=pts[:, (2 * j + c) * 128:(2 * j + c + 1) * 128], start=(c == 0), stop=(c == 1)),
                        reads=[v_b, pts_b], writes=[otp_b])
            r0, jb0 = blocks[0]
            nr = len(set(r for r, _ in blocks))
            nbj = 4 // nr
            jl0 = jb0 - half * hb

            def dest(t3):
                return t3[:, jl0 * 128:(jl0 + nbj) * 128, r0:r0 + nr].rearrange("p (b j) r -> p r b j", j=128)
            P.op('act', lambda e: e.copy(out=dest(o3), in_=otp.rearrange("p (r b j) -> p r b j", r=nr, b=nbj)),
                 reads=[otp_b], writes=[o_b], partial=(not first_of_group))
            yield 0
            P.op('dve', lambda e: e.tensor_tensor(out=bl.rearrange("p (b j) -> p b j", j=128),
                                                  in0=ones4.rearrange("p (b j) -> p b j", j=128),
                                                  in1=ls.unsqueeze(2).to_broadcast([128, 4, 128]), op=ALU.mult),
                 reads=[ls_b, ones4_b], writes=[bl_b])
            lbp, lbp_b = LBps
            for j in range(4):
                P.op('pe', lambda e, j=j: e.transpose(lbp[:, j * 128:(j + 1) * 128], bl[:, j * 128:(j + 1) * 128], ident_f),
                     reads=[bl_b, bif], writes=[lbp_b])
            P.op('dve', lambda e: e.tensor_copy(out=dest(l3), in_=lbp.rearrange("p (r b j) -> p r b j", r=nr, b=nbj)),
                 reads=[lbp_b], writes=[l_b], partial=(not first_of_group))
            yield 0

        def run_batches(gens):
            active = []
            todo = list(gens)
            while todo or active:
                if todo and len(active) < 3:
                    active.append(todo.pop(0))
                for gen in list(active):
                    try:
                        next(gen)
                    except StopIteration:
                        active.remove(gen)

        for h in range(HG):
            base = h * 10
            for g in range(3):
                load_fm(base + 3 * g + 0, g, QT[g][0], QT[g][1])
                load_fm(base + 3 * g + 1, g, KT[g][0][:, 128:S + 128], KT[g][1])
                load_fm(base + 3 * g + 2, g, VTd[0], VTd[1])
                v_ap, v_b = Vtok[g]
                vps, vps_b = VTps
                for q4 in range(S // 512):
                    for j in range(4):
                        blk = q4 * 4 + j
                        P.op('pe', lambda e, blk=blk, j=j: e.transpose(
                            vps[:, j * 128:(j + 1) * 128], VTd[0][:, blk * 128:(blk + 1) * 128], ident_b),
                            reads=[VTd[1], bib], writes=[vps_b])
                    P.op('act', lambda e, q4=q4, v_ap=v_ap: e.copy(out=v_ap[:, 128 + q4 * 512:128 + (q4 + 1) * 512], in_=vps[:, 0:512]),
                         reads=[vps_b], writes=[v_b], partial=True)
                bm_ap, bm_b = BM[g]
                P.op('sp', lambda e, g=g, h=h, bm_ap=bm_ap: e.dma_start(
                    out=bm_ap, in_=self.BT[g, h, :].rearrange("(q k) -> q k", k=256)),
                    reads=[self.b_bt], writes=[bm_b], dma=bm_b)
                b0_ap, b0_b = BM0[g]
                P.op('pool', lambda e, b0_ap=b0_ap: e.memset(b0_ap[:, 0:128], -1e30), writes=[b0_b])
                P.op('pool', lambda e, b0_ap=b0_ap, bm_ap=bm_ap: e.tensor_copy(out=b0_ap[:, 128:256], in_=bm_ap[:, 128:256]),
                     reads=[bm_b], writes=[b0_b], partial=True)
            P.op('sp', lambda e, base=base: e.dma_start(out=gT[0], in_=self.QKVG[base + 9]),
                 reads=[self.b_qkvg[base + 9]], writes=[gT[1]], dma=gT[1])

            for half in range(2):
                gens = []
                for g in range(3):
                    d = cfg.patterns[g][1]
                    nb = (S // d) // 128
                    hb = nb // 2
                    blks = [(r, jb) for r in range(d) for jb in range(half * hb, (half + 1) * hb)]
                    for i in range(0, len(blks), 4):
                        gens.append(batch(g, blks[i:i + 4], half, i == 0))
                run_batches(gens)
                for cc in range(H2 // CW):
                    sl = slice(cc * CW, (cc + 1) * CW)
                    gsl = slice(half * H2 + cc * CW, half * H2 + (cc + 1) * CW)
                    (mx, mx_b), (e0, e0_b), (e1, e1_b), (e2, e2_b), (zz, zz_b), (acc, acc_b) = tmp
                    ee = [(e0, e0_b), (e1, e1_b), (e2, e2_b)]
                    P.op('dve', lambda e, sl=sl: e.tensor_tensor(out=mx, in0=LSE[0][0][:, sl], in1=LSE[1][0][:, sl], op=ALU.max),
                         reads=[LSE[0][1], LSE[1][1]], writes=[mx_b])
                    P.op('dve', lambda e, sl=sl: e.tensor_tensor(out=mx, in0=mx, in1=LSE[2][0][:, sl], op=ALU.max),
                         reads=[LSE[2][1], mx_b], writes=[mx_b])
                    for g in range(3):
                        ea, ea_b = ee[g]
                        P.op('pool', lambda e, ea=ea, g=g, sl=sl: e.tensor_tensor(out=ea, in0=LSE[g][0][:, sl], in1=mx, op=ALU.subtract),
                             reads=[LSE[g][1], mx_b], writes=[ea_b])
                        P.op('act', lambda e, ea=ea: e.activation(out=ea, in_=ea, func=AF.Exp), reads=[ea_b], writes=[ea_b])
                    P.op('dve', lambda e: e.tensor_tensor(out=zz, in0=e0, in1=e1, op=ALU.add), reads=[e0_b, e1_b], writes=[zz_b])
                    P.op('dve', lambda e: e.tensor_tensor(out=zz, in0=zz, in1=e2, op=ALU.add), reads=[zz_b, e2_b], writes=[zz_b])
                    P.op('dve', lambda e: e.reciprocal(out=zz, in_=zz), reads=[zz_b], writes=[zz_b])
                    for g in range(3):
                        ea, ea_b = ee[g]
                        P.op('dve', lambda e, ea=ea, g=g, sl=sl: e.tensor_tensor(out=ea, in0=ea, in1=OT[g][0][:, sl], op=ALU.mult),
                             reads=[ea_b, OT[g][1]], writes=[ea_b])
                    P.op('dve', lambda e: e.tensor_tensor(out=acc, in0=e0, in1=e1, op=ALU.add), reads=[e0_b, e1_b], writes=[acc_b])
                    P.op('dve', lambda e: e.tensor_tensor(out=acc, in0=acc, in1=e2, op=ALU.add), reads=[acc_b, e2_b], writes=[acc_b])
                    P.op('dve', lambda e: e.tensor_tensor(out=acc, in0=acc, in1=zz, op=ALU.mult), reads=[acc_b, zz_b], writes=[acc_b])
                    P.op('act', lambda e, gsl=gsl: e.activation(out=e0, in_=gT[0][:, gsl], func=AF.Exp, scale=-1.0),
                         reads=[gT[1]], writes=[e0_b])
                    P.op('pool', lambda e: e.tensor_scalar(out=e0, in0=e0, scalar1=1.0, scalar2=1.0, op0=ALU.add, op1=ALU.mult),
                         reads=[e0_b], writes=[e0_b])
                    P.op('dve', lambda e: e.reciprocal(out=e0, in_=e0), reads=[e0_b], writes=[e0_b])
                    P.op('dve', lambda e, gsl=gsl: e.tensor_tensor(out=e0, in0=e0, in1=gT[0][:, gsl], op=ALU.mult),
                         reads=[e0_b, gT[1]], writes=[e0_b])
                    P.op('dve', lambda e, gsl=gsl: e.tensor_tensor(out=yT[0][:, gsl], in0=acc, in1=e0, op=ALU.mult),
                         reads=[acc_b, e0_b], writes=[yT[1]], partial=True)
            P.op('act', lambda e, h=h: e.dma_start(out=self.Y0[h * 128:(h + 1) * 128, :], in_=yT[0]),
                 reads=[yT[1]], writes=[self.b_y0], dma=yT[1], partial=True)
        self.pop()

    def outproj_ln(self, Y_d, b_y, W_d, b_w, KCE, x_d, b_x, g_d, bt_d, out_d, b_out, outT_d, b_outT, tag):
        cfg, P = self.cfg, self.P
        S, D = cfg.S, cfg.D
        NDB = D // 512
        KCD = D // 128
        KG = 16 if KCE >= 16 else KCE
        NKG = KCE // KG
        TS = 256
        NSUB = TS // 128
        self.push()
        Gr, Gr_b = self.alloc(f"{tag}G", D, F32)
        Br, Br_b = self.alloc(f"{tag}B", D, F32)
        P.op('sp', lambda e: e.dma_start(out=Gr, in_=g_d.partition_broadcast(128)), writes=[Gr_b], dma=Gr_b)
        P.op('sp', lambda e: e.dma_start(out=Br, in_=bt_d.partition_broadcast(128)), writes=[Br_b], dma=Br_b)
        ysb = [self.alloc(f"{tag}y{i}", KCE * TS, BF16) for i in range(2)]
        wsb = [self.alloc(f"{tag}w{i}", KG * 512, BF16) for i in range(3)]
        xr = [self.alloc(f"{tag}xr{i}", 512, F32) for i in range(4)]
        v = [self.alloc(f"{tag}v{i}", D, F32) for i in range(NSUB)]
        stats = [self.alloc(f"{tag}st{i}", 6 * NDB, F32) for i in range(NSUB)]
        mv = [self.alloc(f"{tag}mv{i}", 2, F32) for i in range(NSUB)]
        rstd = [self.alloc(f"{tag}rs{i}", 1, F32) for i in range(NSUB)]
        if outT_d is not None:
            xb = [self.alloc(f"{tag}xb{i}", D, BF16) for i in range(NSUB)]
            xts = [self.alloc(f"{tag}xt{i}", KCD * TS, BF16) for i in range(2)]
        wc = 0
        xc = 0
        for ts in range(S // TS):
            y_ap, y_b = ysb[ts % 2]
            y3 = y_ap.rearrange("p (k t) -> p k t", k=KCE)
            P.op('sp', lambda e, y3=y3, ts=ts: e.dma_start(
                out=y3, in_=Y_d[:, ts * TS:(ts + 1) * TS].rearrange("(k p) t -> p k t", p=128)),
                reads=[b_y], writes=[y_b], dma=y_b)
            for db in range(NDB):
                for kg in range(NKG):
                    w_ap, w_b = wsb[wc % 3]
                    wc += 1
                    w3 = w_ap.rearrange("p (k c) -> p k c", k=KG)
                    P.op('sp', lambda e, w3=w3, db=db, kg=kg: e.dma_start(
                        out=w3, in_=W_d[db * 128:(db + 1) * 128, kg * KG * 512:(kg + 1) * KG * 512].rearrange("p (k c) -> p k c", k=KG)), reads=[b_w], writes=[w_b], dma=w_b)
                    for k in range(KG):
                        ka = kg * KG + k
                        for sub in range(NSUB):
                            ps, ps_b = self.bank(sub + 2 * (db % 2))
                            P.op('pe', lambda e, ps=ps, y3=y3, w3=w3, ka=ka, k=k, sub=sub: e.matmul(
                                ps, lhsT=y3[:, ka, sub * 128:(sub + 1) * 128], rhs=w3[:, k, :],
                                start=(ka == 0), stop=(ka == KCE - 1)),
                                reads=[y_b, w_b], writes=[ps_b], partial=(ka > 0))
                for sub in range(NSUB):
                    ps, ps_b = self.bank(sub + 2 * (db % 2))
                    x_ap, x_b = xr[xc % 4]
                    xc += 1
                    r0 = ts * TS + sub * 128
                    P.op('sp', lambda e, x_ap=x_ap, r0=r0, db=db: e.dma_start(
                        out=x_ap, in_=x_d[r0:r0 + 128, db * 512:(db + 1) * 512]), reads=[b_x], writes=[x_b], dma=x_b)
                    v_ap, v_b = v[sub]
                    P.op('dve', lambda e, v_ap=v_ap, x_ap=x_ap, ps=ps, db=db: e.scalar_tensor_tensor(
                        out=v_ap[:, db * 512:(db + 1) * 512], in0=x_ap, scalar=cfg.alpha, in1=ps,
                        op0=ALU.mult, op1=ALU.add), reads=[x_b, ps_b], writes=[v_b], partial=(db > 0))
                    s_ap, s_b = stats[sub]
                    P.op('dve', lambda e, s_ap=s_ap, v_ap=v_ap, db=db: e.bn_stats(
                        out=s_ap[:, db * 6:(db + 1) * 6], in_=v_ap[:, db * 512:(db + 1) * 512]),
                        reads=[v_b], writes=[s_b], partial=(db > 0))
            for sub in range(NSUB):
                v_ap, v_b = v[sub]
                s_ap, s_b = stats[sub]
                m_ap, m_b = mv[sub]
                r_ap, r_b = rstd[sub]
                P.op('dve', lambda e, m_ap=m_ap, s_ap=s_ap: e.bn_aggr(out=m_ap, in_=s_ap), reads=[s_b], writes=[m_b])
                P.op('act', lambda e, r_ap=r_ap, m_ap=m_ap: e.activation(out=r_ap, in_=m_ap[:, 1:2], func=AF.Ln, bias=self.eps_ap, scale=1.0),
                     reads=[m_b, self.b_eps], writes=[r_b])
                P.op('act', lambda e, r_ap=r_ap: e.activation(out=r_ap, in_=r_ap, func=AF.Exp, scale=-0.5),
                     reads=[r_b], writes=[r_b])
                P.op('dve', lambda e, v_ap=v_ap, m_ap=m_ap: e.scalar_tensor_tensor(
                    out=v_ap, in0=v_ap, scalar=m_ap[:, 0:1], in1=Gr, op0=ALU.subtract, op1=ALU.mult),
                    reads=[v_b, m_b, Gr_b], writes=[v_b])
                P.op('dve', lambda e, v_ap=v_ap, r_ap=r_ap: e.scalar_tensor_tensor(
                    out=v_ap, in0=v_ap, scalar=r_ap, in1=Br, op0=ALU.mult, op1=ALU.add),
                    reads=[v_b, r_b, Br_b], writes=[v_b])
                r0 = ts * TS + sub * 128
                P.op('pool', lambda e, v_ap=v_ap, r0=r0: e.dma_start(out=out_d[r0:r0 + 128, :], in_=v_ap),
                     reads=[v_b], writes=[b_out], dma=v_b, partial=True)
                if outT_d is not None:
                    xb_ap, xb_b = xb[sub]
                    P.op('act', lambda e, xb_ap=xb_ap, v_ap=v_ap: e.copy(out=xb_ap, in_=v_ap), reads=[v_b], writes=[xb_b])
                    xt_ap, xt_b = xts[ts % 2]
                    xt3 = xt_ap.rearrange("p (k t) -> p k t", k=KCD)
                    for q4 in range(KCD // 4):
                        tp, tp_b = self.bank(5 + (q4 % 2), BF16)
                        for j in range(4):
                            kk = q4 * 4 + j
                            P.op('pe', lambda e, tp=tp, xb_ap=xb_ap, kk=kk, j=j: e.transpose(
                                tp[:, j * 128:(j + 1) * 128], xb_ap[:, kk * 128:(kk + 1) * 128], self.ident_b),
                                reads=[xb_b, self.b_ident_b], writes=[tp_b], partial=(j > 0))
                        P.op('act', lambda e, tp=tp, xt3=xt3, q4=q4, sub=sub: e.copy(
                            out=xt3[:, q4 * 4:(q4 + 1) * 4, sub * 128:(sub + 1) * 128],
                            in_=tp[:, 0:512].rearrange("p (k t) -> p k t", k=4)),
                            reads=[tp_b], writes=[xt_b], partial=not (sub == 0 and q4 == 0))
            if outT_d is not None:
                xt_ap, xt_b = xts[ts % 2]
                xt3 = xt_ap.rearrange("p (k t) -> p k t", k=KCD)
                P.op('act', lambda e, xt3=xt3, ts=ts: e.dma_start(
                    out=outT_d[:, ts * TS:(ts + 1) * TS].rearrange("(k p) t -> p k t", p=128), in_=xt3),
                    reads=[xt_b], writes=[b_outT], dma=xt_b, partial=True)
        self.pop()

    def l1_inproj(self, W_d, xT_d, b_xT):
        cfg, P = self.cfg, self.P
        NB = (cfg.DI + cfg.CONV + cfg.NH) // 128
        self.NBLK1 = NB
        self.ZXs = []
        for i in range(0, NB, 64):
            n = min(64, NB - i)
            self.ZXs.append(self.dscr(f"ZX{i // 64}", [n, 128, cfg.S], F32))
        self.b_zx = [Buf(f"zx{i}") for i in range(NB)]
        self.push()
        stg = [self.alloc(f"l1stg{i}", 512, F32) for i in range(4)]
        st = {'c': 0}

        def epi(blk, tb, ps, ps_b):
            s_ap, s_b = stg[st['c'] % 4]
            eng = 'act' if st['c'] % 2 == 0 else 'dve'
            st['c'] += 1
            if eng == 'act':
                P.op('act', lambda e: e.copy(out=s_ap, in_=ps), reads=[ps_b], writes=[s_b])
            else:
                P.op('dve', lambda e: e.tensor_copy(out=s_ap, in_=ps), reads=[ps_b], writes=[s_b])
            P.op('act', lambda e: e.dma_start(out=self.zx(blk, 1)[0, :, tb * 512:(tb + 1) * 512], in_=s_ap),
                 reads=[s_b], writes=[self.b_zx[blk]], dma=s_b, partial=True)

        self.gemm_fm(W_d, xT_d, b_xT, cfg.KC, cfg.S, NB, cfg.CB, epi, "g1")
        self.pop()

    def zx(self, b0, nb):
        t = self.ZXs[b0 // 64]
        l0 = b0 % 64
        assert l0 + nb <= 64
        return t[l0:l0 + nb]

    def l1_ssd(self, convw_d, convb_d, dtb_d, alog_d, dsk_d, nw_d, triu_d, smask_d):
        cfg, P = self.cfg, self.P
        S, G8, HPG, NH = cfg.S, cfg.G8, cfg.HPG, cfg.NH
        XB = HPG * 64 // 128
        NXB = G8 * XB
        UB = XB + 2
        NCONV = NXB + 2 * G8
        ZB0, XB0 = 0, NXB
        BB0 = 2 * NXB
        CB0 = BB0 + G8
        DTB = CB0 + G8
        self.Y1 = self.dscr("Y1", [cfg.DI, S], BF16)
        self.b_y1 = Buf("Y1")
        self.push()
        triu, triu_b = self.alloc("triu", 128, F32)
        smask, smask_b = self.alloc("smask", 128, F32)
        ones_b, ones_bb = self.alloc("ones_b", 128, BF16)
        cw, cw_b = self.alloc("convw", NCONV * 4, F32)
        cb_, cb_b = self.alloc("convb", NCONV, F32)
        dsk, dsk_b = self.alloc("dsk", NXB, F32)
        nw, nw_b = self.alloc("nw", NXB, F32)
        dtb, dtb_b = self.alloc("dtb", 1, F32)
        acol, acol_b = self.alloc("acol", 1, F32)
        for ap_, b_, src in ((triu, triu_b, triu_d), (smask, smask_b, smask_d), (cw, cw_b, convw_d), (cb_, cb_b, convb_d),
                             (dsk, dsk_b, dsk_d), (nw, nw_b, nw_d), (dtb, dtb_b, dtb_d), (acol, acol_b, alog_d)):
            P.op('sp', lambda e, ap_=ap_, src=src: e.dma_start(out=ap_, in_=src), writes=[b_], dma=b_)
        P.op('dve', lambda e: e.memset(ones_b, 1.0), writes=[ones_bb])
        P.op('act', lambda e: e.activation(out=acol, in_=acol, func=AF.Exp), reads=[acol_b], writes=[acol_b])
        P.op('dve', lambda e: e.tensor_scalar(out=acol, in0=acol, scalar1=-1.0, scalar2=None, op0=ALU.mult),
             reads=[acol_b], writes=[acol_b])
        cw3 = cw.rearrange("p (b k) -> p b k", k=4)
        st, st_b = self.alloc("state", NH * 64, F32)
        stb, stb_b = self.alloc("stateb", NH * 64, BF16)
        st_bs = [Buf(f"st{h}") for h in range(NH)]
        stb_bs = [Buf(f"stb{h}") for h in range(NH)]
        P.op('dve', lambda e: e.memset(st, 0.0), writes=st_bs)
        P.op('pool', lambda e: e.memset(stb, 0.0), writes=stb_bs)
        def ring(name, cols, dt=F32, n=2):
            return [self.alloc(f"{name}{i}", cols, dt) for i in range(n)]
        dtr = ring("dtr", 128); xb_ = ring("xb", 128); ax = ring("ax", 128); dtT = ring("dtT", 128)
        dtaT = ring("dtaT", 128); dt_tok = ring("dt_tok", 128); dta_tok = ring("dta_tok", 128)
        acum = ring("acum", 128); ea_tok = ring("ea_tok", 128); eend = ring("eend", 128)
        toend = ring("toend", 128); dtw_tok = ring("dtw_tok", 128)
        xin = ring("xin", UB * 131); xc = ring("xc", UB * 128); xcb = ring("xcb", UB * 128, BF16)
        zin = ring("zin", XB * 128); ych = ring("ych", XB * 128); ysq = ring("ysq", XB * 128, BF16)
        yb = ring("yb", XB * 128, BF16)
        xdt = ring("xdt", XB * 128, BF16); xdtw = ring("xdtw", XB * 128, BF16)
        btok = ring("btok", 128, BF16); cbTm = ring("cbTm", 128); rs = ring("rs", 128)
        Rr = ring("Rr", 512); dec = ring("dec", 512)
        GT = ring("GT", 512, BF16); CTs = ring("CTs", 512, BF16)
        dsx = ring("dsx", XB * 128)
        xcs_b = [[Buf(f"xcs{i}_{j}") for j in range(UB)] for i in range(2)]
        def sub(bank, off, w, dt=F32):
            ap, b = self.bank(bank, dt)
            return (ap[:, off:off + w], b)
        seg_ps = [sub(0, 0, 512), sub(1, 0, 512)]
        eh_ps = [sub(2, 0, 512), sub(3, 0, 512)]
        y_ps = [sub(4, 0, 512)]
        s_ps = [sub(5, 0, 512)]
        misc_ps = [sub(6, 0, 128)]
        ss_ps = [sub(6, 128, 128), sub(6, 128, 128)]
        xt_ps = [sub(7, 0, 512, BF16)]
        bt_ps = [sub(7, 512, 128, BF16)]
        ident_f, ident_b, ones_f = self.ident_f, self.ident_b, self.ones_f
        bif, bib, bon = self.b_ident_f, self.b_ident_b, self.b_ones
        cnt = {'m': 0, 'h': 0, 'y': 0, 's': 0, 'x': 0}
        NCH = S // 128
        STOP = getattr(cfg, 'ssd_stop', 9)
        dlim = getattr(cfg, 'dt_lim', 10 ** 9)
        dcn = {'c': 0}

        def dop(*a, **k):
            dcn['c'] += 1
            if dcn['c'] <= dlim:
                P.op(*a, **k)
        def dt_pipe(c):
            t0 = c * 128
            r = c % 2
            if STOP >= 1:
                dop('sp', lambda e, r=r, t0=t0: e.dma_start(out=dtr[r][0], in_=self.zx(DTB, 1)[0, :, t0:t0 + 128]),
                     reads=[self.b_zx[DTB]], writes=[dtr[r][1]], dma=dtr[r][1])
                dop('dve', lambda e, r=r: e.tensor_scalar(out=xb_[r][0], in0=dtr[r][0], scalar1=dtb, scalar2=None, op0=ALU.add),
                     reads=[dtr[r][1], dtb_b], writes=[xb_[r][1]])
                dop('dve', lambda e, r=r: e.scalar_tensor_tensor(out=ax[r][0], in0=xb_[r][0], scalar=-1.0, in1=xb_[r][0], op0=ALU.mult, op1=ALU.max),
                     reads=[xb_[r][1]], writes=[ax[r][1]])
                dop('act', lambda e, r=r: e.activation(out=ax[r][0], in_=ax[r][0], func=AF.Exp, scale=-1.0),
                     reads=[ax[r][1]], writes=[ax[r][1]])
                dop('act', lambda e, r=r: e.activation(out=ax[r][0], in_=ax[r][0], func=AF.Ln, bias=ones_f[:, 0:1], scale=1.0),
                     reads=[ax[r][1], bon], writes=[ax[r][1]])
                dop('dve', lambda e, r=r: e.scalar_tensor_tensor(out=dtT[r][0], in0=xb_[r][0], scalar=0.0, in1=ax[r][0],
                                                                 op0=ALU.max, op1=ALU.add),
                     reads=[xb_[r][1], ax[r][1]], writes=[dtT[r][1]])
                dop('dve', lambda e, r=r: e.tensor_scalar(out=dtaT[r][0], in0=dtT[r][0], scalar1=acol, scalar2=None, op0=ALU.mult),
                     reads=[dtT[r][1], acol_b], writes=[dtaT[r][1]])
                for src, dst in ((dtT, dt_tok), (dtaT, dta_tok)):
                    mp, mp_b = misc_ps[cnt['m'] % len(misc_ps)]
                    cnt['m'] += 1
                    dop('pe', lambda e, mp=mp, src=src, r=r: e.transpose(mp, src[r][0], ident_f),
                         reads=[src[r][1], bif], writes=[mp_b])
                    dop('act', lambda e, mp=mp, dst=dst, r=r: e.copy(out=dst[r][0], in_=mp), reads=[mp_b], writes=[dst[r][1]])
                mp, mp_b = misc_ps[cnt['m'] % len(misc_ps)]
                cnt['m'] += 1
                dop('pe', lambda e, mp=mp, r=r: e.matmul(mp, lhsT=triu, rhs=dta_tok[r][0], start=True, stop=True),
                     reads=[triu_b, dta_tok[r][1]], writes=[mp_b])
                dop('dve', lambda e, mp=mp, r=r: e.tensor_copy(out=acum[r][0], in_=mp), reads=[mp_b], writes=[acum[r][1]])
                dop('act', lambda e, mp=mp, r=r: e.activation(out=ea_tok[r][0], in_=mp, func=AF.Exp),
                     reads=[mp_b], writes=[ea_tok[r][1]])
                mp2, mp2_b = misc_ps[cnt['m'] % len(misc_ps)]
                cnt['m'] += 1
                dop('pe', lambda e, mp2=mp2, r=r: e.matmul(mp2, lhsT=ones_f, rhs=dta_tok[r][0], start=True, stop=True),
                     reads=[bon, dta_tok[r][1]], writes=[mp2_b])
                dop('act', lambda e, mp2=mp2, r=r: e.activation(out=eend[r][0], in_=mp2, func=AF.Exp),
                     reads=[mp2_b], writes=[eend[r][1]])
                dop('dve', lambda e, mp2=mp2, r=r: e.tensor_tensor(out=toend[r][0], in0=mp2, in1=acum[r][0], op=ALU.subtract),
                     reads=[mp2_b, acum[r][1]], writes=[toend[r][1]])
                dop('act', lambda e, r=r: e.activation(out=toend[r][0], in_=toend[r][0], func=AF.Exp),
                     reads=[toend[r][1]], writes=[toend[r][1]])
                dop('dve', lambda e, r=r: e.tensor_tensor(out=dtw_tok[r][0], in0=dt_tok[r][0], in1=toend[r][0], op=ALU.mult),
                     reads=[dt_tok[r][1], toend[r][1]], writes=[dtw_tok[r][1]])
        def unit(c, g):
            t0 = c * 128
            r = c % 2
            u = (c * G8 + g) % 2
            xin_ap, xin_b = xin[u]
            xin3 = xin_ap.rearrange("p (b t) -> p b t", t=131)
            xc_ap, xc_b = xc[u]
            xc3 = xc_ap.rearrange("p (b t) -> p b t", t=128)
            xcb_ap, xcb_b = xcb[u]
            xcb3 = xcb_ap.rearrange("p (b t) -> p b t", t=128)
            zin_ap, zin_b = zin[u]
            zin3 = zin_ap.rearrange("p (b t) -> p b t", t=128)
            srcs = [(XB0 + g * XB, XB, 0), (BB0 + g, 1, XB), (CB0 + g, 1, XB + 1)]
            first = True
            if c == 0:
                P.op('pool', lambda e, xin3=xin3: e.memset(xin3[:, :, 0:3], 0.0), writes=[xin_b])
                first = False
            for (b0, nb_, o0) in srcs:
                lo = 0 if c > 0 else 3
                P.op('sp', lambda e, xin3=xin3, b0=b0, nb_=nb_, o0=o0, lo=lo, t0=t0: e.dma_start(
                    out=xin3[:, o0:o0 + nb_, lo:131],
                    in_=self.zx(b0, nb_)[:, :, t0 - 3 + lo:t0 + 128].rearrange("b p t -> p b t")),
                    reads=[self.b_zx[b0 + i] for i in range(nb_)], writes=[xin_b], dma=xin_b, partial=(not first))
                first = False
            P.op('sp', lambda e, zin3=zin3, g=g, t0=t0: e.dma_start(
                out=zin3, in_=self.zx(ZB0 + g * XB, XB)[:, :, t0:t0 + 128].rearrange("b p t -> p b t")),
                reads=[self.b_zx[ZB0 + g * XB + i] for i in range(XB)], writes=[zin_b], dma=zin_b)
            def cblk_of(bi):
                return (g * XB + bi) if bi < XB else (NXB + g if bi == XB else NXB + G8 + g)
            for bi in range(UB):
                cblk = cblk_of(bi)
                P.op('act', lambda e, xc3=xc3, xin3=xin3, bi=bi, cblk=cblk: e.activation(
                    out=xc3[:, bi, :], in_=xin3[:, bi, 0:128], func=AF.Identity, scale=cw3[:, cblk, 0:1],
                    bias=cb_[:, cblk:cblk + 1]), reads=[xin_b, cw_b, cb_b], writes=[xcs_b[u][bi], xc_b], partial=True)
            yield 0
            for k in range(1, 4):
                for bi in range(UB):
                    cblk = cblk_of(bi)
                    P.op('dve', lambda e, xc3=xc3, xin3=xin3, bi=bi, cblk=cblk, k=k: e.scalar_tensor_tensor(
                        out=xc3[:, bi, :], in0=xin3[:, bi, k:k + 128], scalar=cw3[:, cblk, k:k + 1], in1=xc3[:, bi, :],
                        op0=ALU.mult, op1=ALU.add), reads=[xin_b, cw_b, xcs_b[u][bi]], writes=[xcs_b[u][bi]])
                    if bi % 3 == 2:
                        yield 0
            P.op('act', lambda e, xc_ap=xc_ap, xcb_ap=xcb_ap: e.activation(out=xcb_ap, in_=xc_ap, func=AF.Silu),
                 reads=xcs_b[u], writes=[xcb_b])
            P.op('act', lambda e, xc_ap=xc_ap: e.activation(out=xc_ap, in_=xc_ap, func=AF.Silu), reads=xcs_b[u], writes=[xc_b] + xcs_b[u])
            P.op('act', lambda e, zin_ap=zin_ap: e.activation(out=zin_ap, in_=zin_ap, func=AF.Silu), reads=[zin_b], writes=[zin_b])
            yield 0
            if STOP < 3:
                return
            xdt_ap, xdt_b = xdt[u]
            xdtw_ap, xdtw_b = xdtw[u]
            for q in range(XB // 4):
                xp, xp_b = xt_ps[cnt['x'] % len(xt_ps)]
                cnt['x'] += 1
                for j in range(4):
                    P.op('pe', lambda e, xp=xp, xcb3=xcb3, q=q, j=j: e.transpose(
                        xp[:, j * 128:(j + 1) * 128], xcb3[:, q * 4 + j, :], ident_b),
                        reads=[xcb_b, bib], writes=[xp_b], partial=(j > 0))
                h0 = g * HPG + q * 8
                for (dst_ap, dst_b, sc) in ((xdt_ap, xdt_b, dt_tok), (xdtw_ap, xdtw_b, dtw_tok)):
                    P.op('dve', lambda e, xp=xp, dst_ap=dst_ap, sc=sc, q=q, h0=h0, r=r: e.tensor_tensor(
                        out=dst_ap[:, q * 512:(q + 1) * 512].rearrange("p (h c) -> p h c", c=64),
                        in0=xp.rearrange("p (h c) -> p h c", c=64),
                        in1=sc[r][0][:, h0:h0 + 8].unsqueeze(2).to_broadcast([128, 8, 64]), op=ALU.mult),
                        reads=[xp_b, sc[r][1]], writes=[dst_b], partial=(q > 0))
                yield 0
            bp, bp_b = bt_ps[cnt['m'] % len(bt_ps)]
            P.op('pe', lambda e, bp=bp, xcb3=xcb3: e.transpose(bp, xcb3[:, XB, :], ident_b),
                 reads=[xcb_b, bib], writes=[bp_b])
            bt_ap, bt_b = btok[u]
            P.op('act', lambda e, bt_ap=bt_ap, bp=bp: e.copy(out=bt_ap, in_=bp), reads=[bp_b], writes=[bt_b])
            mp, mp_b = misc_ps[cnt['m'] % len(misc_ps)]
            cnt['m'] += 1
            P.op('pe', lambda e, mp=mp, xcb3=xcb3: e.matmul(mp, lhsT=xcb3[:, XB, :], rhs=xcb3[:, XB + 1, :], start=True, stop=True),
                 reads=[xcb_b], writes=[mp_b])
            cm_ap, cm_b = cbTm[u]
            P.op('dve', lambda e, cm_ap=cm_ap, mp=mp: e.tensor_tensor(out=cm_ap, in0=mp, in1=triu, op=ALU.mult),
                 reads=[mp_b, triu_b], writes=[cm_b])
            yield 'SPLIT'
            y_ap, y_b = ych[u]
            y3 = y_ap.rearrange("p (b t) -> p b t", t=128)
            if STOP < 4:
                return
            dsx_ap, dsx_b = dsx[u]
            dsx3 = dsx_ap.rearrange("p (b t) -> p b t", t=128)
            P.op('pool', lambda e, dsx3=dsx3, xc3=xc3, g=g: e.tensor_tensor(
                out=dsx3, in0=xc3[:, 0:XB, :], in1=dsk[:, g * XB:(g + 1) * XB].unsqueeze(2).to_broadcast([128, XB, 128]),
                op=ALU.mult), reads=[xc_b, dsk_b], writes=[dsx_b])
            def stageA(hq):
                hs = g * HPG + hq * 4
                k2 = cnt['h'] % 2
                cnt['h'] += 1
                R_ap, R_b = Rr[k2]
                P.op('pool', lambda e, R_ap=R_ap, hs=hs, r=r: e.tensor_tensor(
                    out=R_ap.rearrange("p (j l) -> p j l", l=128),
                    in0=triu.unsqueeze(1).to_broadcast([128, 4, 128]),
                    in1=dta_tok[r][0][:, hs:hs + 4].unsqueeze(2).to_broadcast([128, 4, 128]), op=ALU.mult),
                    reads=[triu_b, dta_tok[r][1]], writes=[R_b])
                sg, sg_b = seg_ps[k2]
                P.op('pe', lambda e, sg=sg, R_ap=R_ap: e.matmul(sg, lhsT=smask, rhs=R_ap, start=True, stop=True),
                     reads=[smask_b, R_b], writes=[sg_b])
                dc_ap, dc_b = dec[k2]
                P.op('act', lambda e, dc_ap=dc_ap, sg=sg: e.activation(out=dc_ap, in_=sg, func=AF.Exp),
                     reads=[sg_b], writes=[dc_b])
                gt_ap, gt_b = GT[k2]
                P.op('pool', lambda e, gt_ap=gt_ap, dc_ap=dc_ap, cm_ap=cm_ap: e.tensor_tensor(
                    out=gt_ap.rearrange("p (j l) -> p j l", l=128), in0=dc_ap.rearrange("p (j l) -> p j l", l=128),
                    in1=cm_ap.unsqueeze(1).to_broadcast([128, 4, 128]), op=ALU.mult), reads=[dc_b, cm_b], writes=[gt_b])
                eh, eh_b = eh_ps[k2]
                for j in range(4):
                    P.op('pe', lambda e, eh=eh, hs=hs, j=j, r=r: e.transpose(
                        eh[:, j * 128:(j + 1) * 128], ea_tok[r][0][:, hs + j:hs + j + 1].to_broadcast([128, 128]), ident_f),
                        reads=[ea_tok[r][1], bif], writes=[eh_b])
                ct_ap, ct_b = CTs[k2]
                P.op('dve', lambda e, ct_ap=ct_ap, eh=eh, xc3=xc3: e.tensor_tensor(
                    out=ct_ap.rearrange("p (j l) -> p j l", l=128), in0=eh.rearrange("p (j l) -> p j l", l=128),
                    in1=xc3[:, XB + 1, :].unsqueeze(1).to_broadcast([128, 4, 128]), op=ALU.mult),
                    reads=[eh_b, xc_b], writes=[ct_b])
                return dict(gt_ap=gt_ap, gt_b=gt_b, ct_ap=ct_ap, ct_b=ct_b)

            def stageB(hq, ctx):
                gt_ap, gt_b, ct_ap, ct_b = ctx['gt_ap'], ctx['gt_b'], ctx['ct_ap'], ctx['ct_b']
                h8 = (g * HPG + hq * 4) // 8
                yp, yp_b = y_ps[0]
                sp_, sp_b = s_ps[0]
                for j in range(4):
                    hh = hq * 4 + j
                    h = g * HPG + hh
                    pr = (hh % 8) // 2
                    ro = (hh % 2) * 64
                    P.op('pe', lambda e, yp=yp, xdt_ap=xdt_ap, gt_ap=gt_ap, hh=hh, ro=ro, pr=pr, j=j: e.matmul(
                        yp[ro:ro + 64, pr * 128:(pr + 1) * 128], lhsT=xdt_ap[:, hh * 64:(hh + 1) * 64],
                        rhs=gt_ap[:, j * 128:(j + 1) * 128], start=True, stop=False),
                        reads=[xdt_b, gt_b], writes=[yp_b])
                    P.op('pe', lambda e, yp=yp, h=h, ct_ap=ct_ap, ro=ro, pr=pr, j=j: e.matmul(
                        yp[ro:ro + 64, pr * 128:(pr + 1) * 128], lhsT=stb[:, h * 64:(h + 1) * 64],
                        rhs=ct_ap[:, j * 128:(j + 1) * 128], start=False, stop=True),
                        reads=[stb_bs[h8], ct_b], writes=[yp_b])
                for j in range(4):
                    hh = hq * 4 + j
                    P.op('pe', lambda e, sp_=sp_, bt_ap=bt_ap, xdtw_ap=xdtw_ap, hh=hh: e.matmul(
                        sp_[:, (hh % 8) * 64:(hh % 8 + 1) * 64], lhsT=bt_ap, rhs=xdtw_ap[:, hh * 64:(hh + 1) * 64],
                        start=True, stop=True), reads=[bt_b, xdtw_b], writes=[sp_b])
                if hq % 2 == 1:
                    b4 = (hq // 2) * 4
                    h0 = g * HPG + (hq // 2) * 8
                    P.op('dve', lambda e, y3=y3, dsx3=dsx3, yp=yp, b4=b4: e.tensor_tensor(
                        out=y3[:, b4:b4 + 4, :], in0=yp.rearrange("p (b t) -> p b t", t=128), in1=dsx3[:, b4:b4 + 4, :],
                        op=ALU.add), reads=[yp_b, dsx_b], writes=[y_b], partial=(b4 > 0))
                    st8 = st[:, h0 * 64:(h0 + 8) * 64]
                    P.op('pool', lambda e, st8=st8, h0=h0, r=r: e.tensor_tensor(
                        out=st8.rearrange("p (h c) -> p h c", c=64), in0=st8.rearrange("p (h c) -> p h c", c=64),
                        in1=eend[r][0][:, h0:h0 + 8].unsqueeze(2).to_broadcast([128, 8, 64]), op=ALU.mult),
                        reads=[st_bs[h8], eend[r][1]], writes=[st_bs[h8]])
                    P.op('dve', lambda e, st8=st8, sp_=sp_: e.tensor_tensor(out=st8, in0=sp_, in1=st8, op=ALU.add),
                         reads=[st_bs[h8], sp_b], writes=[st_bs[h8]])
                    P.op('act', lambda e, st8=st8, h0=h0: e.copy(out=stb[:, h0 * 64:(h0 + 8) * 64], in_=st8),
                         reads=[st_bs[h8]], writes=[stb_bs[h8]])

            NB4 = HPG // 4
            ctxs = {0: stageA(0)}
            yield 0
            for hq in range(NB4):
                if hq + 1 < NB4:
                    ctxs[hq + 1] = stageA(hq + 1)
                    yield 0
                stageB(hq, ctxs.pop(hq))
                yield 0
            if STOP < 5:
                return
            P.op('dve', lambda e, y_ap=y_ap, zin_ap=zin_ap: e.tensor_tensor(out=y_ap, in0=y_ap, in1=zin_ap, op=ALU.mult),
                 reads=[y_b, zin_b], writes=[y_b])
            q_ap, q_b = ysq[u]
            q3 = q_ap.rearrange("p (b t) -> p b t", t=128)
            P.op('pool', lambda e, q_ap=q_ap, y_ap=y_ap: e.tensor_tensor(out=q_ap, in0=y_ap, in1=y_ap, op=ALU.mult),
                 reads=[y_b], writes=[q_b])
            yield 0
            sp2, sp2_b = ss_ps[u]
            for bi in range(XB):
                P.op('pe', lambda e, sp2=sp2, q3=q3, bi=bi: e.matmul(sp2, lhsT=ones_b, rhs=q3[:, bi, :],
                                                                     start=(bi == 0), stop=(bi == XB - 1)),
                     reads=[ones_bb, q_b], writes=[sp2_b], partial=(bi > 0))
            rs_ap, rs_b = rs[u]
            P.op('act', lambda e, rs_ap=rs_ap, sp2=sp2: e.activation(out=rs_ap, in_=sp2, func=AF.Ln, bias=self.eps_ap,
                                                                     scale=1.0 / (XB * 128)),
                 reads=[sp2_b, self.b_eps], writes=[rs_b])
            P.op('act', lambda e, rs_ap=rs_ap: e.activation(out=rs_ap, in_=rs_ap, func=AF.Exp, scale=-0.5),
                 reads=[rs_b], writes=[rs_b])
            yield 0
            P.op('dve', lambda e, y3=y3, rs_ap=rs_ap: e.tensor_tensor(
                out=y3, in0=y3, in1=rs_ap.unsqueeze(1).to_broadcast([128, XB, 128]), op=ALU.mult),
                reads=[y_b, rs_b], writes=[y_b])
            yb_ap, yb_b = yb[u]
            yb3 = yb_ap.rearrange("p (b t) -> p b t", t=128)
            P.op('pool', lambda e, yb3=yb3, y3=y3, g=g: e.tensor_tensor(
                out=yb3, in0=y3, in1=nw[:, g * XB:(g + 1) * XB].unsqueeze(2).to_broadcast([128, XB, 128]), op=ALU.mult),
                reads=[y_b, nw_b], writes=[yb_b])
            P.op('pool', lambda e, yb3=yb3, g=g, t0=t0: e.dma_start(
                out=self.Y1[g * XB * 128:(g + 1) * XB * 128, t0:t0 + 128].rearrange("(b p) t -> p b t", p=128), in_=yb3),
                reads=[yb_b], writes=[self.b_y1], dma=yb_b, partial=True)

        nchk = min(NCH, getattr(cfg, 'ssd_chunks', NCH))
        units = [(c, g) for c in range(nchk) for g in range(G8 if STOP >= 2 else 0)]
        if not units:
            for c in range(nchk):
                dt_pipe(c)

        def run_front(gen):
            for v in gen:
                if v == 'SPLIT':
                    return True
            return False

        gens = {}
        if units:
            dt_pipe(units[0][0])
            gens[0] = unit(*units[0])
            alive = run_front(gens[0])
            for ui in range(len(units)):
                cur = gens.pop(ui)
                nxt = None
                if ui + 1 < len(units):
                    if units[ui + 1][1] == 0:
                        dt_pipe(units[ui + 1][0])
                    nxt = unit(*units[ui + 1])
                    gens[ui + 1] = nxt
                cur_done = False
                nxt_done = nxt is None
                while not (cur_done and nxt_done):
                    if not cur_done:
                        try:
                            next(cur)
                        except StopIteration:
                            cur_done = True
                    if not nxt_done:
                        try:
                            if next(nxt) == 'SPLIT':
                                nxt_done = True
                        except StopIteration:
                            nxt_done = True
        self.pop()

    def setup_eps(self):
        self.eps_ap, self.b_eps = self.alloc("eps", 1, F32)
        self.P.op('dve', lambda e: e.memset(self.eps_ap, 1e-5), writes=[self.b_eps])


def _t5_bucket(dist):
    max_exact = 16
    d_f = np.maximum(dist, 1).astype(np.float32)
    large = max_exact + (np.log(d_f / np.float32(max_exact)) / np.float32(math.log(2048 / max_exact))
                         * np.float32(32 - max_exact)).astype(np.int32)
    large = np.minimum(large, 31)
    return np.where(dist < max_exact, dist, large)


def make_onehot():
    qi = np.arange(128)[:, None]
    ki = np.arange(256)[None, :]
    delta = 128 + qi - ki
    band = (delta >= 0) & (delta <= 128)
    oh = np.zeros((3, 33, 128 * 256), np.float32)
    for g, dil in enumerate((1, 4, 16)):
        bucket = _t5_bucket(np.clip(delta, 0, None) * dil)
        for b in range(32):
            oh[g, b] = ((bucket == b) & band).reshape(-1)
        oh[g, 32] = (~band).reshape(-1)
    return oh


def build_program(cfg):
    mk = MK(cfg)
    D, S = cfg.D, cfg.S
    P = mk.P
    ident_d = mk.din("ident", [128, 128])
    xT_d = mk.din("xT", [D, S])
    x_d = mk.din("x", [S, D])
    lng_d = mk.din("ln_g", [2, D])
    lnb_d = mk.din("ln_b", [2, D])
    out_d = mk.dout("out", [S, D])
    b_out = Buf("out")
    mk.setup_consts(ident_d)
    mk.setup_eps()
    b_x = Buf("x_in")
    xTb = mk.dscr("xTb", [D, S], BF16)
    b_xTb = Buf("xTb")
    mk.cast_dram(xTb, xT_d, D, b_xTb)
    has0, has1 = (0 in cfg.layers), (1 in cfg.layers)
    if has0:
        NSUP0 = (cfg.NBLK0 + cfg.CB - 1) // cfg.CB
        W0_d = mk.din("W0", [NSUP0, 128, cfg.KC, cfg.CB * 128])
        relb_d = mk.din("relb", [32, 3 * cfg.HG])
        onehot_d = mk.din("onehot", [3, 33, 128 * 256])
        KCE0 = cfg.DATT // 128
        Wo0_d = mk.din("Wo0", [D // 512 * 128, KCE0 * 512])
        Wo0b = mk.dscr("Wo0b", [D // 512 * 128, KCE0 * 512], BF16)
        b_wo0 = Buf("Wo0b")
        mk.cast_dram(Wo0b, Wo0_d, D // 512 * 128, b_wo0, nsplit=4)
        mk.l0_bias_tables(relb_d, onehot_d)
        mk.l0_inproj(W0_d, xTb, b_xTb)
        mk.l0_attention()
        if has1:
            X1 = mk.dscr("X1", [S, D], F32)
            b_x1 = Buf("X1")
            X1T = mk.dscr("X1T", [D, S], BF16)
            b_x1T = Buf("X1T")
            mk.outproj_ln(mk.Y0, mk.b_y0, Wo0b, b_wo0, KCE0, x_d, b_x, lng_d[0], lnb_d[0],
                          X1, b_x1, X1T, b_x1T, "o0")
        else:
            mk.outproj_ln(mk.Y0, mk.b_y0, Wo0b, b_wo0, KCE0, x_d, b_x, lng_d[0], lnb_d[0],
                          out_d, b_out, None, None, "o0")
    else:
        X1, b_x1, X1T, b_x1T = x_d, b_x, xTb, b_xTb
    if has1:
        NB1 = (cfg.DI + cfg.CONV + cfg.NH) // 128
        NSUP1 = (NB1 + cfg.CB - 1) // cfg.CB
        NCONV = cfg.CONV // 128
        NXB = cfg.DI // 128
        W1_d = mk.din("W1", [NSUP1, 128, cfg.KC, cfg.CB * 128])
        convw_d = mk.din("convw", [128, NCONV * 4])
        convb_d = mk.din("convb", [128, NCONV])
        dtb_d = mk.din("dtb", [128, 1])
        alog_d = mk.din("alog", [128, 1])
        dsk_d = mk.din("dsk", [128, NXB])
        nw_d = mk.din("nw", [128, NXB])
        triu_d = mk.din("triu", [128, 128])
        smask_d = mk.din("smask", [128, 128])
        KCE1 = cfg.DI // 128
        Wo1_d = mk.din("Wo1", [D // 512 * 128, KCE1 * 512])
        Wo1b = mk.dscr("Wo1b", [D // 512 * 128, KCE1 * 512], BF16)
        b_wo1 = Buf("Wo1b")
        mk.cast_dram(Wo1b, Wo1_d, D // 512 * 128, b_wo1, nsplit=4)
        stg = getattr(cfg, 'stages', ('inproj', 'ssd', 'outproj'))
        if 'inproj' in stg:
            mk.l1_inproj(W1_d, X1T, b_x1T)
        if 'ssd' in stg:
            mk.l1_ssd(convw_d, convb_d, dtb_d, alog_d, dsk_d, nw_d, triu_d, smask_d)
        else:
            mk.Y1 = mk.dscr("Y1", [cfg.DI, S], BF16)
            mk.b_y1 = Buf("Y1")
        if 'outproj' in stg:
            mk.outproj_ln(mk.Y1, mk.b_y1, Wo1b, b_wo1, KCE1, X1, b_x1, lng_d[1], lnb_d[1],
                          out_d, b_out, None, None, "o1")
    mk.P.finalize()
    return mk


def host_wout(w_out, D):
    E = w_out.shape[0]
    KCE = E // 128
    NDB = D // 512
    return np.ascontiguousarray(w_out.reshape(KCE, 128, NDB, 512).transpose(2, 1, 0, 3).reshape(NDB * 128, KCE * 512))


def host_win(w_in, cols, KC, CB):
    NBLK = len(cols) // 128
    NSUP = (NBLK + CB - 1) // CB
    if NSUP * CB > NBLK:
        cols = np.concatenate([cols, np.tile(cols[-128:], NSUP * CB - NBLK)])
    Wp = w_in[:, cols]
    return np.ascontiguousarray(Wp.reshape(KC, 128, NSUP, CB * 128).transpose(2, 1, 0, 3))


def host_layout_l1(cfg, w_in_ssm, conv_w, conv_b, dt_bias, a_log, d_skip, norm_w, w_out_ssm):
    NCONV = cfg.CONV // 128
    NXB = cfg.DI // 128
    W1 = host_win(w_in_ssm, np.arange(w_in_ssm.shape[1]), cfg.KC, cfg.CB)
    convw = np.ascontiguousarray(conv_w.reshape(4, NCONV, 128).transpose(2, 1, 0).reshape(128, NCONV * 4))
    convb = np.ascontiguousarray(conv_b.reshape(NCONV, 128).T)
    dsk = np.ascontiguousarray(np.repeat(d_skip, cfg.P).reshape(NXB, 128).T)
    nw = np.ascontiguousarray(norm_w.reshape(NXB, 128).T)
    t = np.arange(128)
    triu = (t[:, None] <= t[None, :]).astype(np.float32)
    smask = (t[:, None] > t[None, :]).astype(np.float32)
    return {"W1": W1, "convw": convw, "convb": convb, "dtb": np.ascontiguousarray(dt_bias.reshape(128, 1)),
            "alog": np.ascontiguousarray(a_log.reshape(128, 1)), "dsk": dsk, "nw": nw, "triu": triu, "smask": smask,
            "Wo1": host_wout(w_out_ssm, cfg.D)}


def host_layout_l0(cfg, w_in_attn, w_out_attn, rel_bias):
    D, HG, KC, CB = cfg.D, cfg.HG, cfg.KC, cfg.CB
    DATT = cfg.DATT
    cols = []
    for h in range(HG):
        for g in range(3):
            for j in range(3):
                c0 = g * 3 * DATT + j * DATT + h * 128
                cols.append(np.arange(c0, c0 + 128))
        c0 = 9 * DATT + h * 128
        cols.append(np.arange(c0, c0 + 128))
    NBLK = len(cols)
    NSUP = (NBLK + CB - 1) // CB
    while len(cols) < NSUP * CB:
        cols.append(cols[-1])
    cols = np.concatenate(cols)
    Wp = w_in_attn[:, cols]
    W0 = Wp.reshape(KC, 128, NSUP, CB * 128).transpose(2, 1, 0, 3)
    Wo = host_wout(w_out_attn, D)
    hs = np.concatenate([np.arange(g * (rel_bias.shape[1] // 3), g * (rel_bias.shape[1] // 3) + HG) for g in range(3)])
    return np.ascontiguousarray(W0), Wo, np.ascontiguousarray(rel_bias[:, hs])


_CACHE = {}


def kernel(x, w_in_attn, w_out_attn, rel_bias, w_in_ssm, conv_w, conv_b, dt_bias,
           a_log, d_skip, ssm_norm_w, w_out_ssm, ln_g, ln_b):
    x = np.asarray(x, dtype=np.float32)
    B, S, D = x.shape
    cfg = Cfg(D=D, S=S)
    if 'mk' not in _CACHE:
        _CACHE['mk'] = build_program(cfg)
    mk = _CACHE['mk']
    f = lambda a: np.asarray(a, dtype=np.float32)
    W0, Wo0, rb = host_layout_l0(cfg, f(w_in_attn)[0], f(w_out_attn)[0], f(rel_bias))
    shared = {"ident": np.eye(128, dtype=np.float32), "W0": W0, "relb": rb, "onehot": make_onehot(), "Wo0": Wo0,
              "ln_g": np.ascontiguousarray(f(ln_g)), "ln_b": np.ascontiguousarray(f(ln_b))}
    shared.update(host_layout_l1(cfg, f(w_in_ssm)[0], f(conv_w)[0], f(conv_b)[0], f(dt_bias)[0], f(a_log)[0],
                                 f(d_skip)[0], f(ssm_norm_w)[0], f(w_out_ssm)[0]))
    active = [0, 2, 4, 6]
    zeros = {k: np.zeros_like(v) for k, v in shared.items()}
    zeros["x"] = np.zeros((S, D), np.float32)
    zeros["xT"] = np.zeros((D, S), np.float32)
    in_maps = []
    for c in range(8):
        if c in active:
            b = active.index(c)
            m = dict(shared)
            m["x"] = np.ascontiguousarray(x[b])
            m["xT"] = np.ascontiguousarray(x[b].T)
        else:
            m = zeros
        in_maps.append(m)
    res = run_bass_kernel_spmd(mk.nc, in_maps, core_ids=list(range(8)))
    out = np.stack([np.asarray(res.results[active[b]]["out"], dtype=np.float32) for b in range(B)], axis=0)
    return out
```
